# Optimizing a Trainium2 kernel written in Bass

```python
import math
import jax, jax.numpy as jnp
from jax import lax
import numpy as np

D_MODEL = 1024
BATCH = 8
SEQ = 8192
DEPTH = 4

N_MIXERS = 3
NORM_EPS = 1e-6

HG_HEADS = 8
HG_DK = D_MODEL // HG_HEADS
HG_DV = D_MODEL // HG_HEADS
HG_HK = HG_HEADS * HG_DK
HG_HV = HG_HEADS * HG_DV
HG_CHUNK = 16

FOX_HEADS = 16
FOX_DH = D_MODEL // FOX_HEADS
FOX_BLOCK = 128

SSM_DI = 2 * D_MODEL
SSM_HEADDIM = 64
SSM_HEADS = SSM_DI // SSM_HEADDIM
SSM_GROUPS = 8
SSM_STATE = 128
SSM_CONV = 4
SSM_CHUNK = 64
SSM_CONV_DIM = SSM_DI + 2 * SSM_GROUPS * SSM_STATE
SSM_IN = 2 * SSM_DI + 2 * SSM_GROUPS * SSM_STATE + SSM_HEADS

PEER_HEADS = 8
PEER_NKEYS = 128
PEER_EXPERTS = PEER_NKEYS * PEER_NKEYS
PEER_DKEY = 256
PEER_HALF = PEER_DKEY // 2
PEER_TOPK = 16
PEER_TOKEN_BLOCK = 128

N_HGRN = (DEPTH + 2) // 3
N_FOX = (DEPTH + 1) // 3
N_SSM = DEPTH // 3

kernel_name = 'hgrn2_fox_ssd_peer_interleaved_trunk'


def rmsnorm(x, g, eps=NORM_EPS):
    xf = x.astype(jnp.float32)
    y = xf * lax.rsqrt(jnp.mean(xf * xf, axis=-1, keepdims=True) + eps)
    return (y * g.astype(jnp.float32)).astype(x.dtype)


def hgrn_lower_bounds(logits):
    cum = jnp.cumsum(jax.nn.softmax(logits.astype(jnp.float32), axis=0), axis=0)
    return cum - cum[0:1]


def chunk_gla(q, k, v, log_f, chunk):
    B, T, H, K = q.shape
    V = v.shape[-1]
    n = T // chunk

    def to_chunks(a):
        return a.reshape(B, n, chunk, H, a.shape[-1]).transpose(1, 0, 3, 2, 4)

    causal = jnp.tril(jnp.ones((chunk, chunk), dtype=bool))

    def step(S, inp):
        qc, kc, vc, gc = inp
        b = jnp.cumsum(gc.astype(jnp.float32), axis=2)
        b_last = b[:, :, -1:, :]
        q_dec = qc * jnp.exp(b)
        k_inv = kc * jnp.exp(-b)
        k_dec = kc * jnp.exp(b_last - b)
        A = jnp.where(causal, jnp.einsum('bhtk,bhsk->bhts', q_dec, k_inv), 0.0)
        o = jnp.einsum('bhts,bhsv->bhtv', A, vc) + jnp.einsum('bhtk,bhkv->bhtv', q_dec, S)
        S_new = jnp.exp(b_last[:, :, 0, :])[..., None] * S + jnp.einsum('bhsk,bhsv->bhkv', k_dec, vc)
        return S_new.astype(jnp.float32), o.astype(jnp.float32)

    S0 = jnp.zeros((B, H, K, V), jnp.float32)
    _, o = lax.scan(step, S0, (to_chunks(q), to_chunks(k), to_chunks(v), to_chunks(log_f)))
    return o.transpose(1, 0, 3, 2, 4).reshape(B, T, H, V)


def hgrn2_mixer(h, w_in, g_norm, w_out, lb):
    B, T, _ = h.shape
    q, f_logit, i_in, gate = jnp.split(h @ w_in, [HG_HK, 2 * HG_HK, 2 * HG_HK + HG_HV], axis=-1)
    lbf = lb.astype(jnp.float32)
    f = lbf + (1.0 - lbf) * jax.nn.sigmoid(f_logit.astype(jnp.float32))

    def heads(a, d):
        return a.reshape(B, T, HG_HEADS, d)

    o = chunk_gla(heads(q * (HG_DK ** -0.5), HG_DK), heads(1.0 - f, HG_DK),
                  heads(i_in, HG_DV), heads(jnp.log(f), HG_DK), HG_CHUNK)
    o = rmsnorm(o, g_norm) * jax.nn.silu(heads(gate, HG_DV))
    return o.reshape(B, T, HG_HV).astype(h.dtype) @ w_out


def fox_mixer(h, w_in, b_f, w_out):
    B, T, _ = h.shape
    q, k, v, f_logit = jnp.split(h @ w_in, [D_MODEL, 2 * D_MODEL, 3 * D_MODEL], axis=-1)
    log_f = jax.nn.log_sigmoid((f_logit + b_f).astype(jnp.float32))
    c = jnp.cumsum(log_f, axis=1).transpose(0, 2, 1)

    def heads(a):
        return a.reshape(B, T, FOX_HEADS, FOX_DH).transpose(0, 2, 1, 3)

    q, k, v = heads(q), heads(k), heads(v)
    scale = FOX_DH ** -0.5
    qpos = jnp.arange(FOX_BLOCK)
    outs = []
    for blk in range(T // FOX_BLOCK):
        q0 = blk * FOX_BLOCK
        kv_len = q0 + FOX_BLOCK
        logits = (jnp.einsum('bhtd,bhsd->bhts', q[:, :, q0:kv_len], k[:, :, :kv_len]).astype(jnp.float32) * scale
                  + c[:, :, q0:kv_len, None] - c[:, :, None, :kv_len])
        mask = (q0 + qpos)[:, None] >= jnp.arange(kv_len)[None, :]
        p = jax.nn.softmax(jnp.where(mask, logits, -jnp.inf), axis=-1)
        outs.append(jnp.einsum('bhts,bhsd->bhtd', p.astype(v.dtype), v[:, :, :kv_len]))
    o = jnp.concatenate(outs, axis=2).transpose(0, 2, 1, 3).reshape(B, T, D_MODEL)
    return o.astype(h.dtype) @ w_out


def causal_depthwise_conv(x, w, b):
    W = w.shape[0]
    out = lax.conv_general_dilated(x, w[:, None, :].astype(x.dtype), window_strides=(1,),
                                   padding=[(W - 1, 0)], dimension_numbers=('NWC', 'WIO', 'NWC'),
                                   feature_group_count=x.shape[-1])
    return out + b


def ssd_chunk_scan(x, dA, Bm, Cm, chunk):
    Bsz, T, H, P = x.shape
    G, N = Bm.shape[-2:]
    R = H // G
    n = T // chunk
    xc = x.reshape(Bsz, n, chunk, G, R, P).transpose(1, 0, 2, 3, 4, 5)
    ac = dA.reshape(Bsz, n, chunk, G, R).transpose(1, 0, 3, 4, 2)
    bc = Bm.reshape(Bsz, n, chunk, G, N).transpose(1, 0, 2, 3, 4)
    cc = Cm.reshape(Bsz, n, chunk, G, N).transpose(1, 0, 2, 3, 4)
    causal = jnp.tril(jnp.ones((chunk, chunk), dtype=bool))

    def step(S, inp):
        x_, a_, b_, c_ = inp
        cum = jnp.cumsum(a_.astype(jnp.float32), axis=-1)
        L = jnp.exp(jnp.where(causal, cum[..., :, None] - cum[..., None, :], -jnp.inf))
        cb = jnp.einsum('btgn,bsgn->bgts', c_, b_)
        y_intra = jnp.einsum('bgrts,bsgrp->btgrp', cb[:, :, None] * L, x_)
        y_inter = jnp.einsum('btgn,bgrpn,bgrt->btgrp', c_, S, jnp.exp(cum))
        decay_to_end = jnp.exp(cum[..., -1:] - cum)
        S_new = (jnp.exp(cum[..., -1])[..., None, None] * S
                 + jnp.einsum('bgrs,bsgn,bsgrp->bgrpn', decay_to_end, b_, x_))
        return S_new.astype(jnp.float32), (y_intra + y_inter).astype(jnp.float32)

    S0 = jnp.zeros((Bsz, G, R, P, N), jnp.float32)
    _, y = lax.scan(step, S0, (xc, ac, bc, cc))
    return y.transpose(1, 0, 2, 3, 4, 5).reshape(Bsz, T, H, P)


def ssd_mixer(h, w_in, conv_w, conv_b, dt_bias, a_log, d_skip, g_norm, w_out):
    B, T, _ = h.shape
    z, xbc, dt_raw = jnp.split(h @ w_in, [SSM_DI, SSM_DI + SSM_CONV_DIM], axis=-1)
    xbc = jax.nn.silu(causal_depthwise_conv(xbc, conv_w, conv_b))
    xs, Bm, Cm = jnp.split(xbc, [SSM_DI, SSM_DI + SSM_GROUPS * SSM_STATE], axis=-1)
    xs = xs.reshape(B, T, SSM_HEADS, SSM_HEADDIM)
    Bm = Bm.reshape(B, T, SSM_GROUPS, SSM_STATE)
    Cm = Cm.reshape(B, T, SSM_GROUPS, SSM_STATE)
    dt = jax.nn.softplus((dt_raw + dt_bias).astype(jnp.float32))
    dA = dt * (-jnp.exp(a_log.astype(jnp.float32)))
    y = ssd_chunk_scan(xs * dt[..., None], dA, Bm, Cm, SSM_CHUNK)
    y = (y + xs * d_skip[:, None]).reshape(B, T, SSM_DI)
    gs = SSM_DI // SSM_GROUPS
    y = rmsnorm((y * jax.nn.silu(z.astype(jnp.float32))).reshape(B, T, SSM_GROUPS, gs),
                g_norm.reshape(SSM_GROUPS, gs)).reshape(B, T, SSM_DI)
    return y.astype(h.dtype) @ w_out


def peer_ffn(h, w_q, sub_keys, u_tab, v_tab):
    B, T, D = h.shape
    tokens = h.reshape(-1, PEER_TOKEN_BLOCK, D)

    def block(xb):
        M = xb.shape[0]
        q = (xb @ w_q).reshape(M, PEER_HEADS, 2, PEER_HALF)
        s = jnp.einsum('mhcd,hckd->mhck', q, sub_keys).astype(jnp.float32)
        sv, si = lax.top_k(s, PEER_TOPK)
        cand_s = (sv[:, :, 0, :, None] + sv[:, :, 1, None, :]).reshape(M, PEER_HEADS, PEER_TOPK * PEER_TOPK)
        cand_i = (si[:, :, 0, :, None] * PEER_NKEYS + si[:, :, 1, None, :]).reshape(M, PEER_HEADS, PEER_TOPK * PEER_TOPK)
        top_s, pos = lax.top_k(cand_s, PEER_TOPK)
        idx = jnp.take_along_axis(cand_i, pos, axis=-1)
        gates = jax.nn.softmax(top_s, axis=-1)
        u = jnp.take(u_tab, idx, axis=0)
        v = jnp.take(v_tab, idx, axis=0)
        act = jax.nn.gelu(jnp.einsum('md,mhkd->mhk', xb, u).astype(jnp.float32), approximate=False) * gates
        return jnp.einsum('mhk,mhkd->md', act.astype(xb.dtype), v)

    return lax.map(block, tokens).reshape(B, T, D)


def setup_inputs(seed: int = 0) -> dict:
    key = jax.random.key(seed)
    k = jax.random.split(key, 24)
    f32 = jnp.float32

    def nrm(i, shape, scale):
        return jax.random.normal(k[i], shape, f32) * scale

    dsc = D_MODEL ** -0.5
    dt0 = jnp.exp(jax.random.uniform(k[15], (N_SSM, SSM_HEADS), f32, math.log(1e-3), math.log(1e-1)))
    return {
        'x': nrm(0, (BATCH, SEQ, D_MODEL), 1.0),
        'norm_mix': 1.0 + nrm(1, (DEPTH, D_MODEL), 0.02),
        'norm_ffn': 1.0 + nrm(2, (DEPTH, D_MODEL), 0.02),
        'norm_final': 1.0 + nrm(3, (D_MODEL,), 0.02),
        'hgrn_lb_logits': nrm(4, (DEPTH, HG_HK), 0.5),
        'hgrn_w_in': nrm(5, (N_HGRN, D_MODEL, 2 * HG_HK + 2 * HG_HV), dsc),
        'hgrn_gnorm': 1.0 + nrm(6, (N_HGRN, HG_DV), 0.02),
        'hgrn_w_out': nrm(7, (N_HGRN, HG_HV, D_MODEL), HG_HV ** -0.5),
        'fox_w_in': nrm(8, (N_FOX, D_MODEL, 3 * D_MODEL + FOX_HEADS), dsc),
        'fox_b_f': jax.random.uniform(k[9], (N_FOX, FOX_HEADS), f32, 1.0, 6.0),
        'fox_w_out': nrm(10, (N_FOX, D_MODEL, D_MODEL), dsc),
        'ssm_w_in': nrm(11, (N_SSM, D_MODEL, SSM_IN), dsc),
        'ssm_conv_w': nrm(12, (N_SSM, SSM_CONV, SSM_CONV_DIM), SSM_CONV ** -0.5),
        'ssm_conv_b': nrm(13, (N_SSM, SSM_CONV_DIM), 0.02),
        'ssm_dt_bias': dt0 + jnp.log(-jnp.expm1(-dt0)),
        'ssm_a_log': jnp.log(jax.random.uniform(k[14], (N_SSM, SSM_HEADS), f32, 1.0, 16.0)),
        'ssm_d': 1.0 + nrm(16, (N_SSM, SSM_HEADS), 0.1),
        'ssm_gnorm': 1.0 + nrm(17, (N_SSM, SSM_DI), 0.02),
        'ssm_w_out': nrm(18, (N_SSM, SSM_DI, D_MODEL), SSM_DI ** -0.5),
        'peer_w_q': nrm(19, (DEPTH, D_MODEL, PEER_HEADS * PEER_DKEY), dsc),
        'peer_keys': nrm(20, (DEPTH, PEER_HEADS, 2, PEER_NKEYS, PEER_HALF), PEER_HALF ** -0.5),
        'peer_u': nrm(21, (DEPTH, PEER_EXPERTS, D_MODEL), dsc),
        'peer_v': nrm(22, (DEPTH, PEER_EXPERTS, D_MODEL), 0.05),
    }


def reference(x, norm_mix, norm_ffn, norm_final, hgrn_lb_logits, hgrn_w_in, hgrn_gnorm, hgrn_w_out,
              fox_w_in, fox_b_f, fox_w_out, ssm_w_in, ssm_conv_w, ssm_conv_b, ssm_dt_bias, ssm_a_log,
              ssm_d, ssm_gnorm, ssm_w_out, peer_w_q, peer_keys, peer_u, peer_v):
    lb_all = hgrn_lower_bounds(hgrn_lb_logits)
    h = x
    for i in range(DEPTH):
        kind, j = i % N_MIXERS, i // N_MIXERS
        hn = rmsnorm(h, norm_mix[i])
        if kind == 0:
            mix = hgrn2_mixer(hn, hgrn_w_in[j], hgrn_gnorm[j], hgrn_w_out[j], lb_all[i])
        elif kind == 1:
            mix = fox_mixer(hn, fox_w_in[j], fox_b_f[j], fox_w_out[j])
        else:
            mix = ssd_mixer(hn, ssm_w_in[j], ssm_conv_w[j], ssm_conv_b[j], ssm_dt_bias[j], ssm_a_log[j],
                            ssm_d[j], ssm_gnorm[j], ssm_w_out[j])
        h = h + mix.astype(h.dtype)
        ffn = peer_ffn(rmsnorm(h, norm_ffn[i]), peer_w_q[i], peer_keys[i], peer_u[i], peer_v[i])
        h = h + ffn.astype(h.dtype)
    return rmsnorm(h, norm_final)
```

```python
import contextlib
import numpy as np
import concourse.bass as bass
import concourse.mybir as mybir
from concourse.bass_utils import run_bass_kernel_spmd

F32 = mybir.dt.float32
BF16 = mybir.dt.bfloat16
U32 = mybir.dt.uint32
AF = mybir.ActivationFunctionType
ALU = mybir.AluOpType
AX = mybir.AxisListType

D = 1024
NCORES = 8
SEQ = 8192
EPS = 1e-6
NEG = -1.0e30
DEBUG_SCRATCH = False


class KB:
    R_DMA = 8

    def __init__(self):
        self.nc = bass.Bass("TRN2", target_bir_lowering=False)
        self.es = contextlib.ExitStack()
        nc = self.nc
        self.eng = {"pe": nc.tensor, "dve": nc.vector, "act": nc.scalar, "pool": nc.gpsimd, "sp": nc.sync}
        self.sems = {}
        for e in ("pe", "dve", "act", "pool"):
            self.sems[e] = self.es.enter_context(nc.semaphore("s_" + e))
        for q in ("sp", "pool", "act"):
            for s in range(self.R_DMA):
                self.sems[("dma", q, s)] = self.es.enter_context(nc.semaphore(f"d_{q}_{s}"))
        self.cnt = {k: 0 for k in self.sems}
        self.dma_i = {"sp": 0, "pool": 0, "act": 0}
        self.waited = {e: {} for e in self.eng}
        self.last_w = {}
        self.readers = {}
        self.n_ins = 0
        self.uid = 0
        self.psum_names = set()

    def sb(self, name, shape, dt, es=None):
        self.uid += 1
        return (es or self.es).enter_context(self.nc.sbuf_tensor(f"{name}_{self.uid}", list(shape), dt))

    def ps(self, name, shape, dt, es=None):
        self.uid += 1
        self.psum_names.add(name)
        shape = list(shape)
        esz = 2 if dt == BF16 else 4
        per_part = esz
        for s in shape[1:]:
            per_part *= s
        if per_part >= 2048 or len(shape) != 2:
            assert per_part % 2048 == 0, (name, shape)
            return (es or self.es).enter_context(self.nc.psum_tensor(f"{name}_{self.uid}", shape, dt))
        t = (es or self.es).enter_context(self.nc.psum_tensor(f"{name}_{self.uid}", [shape[0], 2048 // esz], dt))
        return t[:, 0:shape[1]]

    def dram(self, name, shape, dt, kind="Internal"):
        return self.nc.dram_tensor(name, list(shape), dt, kind=kind).ap()

    def _deps(self, reads, writes):
        deps = {}

        def add(h):
            if h is None:
                return
            k, c = h
            if deps.get(k, 0) < c:
                deps[k] = c

        for k in reads:
            add(self.last_w.get(k))
        for k in writes:
            add(self.last_w.get(k))
            for sk, c in self.readers.get(k, {}).items():
                add((sk, c))
        return deps

    def _waits(self, eng, deps):
        E = self.eng[eng]
        w = self.waited[eng]
        for sk, c in deps.items():
            if eng == "pe" and sk == "pe":
                continue
            if w.get(sk, 0) < c:
                E.wait_ge(self.sems[sk], c)
                w[sk] = c
                self.n_ins += 1

    def _record(self, h, reads, writes):
        sk, c = h
        for k in reads:
            self.readers.setdefault(k, {})[sk] = c
        for k in writes:
            self.last_w[k] = h
            self.readers[k] = {}

    def _excl(self, reads, writes):
        r2, w2 = [], list(writes)
        for k in reads:
            root = k if isinstance(k, str) else k[0]
            (w2 if root in self.psum_names else r2).append(k)
        return r2, w2

    def op(self, eng, fn, reads=(), writes=()):
        reads, writes = self._excl(reads, writes)
        deps = self._deps(reads, writes)
        self._waits(eng, deps)
        ins = fn(self.eng[eng])
        self.cnt[eng] += 1
        ins.then_inc(self.sems[eng], 1)
        self.n_ins += 1
        self._record((eng, self.cnt[eng]), reads, writes)

    def dma(self, q, fn, reads=(), writes=()):
        slot = self.dma_i[q] % self.R_DMA
        self.dma_i[q] += 1
        sk = ("dma", q, slot)
        reads, writes = self._excl(reads, writes)
        deps = self._deps(reads, writes)
        if self.cnt[sk] > 0:
            deps[sk] = max(deps.get(sk, 0), self.cnt[sk])
        self._waits(q, deps)
        ins = fn(self.eng[q])
        self.cnt[sk] += 16
        ins.then_inc(self.sems[sk], 16)
        self.n_ins += 1
        self._record((sk, self.cnt[sk]), reads, writes)

    def barrier(self):
        deps = {sk: c for sk, c in self.cnt.items() if c > 0}
        for eng in self.eng:
            self._waits(eng, dict(deps))

    def finish(self, keys):
        deps = self._deps(keys, ())
        self._waits("sp", deps)


class Common:
    def __init__(self, kb, T):
        self.kb = kb
        self.T = T
        self.NT = T // 128
        nc = kb.nc
        self.ident_d = kb.dram("c_ident", [128, 128], F32, kind="ExternalInput")
        self.ident = kb.sb("ident", [128, 128], BF16)
        kb.dma("pool", lambda e: e.dma_start(out=self.ident[:], in_=self.ident_d[:, :]), reads=(), writes=("ident",))


def rmsnorm_tile(kb, h_t, hk, gb, gk, out_t, ok, scr, sk, stat, stk, es_keys=()):
    kb.op("act", lambda e: e.activation(out=scr[:], in_=h_t[:], func=AF.Square, accum_out=stat[:, 0:1]),
          reads=(hk,), writes=(sk, stk))
    kb.op("dve", lambda e: e.tensor_scalar(out=stat[:, 1:2], in0=stat[:, 0:1], scalar1=1.0 / D, scalar2=EPS,
                                           op0=ALU.mult, op1=ALU.add), reads=(stk,), writes=(stk,))
    kb.op("act", lambda e: e.activation(out=stat[:, 2:3], in_=stat[:, 1:2], func=AF.Sqrt), reads=(stk,), writes=(stk,))
    kb.op("dve", lambda e: e.reciprocal(out=stat[:, 3:4], in_=stat[:, 2:3]), reads=(stk,), writes=(stk,))
    kb.op("dve", lambda e: e.scalar_tensor_tensor(out=out_t[:], in0=h_t[:], scalar=stat[:, 3:4], in1=gb[:],
                                                  op0=ALU.mult, op1=ALU.mult), reads=(hk, stk, gk), writes=(ok,))


def transpose_tile(kb, cm, src_bf, srck, dst_ap, dstk, pst, pstk, nch=8):
    for c in range(nch):
        kb.op("pe", lambda e, c=c: e.transpose(out=pst[:, c, :], in_=src_bf[:, c * 128:(c + 1) * 128], identity=cm.ident[:]),
              reads=(srck, "ident"), writes=(pstk,))
    kb.op("act", lambda e: e.activation(out=dst_ap, in_=pst[:, 0:nch, :], func=AF.Copy), reads=(pstk,), writes=(dstk,))


def peer_phase(kb, cm, li, hsrc, hsrc_name, hdst, hdst_name, W):
    NT = cm.NT
    NG = 4
    with contextlib.ExitStack() as es:
        sb = lambda n, s, d: kb.sb(n, s, d, es)
        ps = lambda n, s, d: kb.ps(n, s, d, es)
        wq = sb("wq", [128, 8, 2048], BF16)
        keysT = sb("keysT", [128, 16, 128], BF16)
        gb = sb("gb", [128, D], F32)
        h_t = [sb("h_t", [128, D], F32) for _ in range(2)]
        hn = sb("hn", [128, D], F32)
        hnb = sb("hnb", [128, D], BF16)
        hnT = sb("hnT", [128, 8, 128], BF16)
        scr = sb("scr", [128, D], F32)
        stat = sb("stat", [128, 4], F32)
        qT = sb("qT", [128, 16, 128], BF16)
        s_sb = sb("s_sb", [128, 16, 128], F32)
        wk = sb("wk", [128, 16, 128], F32)
        top = sb("top", [128, 16, 16], F32)
        ix = sb("ix", [128, 16, 16], U32)
        ixf = sb("ixf", [128, 16, 16], F32)
        cs = sb("cs", [128, 8, 256], F32)
        ci = sb("ci", [128, 8, 256], F32)
        wk2 = sb("wk2", [128, 8, 256], F32)
        top2 = sb("top2", [128, 8, 16], F32)
        nmax = sb("nmax", [128, 8], F32)
        ez = sb("ez", [128, 8, 16], F32)
        zz = sb("zz", [128, 8], F32)
        rz = sb("rz", [128, 8], F32)
        gates = sb("gates", [128, 128], F32)
        eidx = sb("eidx", [128, 128], F32)
        eidu = sb("eidu", [128, 128], U32)
        adot = sb("adot", [128, 128], F32)
        aact = sb("aact", [128, 128], F32)
        junk = sb("junk", [128, D], F32)
        junk2 = sb("junk2", [128, 256], F32)
        ug = [sb("ug", [128, D], F32) for _ in range(NG)]
        vg = [sb("vg", [128, D], F32) for _ in range(NG)]
        pst = ps("pst", [128, 8, 128], BF16)
        pq = [ps("pq", [128, 4, 128], F32) for _ in range(2)]
        psc = [ps("psc", [128, 4, 128], F32) for _ in range(4)]

        for c in range(8):
            kb.dma("pool", lambda e, c=c: e.dma_start(out=wq[:, c, :], in_=W["peer_w_q"][li, c * 128:(c + 1) * 128, :]),
                   writes=("wq",))
        kb.dma("pool", lambda e: e.dma_start(out=keysT[:], in_=W["peer_keysT"][li].rearrange("j d k -> d j k")),
               writes=("keysT",))
        kb.dma("sp", lambda e: e.dma_start(out=gb[:], in_=W["norm_ffn"][li].partition_broadcast(128)), writes=("gb",))

        for t in range(NT):
            hb = h_t[t % 2]
            hk = ("h_t", t % 2)
            kb.dma("sp", lambda e: e.dma_start(out=hb[:], in_=hsrc[t * 128:(t + 1) * 128, :]),
                   reads=((hsrc_name, t),), writes=(hk,))
            rmsnorm_tile(kb, hb, hk, gb, "gb", hn, "hn", scr, "scr", stat, "stat")
            kb.op("act", lambda e: e.activation(out=hnb[:], in_=hn[:], func=AF.Copy), reads=("hn",), writes=("hnb",))
            transpose_tile(kb, cm, hnb, "hnb", hnT[:, :, :], "hnT", pst, "pst")
            for jg in range(4):
                p = pq[jg % 2]
                pk = ("pq", jg % 2)
                for jj in range(4):
                    j = jg * 4 + jj
                    for c in range(8):
                        kb.op("pe", lambda e, c=c, j=j, jj=jj: e.matmul(p[:, jj, :], wq[:, c, j * 128:(j + 1) * 128], hnT[:, c, :],
                                                                    start=(c == 0), stop=(c == 7)),
                              reads=("wq", "hnT"), writes=(pk,))
                kb.op("act", lambda e, jg=jg: e.activation(out=qT[:, jg * 4:(jg + 1) * 4, :], in_=p[:], func=AF.Copy),
                      reads=(pk,), writes=(("qT", jg),))
            for jg in range(4):
                for jj in range(4):
                    j = jg * 4 + jj
                    kb.op("pe", lambda e, j=j, jj=jj, jg=jg: e.matmul(psc[jg][:, jj, :], qT[:, j, :], keysT[:, j, :], start=True, stop=True),
                          reads=(("qT", jg), "keysT"), writes=(("psc", jg),))
                kb.op("act", lambda e, jg=jg: e.activation(out=s_sb[:, jg * 4:(jg + 1) * 4, :], in_=psc[jg][:], func=AF.Copy),
                      reads=(("psc", jg),), writes=(("s_sb", jg),))
            for j in range(16):
                sk = ("s_sb", j // 4)
                tk = ("top", j)
                kb.op("dve", lambda e, j=j: e.max(out=top[:, j, 0:8], in_=s_sb[:, j, :]), reads=(sk,), writes=(tk,))
                kb.op("dve", lambda e, j=j: e.max_index(out=ix[:, j, 0:8], in_max=top[:, j, 0:8], in_values=s_sb[:, j, :]),
                      reads=(sk, tk), writes=(("ix", j),))
                kb.op("dve", lambda e, j=j: e.match_replace(out=wk[:, j, :], in_to_replace=top[:, j, 0:8], in_values=s_sb[:, j, :], imm_value=NEG),
                      reads=(sk, tk), writes=(("wk", j),))
                kb.op("dve", lambda e, j=j: e.max(out=top[:, j, 8:16], in_=wk[:, j, :]), reads=(("wk", j),), writes=(tk,))
                kb.op("dve", lambda e, j=j: e.max_index(out=ix[:, j, 8:16], in_max=top[:, j, 8:16], in_values=wk[:, j, :]),
                      reads=(("wk", j), tk), writes=(("ix", j),))
            allix = tuple(("ix", j) for j in range(16))
            alltop = tuple(("top", j) for j in range(16))
            kb.op("dve", lambda e: e.tensor_copy(ixf[:], ix[:]), reads=allix, writes=("ixf",))
            for hh in range(8):
                j0, j1 = 2 * hh, 2 * hh + 1
                csv = cs[:, hh, :].rearrange("p (a b) -> p a b", b=16)
                civ = ci[:, hh, :].rearrange("p (a b) -> p a b", b=16)
                kb.op("dve", lambda e, j0=j0, j1=j1, csv=csv: e.tensor_tensor(
                    out=csv, in0=top[:, j0, :].unsqueeze(2).to_broadcast([128, 16, 16]),
                    in1=top[:, j1, :].unsqueeze(1).to_broadcast([128, 16, 16]), op=ALU.add),
                    reads=alltop, writes=(("cs", hh),))
                kb.op("dve", lambda e, j0=j0, civ=civ: e.tensor_scalar(
                    out=civ, in0=ixf[:, j0, :].unsqueeze(2).to_broadcast([128, 16, 16]), scalar1=128.0, scalar2=None, op0=ALU.mult),
                    reads=("ixf",), writes=(("ci", hh),))
                kb.op("dve", lambda e, j1=j1, civ=civ: e.tensor_tensor(
                    out=civ, in0=civ, in1=ixf[:, j1, :].unsqueeze(1).to_broadcast([128, 16, 16]), op=ALU.add),
                    reads=("ixf", ("ci", hh)), writes=(("ci", hh),))
                t2k = ("top2", hh)
                kb.op("dve", lambda e, hh=hh: e.max(out=top2[:, hh, 0:8], in_=cs[:, hh, :]), reads=(("cs", hh),), writes=(t2k,))
                kb.op("dve", lambda e, hh=hh: e.match_replace(out=wk2[:, hh, :], in_to_replace=top2[:, hh, 0:8], in_values=cs[:, hh, :], imm_value=NEG),
                      reads=(("cs", hh), t2k), writes=(("wk2", hh),))
                kb.op("dve", lambda e, hh=hh: e.max(out=top2[:, hh, 8:16], in_=wk2[:, hh, :]), reads=(("wk2", hh),), writes=(t2k,))
                for k in range(16):
                    kb.op("dve", lambda e, hh=hh, k=k: e.scalar_tensor_tensor(
                        out=junk2[:], in0=cs[:, hh, :], scalar=top2[:, hh, k:k + 1], in1=ci[:, hh, :],
                        op0=ALU.is_equal, op1=ALU.mult, accum_out=eidx[:, hh * 16 + k:hh * 16 + k + 1]),
                        reads=(("cs", hh), ("ci", hh), t2k), writes=("junk2", "eidx"))
            allt2 = tuple(("top2", hh) for hh in range(8))
            kb.op("dve", lambda e: e.tensor_scalar(out=nmax[:], in0=top2[:, :, 0], scalar1=-1.0, scalar2=None, op0=ALU.mult),
                  reads=allt2, writes=("nmax",))
            for hh in range(8):
                kb.op("act", lambda e, hh=hh: e.activation(out=ez[:, hh, :], in_=top2[:, hh, :], func=AF.Exp,
                                                           bias=nmax[:, hh:hh + 1], scale=1.0, accum_out=zz[:, hh:hh + 1]),
                      reads=allt2 + ("nmax",), writes=("ez", "zz"))
            kb.op("dve", lambda e: e.reciprocal(out=rz[:], in_=zz[:]), reads=("zz",), writes=("rz",))
            kb.op("dve", lambda e: e.tensor_tensor(out=gates[:].rearrange("p (h k) -> p h k", k=16), in0=ez[:],
                                                   in1=rz[:].unsqueeze(2).to_broadcast([128, 8, 16]), op=ALU.mult),
                  reads=("ez", "rz"), writes=("gates",))
            kb.op("dve", lambda e: e.tensor_scalar(out=eidx[:], in0=eidx[:], scalar1=0.0, scalar2=16383.0, op0=ALU.max, op1=ALU.min),
                  reads=("eidx",), writes=("eidx",))
            kb.op("dve", lambda e: e.tensor_copy(eidu[:], eidx[:]), reads=("eidx",), writes=("eidu",))
            for s in range(128):
                b = s % NG
                kb.dma("pool", lambda e, s=s, b=b: e.indirect_dma_start(
                    out=ug[b][:], out_offset=None, in_=W["peer_u%d" % li],
                    in_offset=bass.IndirectOffsetOnAxis(ap=eidu[:, s:s + 1], axis=0)),
                    reads=("eidu",), writes=(("ug", b),))
                kb.op("dve", lambda e, s=s, b=b: e.scalar_tensor_tensor(
                    out=junk[:], in0=ug[b][:], scalar=1.0, in1=hn[:], op0=ALU.mult, op1=ALU.mult, accum_out=adot[:, s:s + 1]),
                    reads=(("ug", b), "hn"), writes=("junk", "adot"))
            kb.op("act", lambda e: e.activation(out=aact[:], in_=adot[:], func=AF.Gelu), reads=("adot",), writes=("aact",))
            kb.op("dve", lambda e: e.tensor_tensor(out=aact[:], in0=aact[:], in1=gates[:], op=ALU.mult),
                  reads=("aact", "gates"), writes=("aact",))
            for s in range(128):
                b = s % NG
                kb.dma("pool", lambda e, s=s, b=b: e.indirect_dma_start(
                    out=vg[b][:], out_offset=None, in_=W["peer_v%d" % li],
                    in_offset=bass.IndirectOffsetOnAxis(ap=eidu[:, s:s + 1], axis=0)),
                    reads=("eidu",), writes=(("vg", b),))
                kb.op("dve", lambda e, s=s, b=b: e.scalar_tensor_tensor(
                    out=hb[:], in0=vg[b][:], scalar=aact[:, s:s + 1], in1=hb[:], op0=ALU.mult, op1=ALU.add),
                    reads=(("vg", b), "aact", hk), writes=(hk,))
            kb.dma("sp", lambda e: e.dma_start(out=hdst[t * 128:(t + 1) * 128, :], in_=hb[:]),
                   reads=(hk,), writes=((hdst_name, t),))


        kb.barrier()
def load_norm_T(kb, cm, t, hsrc, hsrc_name, hb, hk, gb, gk, hn, hnb, scr, stat, pst, dst_ap, dstk):
    kb.dma("sp", lambda e: e.dma_start(out=hb[:], in_=hsrc[t * 128:(t + 1) * 128, :]), reads=((hsrc_name, t),), writes=(hk,))
    rmsnorm_tile(kb, hb, hk, gb, gk, hn, "hn", scr, "scr", stat, "stat")
    kb.op("act", lambda e: e.activation(out=hnb[:], in_=hn[:], func=AF.Copy), reads=("hn",), writes=("hnb",))
    transpose_tile(kb, cm, hnb, "hnb", dst_ap, dstk, pst, "pst")


def hgrn_phase(kb, cm, li, j, hsrc, hsrc_name, hdst, hdst_name, W):
    NT = cm.NT
    GT = min(4, NT)
    NTOK = GT * 128
    NCH = NTOK // 16
    SCALE = 128.0 ** -0.5
    with contextlib.ExitStack() as es:
        sb = lambda n, s, d: kb.sb(n, s, d, es)
        ps = lambda n, s, d: kb.ps(n, s, d, es)
        w_in = sb("w_in", [128, 8, 4096], BF16)
        w_out = sb("w_out", [128, 8, 1024], BF16)
        gb = sb("gb", [128, D], F32)
        gn = sb("gn", [128, 1], F32)
        lbz = sb("lbz", [128, 4, 8], F32)
        lbe = sb("lbe", [128, 4, 8], F32)
        den = sb("den", [128, 8], F32)
        num = sb("num", [128, 8], F32)
        lb = sb("lb", [128, 8], F32)
        oml = sb("oml", [128, 8], F32)
        epst = sb("epst", [128, 1], F32)
        rmask = sb("rmask", [128, 512], F32)
        maskT = sb("maskT", [128, 128], F32)
        cmask = sb("cmask", [128, 8], F32)
        ones = sb("ones", [128, 128], BF16)
        hb2 = [sb("hb", [128, D], F32) for _ in range(2)]
        hn = sb("hn", [128, D], F32)
        hnb = sb("hnb", [128, D], BF16)
        scr = sb("scr", [128, D], F32)
        stat = sb("stat", [128, 4], F32)
        hnT = sb("hnT", [128, 8, NTOK], BF16)
        v_tok = sb("v_tok", [128, GT, 1024], BF16)
        fs = sb("fs", [128, NTOK], F32)
        fT = sb("fT", [128, NTOK], F32)
        lf = sb("lf", [128, NTOK], F32)
        kk = sb("kk", [128, NTOK], F32)
        bT = sb("bT", [128, NTOK], F32)
        dT = sb("dT", [128, NTOK], F32)
        eb = sb("eb", [128, NTOK], F32)
        enb = sb("enb", [128, NTOK], F32)
        ed = sb("ed", [128, NTOK], F32)
        qd = sb("qd", [128, NTOK], BF16)
        kinv = sb("kinv", [128, NTOK], BF16)
        kdT = sb("kdT", [128, NTOK], BF16)
        kd_tok = sb("kd_tok", [128, GT, 128], BF16)
        sg = sb("sg", [128, NTOK], BF16)
        yT = sb("yT", [128, 8, NTOK], BF16)
        carry = [sb("carry", [128, 128], F32) for _ in range(8)]
        Sd = sb("Sd", [128, 128, 9], F32)
        So = sb("So", [128, 128, 9], F32)
        a9 = sb("a9", [128, 128, 9], F32)
        Sbf = sb("Sbf", [128, 8, 128], BF16)
        Vblk = sb("Vblk", [128, 8, 128], BF16)
        AT = sb("AT", [128, 128], BF16)
        osq = sb("osq", [128, 128], BF16)
        sdv = sb("sdv", [128, 128], F32)
        rsv = sb("rsv", [128, 128], F32)
        t1 = sb("t1", [128, 128], F32)
        pst = ps("pst", [128, 8, 128], BF16)
        pp = [ps("pp", [128, 512], F32) for _ in range(2)]
        pA = ps("pA", [128, 128], F32)
        pS = ps("pS", [128, 1024], F32)
        po = ps("po", [128, 128], F32)
        pss = ps("pss", [128, 128], F32)

        for c in range(8):
            kb.dma("pool", lambda e, c=c: e.dma_start(out=w_in[:, c, :], in_=W["hgrn_w_in"][j, c * 128:(c + 1) * 128, :]), writes=("w_in",))
            kb.dma("pool", lambda e, c=c: e.dma_start(out=w_out[:, c, :], in_=W["hgrn_w_out"][j, c * 128:(c + 1) * 128, :]), writes=("w_out",))
        kb.dma("pool", lambda e: e.dma_start(out=ones[:], in_=W["c_ones"][:, :]), writes=("ones",))
        kb.dma("sp", lambda e: e.dma_start(out=gb[:], in_=W["norm_mix"][li].partition_broadcast(128)), writes=("gb",))
        kb.dma("sp", lambda e: e.dma_start(out=gn[:], in_=W["hgrn_gnorm"][j].rearrange("(p o) -> p o", o=1)), writes=("gn",))
        kb.dma("sp", lambda e: e.dma_start(out=lbz[:], in_=W["hgrn_lbT"][:, :, :]), writes=("lbz",))
        kb.dma("sp", lambda e: e.dma_start(out=rmask[:], in_=W["c_rmask"][:, :]), writes=("rmask",))
        kb.dma("sp", lambda e: e.dma_start(out=maskT[:], in_=W["c_maskT16"][:, :]), writes=("maskT",))
        kb.dma("sp", lambda e: e.dma_start(out=cmask[:], in_=W["c_cmask"][:, :]), writes=("cmask",))
        kb.op("dve", lambda e: e.memset(epst[:], EPS), writes=("epst",))
        kb.op("dve", lambda e: e.memset(a9[:], 0.0), writes=("a9",))
        for hh in range(8):
            kb.op("dve", lambda e, hh=hh: e.memset(carry[hh][:], 0.0), writes=(("carry", hh),))
        kb.op("act", lambda e: e.activation(out=lbe[:], in_=lbz[:], func=AF.Exp), reads=("lbz",), writes=("lbe",))
        kb.op("dve", lambda e: e.tensor_tensor(out=den[:], in0=lbe[:, 0, :], in1=lbe[:, 1, :], op=ALU.add), reads=("lbe",), writes=("den",))
        kb.op("dve", lambda e: e.tensor_tensor(out=den[:], in0=den[:], in1=lbe[:, 2, :], op=ALU.add), reads=("lbe", "den"), writes=("den",))
        kb.op("dve", lambda e: e.tensor_tensor(out=den[:], in0=den[:], in1=lbe[:, 3, :], op=ALU.add), reads=("lbe", "den"), writes=("den",))
        kb.op("dve", lambda e: e.memset(num[:], 0.0), writes=("num",))
        for l in range(1, li + 1):
            kb.op("dve", lambda e, l=l: e.tensor_tensor(out=num[:], in0=num[:], in1=lbe[:, l, :], op=ALU.add), reads=("lbe", "num"), writes=("num",))
        kb.op("dve", lambda e: e.reciprocal(out=den[:], in_=den[:]), reads=("den",), writes=("den",))
        kb.op("dve", lambda e: e.tensor_tensor(out=lb[:], in0=num[:], in1=den[:], op=ALU.mult), reads=("num", "den"), writes=("lb",))
        kb.op("dve", lambda e: e.tensor_scalar(out=oml[:], in0=lb[:], scalar1=-1.0, scalar2=1.0, op0=ALU.mult, op1=ALU.add), reads=("lb",), writes=("oml",))

        ppi = [0]

        def proj_fm(col0):
            b = ppi[0] % 2
            ppi[0] += 1
            for c in range(8):
                kb.op("pe", lambda e, c=c: e.matmul(pp[b][:, 0:NTOK], w_in[:, c, col0:col0 + 128], hnT[:, c, :], start=(c == 0), stop=(c == 7)),
                      reads=("w_in", "hnT"), writes=(("pp", b),))
            return pp[b], ("pp", b)

        for g in range(NT // GT):
            for tt in range(GT):
                t = g * GT + tt
                load_norm_T(kb, cm, t, hsrc, hsrc_name, hb2[t % 2], ("hb", t % 2), gb, "gb", hn, hnb, scr, stat, pst,
                            hnT[:, :, tt * 128:(tt + 1) * 128], "hnT")
            for tt in range(GT):
                for cg in range(2):
                    b = ppi[0] % 2
                    ppi[0] += 1
                    for c in range(8):
                        kb.op("pe", lambda e, c=c: e.matmul(pp[b][:, :], hnT[:, c, tt * 128:(tt + 1) * 128],
                                                            w_in[:, c, 2048 + cg * 512:2048 + (cg + 1) * 512], start=(c == 0), stop=(c == 7)),
                              reads=("w_in", "hnT"), writes=(("pp", b),))
                    kb.op("act", lambda e: e.activation(out=v_tok[:, tt, cg * 512:(cg + 1) * 512], in_=pp[b][:, :], func=AF.Copy),
                          reads=(("pp", b),), writes=("v_tok",))
            for hh in range(8):
                p, pk = proj_fm(1024 + hh * 128)
                kb.op("act", lambda e: e.activation(out=fs[:], in_=p[:, 0:NTOK], func=AF.Sigmoid), reads=(pk,), writes=("fs",))
                kb.op("dve", lambda e: e.tensor_scalar(out=fT[:], in0=fs[:], scalar1=oml[:, hh:hh + 1], scalar2=lb[:, hh:hh + 1],
                                                       op0=ALU.mult, op1=ALU.add), reads=("fs", "oml", "lb"), writes=("fT",))
                kb.op("act", lambda e: e.activation(out=lf[:], in_=fT[:], func=AF.Ln), reads=("fT",), writes=("lf",))
                kb.op("pool", lambda e: e.tensor_scalar(out=kk[:], in0=fT[:], scalar1=-1.0, scalar2=1.0, op0=ALU.mult, op1=ALU.add),
                      reads=("fT",), writes=("kk",))
                kb.op("dve", lambda e: e.tensor_tensor_scan(out=bT[:], data0=rmask[:, 0:NTOK], data1=lf[:], initial=0.0,
                                                            op0=ALU.mult, op1=ALU.add), reads=("rmask", "lf"), writes=("bT",))
                b3 = bT[:].rearrange("p (c k) -> p c k", k=16)
                kb.op("dve", lambda e: e.tensor_tensor(out=dT[:].rearrange("p (c k) -> p c k", k=16),
                                                       in0=b3[:, :, 15:16].to_broadcast([128, NCH, 16]), in1=b3, op=ALU.subtract),
                      reads=("bT",), writes=("dT",))
                kb.op("act", lambda e: e.activation(out=eb[:], in_=bT[:], func=AF.Exp), reads=("bT",), writes=("eb",))
                kb.op("act", lambda e: e.activation(out=enb[:], in_=bT[:], func=AF.Exp, scale=-1.0), reads=("bT",), writes=("enb",))
                kb.op("act", lambda e: e.activation(out=ed[:], in_=dT[:], func=AF.Exp), reads=("dT",), writes=("ed",))
                kb.op("pool", lambda e: e.tensor_tensor(out=kinv[:], in0=kk[:], in1=enb[:], op=ALU.mult), reads=("kk", "enb"), writes=("kinv",))
                kb.op("pool", lambda e: e.tensor_tensor(out=kdT[:], in0=kk[:], in1=ed[:], op=ALU.mult), reads=("kk", "ed"), writes=("kdT",))
                p, pk = proj_fm(hh * 128)
                kb.op("dve", lambda e: e.scalar_tensor_tensor(out=qd[:], in0=p[:, 0:NTOK], scalar=SCALE, in1=eb[:], op0=ALU.mult, op1=ALU.mult),
                      reads=(pk, "eb"), writes=("qd",))
                p, pk = proj_fm(3072 + hh * 128)
                kb.op("act", lambda e: e.activation(out=sg[:], in_=p[:, 0:NTOK], func=AF.Silu), reads=(pk,), writes=("sg",))
                for tt in range(GT):
                    kb.op("pe", lambda e, tt=tt: e.transpose(out=pst[:, tt, :], in_=kdT[:, tt * 128:(tt + 1) * 128], identity=cm.ident[:]),
                          reads=("kdT", "ident"), writes=("pst",))
                kb.op("act", lambda e: e.activation(out=kd_tok[:, :, :], in_=pst[:, 0:GT, :], func=AF.Copy), reads=("pst",), writes=("kd_tok",))
                for tt in range(GT):
                    tsl = slice(tt * 128, (tt + 1) * 128)
                    vh = v_tok[:, tt, hh * 128:(hh + 1) * 128]
                    kb.op("pe", lambda e: e.matmul(pA[:, :], kinv[:, tsl], qd[:, tsl], start=True, stop=True), reads=("kinv", "qd"), writes=("pA",))
                    kb.op("dve", lambda e: e.tensor_tensor(out=AT[:], in0=pA[:, :], in1=maskT[:], op=ALU.mult), reads=("pA", "maskT"), writes=("AT",))
                    kb.op("pool", lambda e: e.tensor_tensor(out=Vblk[:], in0=vh.unsqueeze(1).to_broadcast([128, 8, 128]),
                                                            in1=cmask[:, :].unsqueeze(2).to_broadcast([128, 8, 128]), op=ALU.mult),
                          reads=("v_tok", "cmask"), writes=("Vblk",))
                    for half in range(2):
                        kb.op("pe", lambda e, half=half: e.matmul(pS[:, half * 512:(half + 1) * 512], kd_tok[:, tt, :],
                                                                  Vblk[:, half * 4:(half + 1) * 4, :].rearrange("p c v -> p (c v)"), start=True, stop=True),
                              reads=("kd_tok", "Vblk"), writes=("pS",))
                    kb.op("act", lambda e: e.activation(out=Sd[:, :, 1:9], in_=pS[:, :].rearrange("k (c v) -> k v c", v=128), func=AF.Copy),
                          reads=("pS",), writes=("Sd",))
                    kb.op("pool", lambda e: e.tensor_copy(Sd[:, :, 0], carry[hh][:]), reads=(("carry", hh), "Sd"), writes=("Sd",))
                    ebv = eb[:].rearrange("p (c k) -> p c k", k=16)
                    kb.op("act", lambda e: e.activation(out=a9[:, :, 1:9], in_=ebv[:, tt * 8:(tt + 1) * 8, 15].unsqueeze(1).to_broadcast([128, 128, 8]), func=AF.Copy), reads=("eb",), writes=("a9",))
                    kb.op("dve", lambda e: e.tensor_tensor_scan(out=So[:].rearrange("k v c -> k (v c)"), data0=a9[:].rearrange("k v c -> k (v c)"),
                                                                data1=Sd[:].rearrange("k v c -> k (v c)"), initial=0.0, op0=ALU.mult, op1=ALU.add),
                          reads=("a9", "Sd"), writes=("So",))
                    kb.op("act", lambda e: e.activation(out=carry[hh][:], in_=So[:, :, 8], func=AF.Copy), reads=("So",), writes=(("carry", hh),))
                    kb.op("pool", lambda e: e.tensor_copy(Sbf[:].rearrange("k c v -> k v c"), So[:, :, 0:8]), reads=("So",), writes=("Sbf",))
                    for c in range(8):
                        csl = slice(16 * c, 16 * c + 16)
                        kb.op("pe", lambda e, c=c: e.matmul(po[:, csl], Sbf[:, c, :], qd[:, tt * 128 + 16 * c:tt * 128 + 16 * c + 16], start=True, stop=False),
                              reads=("Sbf", "qd"), writes=("po",))
                        kb.op("pe", lambda e, c=c: e.matmul(po[:, csl], vh, AT[:, csl], start=False, stop=True),
                              reads=("v_tok", "AT"), writes=("po",))
                    kb.op("act", lambda e: e.activation(out=osq[:], in_=po[:, :], func=AF.Square), reads=("po",), writes=("osq",))
                    kb.op("pe", lambda e: e.matmul(pss[:, :], ones[:], osq[:], start=True, stop=True), reads=("ones", "osq"), writes=("pss",))
                    kb.op("act", lambda e: e.activation(out=sdv[:], in_=pss[:, :], func=AF.Sqrt, bias=epst[:, 0:1], scale=1.0 / 128.0),
                          reads=("pss", "epst"), writes=("sdv",))
                    kb.op("dve", lambda e: e.reciprocal(out=rsv[:], in_=sdv[:]), reads=("sdv",), writes=("rsv",))
                    kb.op("dve", lambda e: e.tensor_tensor(out=t1[:], in0=po[:, :], in1=rsv[:], op=ALU.mult), reads=("po", "rsv"), writes=("t1",))
                    kb.op("dve", lambda e: e.scalar_tensor_tensor(out=yT[:, hh, tsl], in0=t1[:], scalar=gn[:, 0:1], in1=sg[:, tsl],
                                                                  op0=ALU.mult, op1=ALU.mult), reads=("t1", "gn", "sg"), writes=("yT",))
            for tt in range(GT):
                t = g * GT + tt
                hb = hb2[t % 2]
                hk = ("hb", t % 2)
                for cg in range(2):
                    for hh in range(8):
                        kb.op("pe", lambda e, hh=hh: e.matmul(pS[:, cg * 512:(cg + 1) * 512], yT[:, hh, tt * 128:(tt + 1) * 128],
                                                              w_out[:, hh, cg * 512:(cg + 1) * 512], start=(hh == 0), stop=(hh == 7)),
                              reads=("yT", "w_out"), writes=("pS",))
                kb.dma("sp", lambda e: e.dma_start(out=hb[:], in_=hsrc[t * 128:(t + 1) * 128, :]), reads=((hsrc_name, t),), writes=(hk,))
                for cg in range(2):
                    kb.op("dve", lambda e: e.tensor_tensor(out=hb[:, cg * 512:(cg + 1) * 512], in0=pS[:, cg * 512:(cg + 1) * 512],
                                                           in1=hb[:, cg * 512:(cg + 1) * 512], op=ALU.add), reads=("pS", hk), writes=(hk,))
                kb.dma("sp", lambda e: e.dma_start(out=hdst[t * 128:(t + 1) * 128, :], in_=hb[:]), reads=(hk,), writes=((hdst_name, t),))


        kb.barrier()
def fox_phase(kb, cm, li, hsrc, hsrc_name, hdst, hdst_name, W, scr_d):
    NT = cm.NT
    T = cm.T
    GT = min(4, NT)
    NTOK = GT * 128
    qT_d, kT_d, v_d, ca_d, o_d = scr_d["qT"], scr_d["kT"], scr_d["v"], scr_d["ca"], scr_d["o"]
    with contextlib.ExitStack() as es:
        sb = lambda n, s, d: kb.sb(n, s, d, es)
        ps = lambda n, s, d: kb.ps(n, s, d, es)
        w_in = sb("fw_in", [128, 8, 3072], BF16)
        w_f = sb("fw_f", [128, 8, 16], BF16)
        gb = sb("gb", [128, D], F32)
        bfb = sb("bfb", [128, 16], F32)
        tri = sb("tri", [128, 128], F32)
        onesf = sb("onesf", [128, 128], F32)
        hb2 = [sb("hb", [128, D], F32) for _ in range(2)]
        hn = sb("hn", [128, D], F32)
        hnb = sb("hnb", [128, D], BF16)
        scr = sb("scr", [128, D], F32)
        stat = sb("stat", [128, 4], F32)
        hnT = sb("hnT", [128, 8, NTOK], BF16)
        ob = [sb("ob", [128, NTOK], BF16) for _ in range(2)]
        vb = [sb("vb", [128, 1024], BF16) for _ in range(2)]
        fz = sb("fz", [128, 16], F32)
        fe = sb("fe", [128, 16], F32)
        lf = sb("lf", [128, 16], F32)
        negc = sb("negc", [128, 16], F32)
        carry_b = sb("carry_b", [128, 16], F32)
        ctok = sb("ctok", [128, 16], F32)
        negct = sb("negct", [128, 16], F32)
        identf = sb("identf", [128, 128], F32)
        cT = sb("cT", [16, NTOK], F32)
        hi = sb("hi", [16, NTOK], BF16)
        hi32 = sb("hi32", [16, NTOK], F32)
        r1 = sb("r1", [16, NTOK], F32)
        mid = sb("mid", [16, NTOK], BF16)
        mid32 = sb("mid32", [16, NTOK], F32)
        r2 = sb("r2", [16, NTOK], F32)
        lo = sb("lo", [16, NTOK], BF16)
        pst = ps("pst", [128, 8, 128], BF16)
        pp = [ps("pp", [128, 512], F32) for _ in range(2)]
        pf = ps("pf", [128, 16], F32)
        pc = ps("pc", [16, NTOK], F32)
        pct = ps("pct", [128, 16], F32)
        ptot = ps("ptot", [128, 16], F32)

        for c in range(8):
            kb.dma("pool", lambda e, c=c: e.dma_start(out=w_in[:, c, :], in_=W["fox_w_in"][0, c * 128:(c + 1) * 128, 0:3072]), writes=("fw_in",))
            kb.dma("pool", lambda e, c=c: e.dma_start(out=w_f[:, c, :], in_=W["fox_w_in"][0, c * 128:(c + 1) * 128, 3072:3088]), writes=("fw_f",))
        kb.dma("sp", lambda e: e.dma_start(out=gb[:], in_=W["norm_mix"][li].partition_broadcast(128)), writes=("gb",))
        kb.dma("sp", lambda e: e.dma_start(out=bfb[:], in_=W["fox_b_f"][0].partition_broadcast(128)), writes=("bfb",))
        kb.dma("sp", lambda e: e.dma_start(out=tri[:], in_=W["c_tri"][:, :]), writes=("tri",))
        kb.dma("sp", lambda e: e.dma_start(out=onesf[:], in_=W["c_ones"][:, :]), writes=("onesf",))
        kb.op("dve", lambda e: e.memset(carry_b[:], 0.0), writes=("carry_b",))
        kb.dma("sp", lambda e: e.dma_start(out=identf[:], in_=cm.ident_d[:, :]), writes=("identf",))
        ppi = [0]
        for g in range(NT // GT):
            for tt in range(GT):
                t = g * GT + tt
                load_norm_T(kb, cm, t, hsrc, hsrc_name, hb2[t % 2], ("hb", t % 2), gb, "gb", hn, hnb, scr, stat, pst,
                            hnT[:, :, tt * 128:(tt + 1) * 128], "hnT")
            for which, dst in ((0, qT_d), (1, kT_d)):
                for ch in range(8):
                    b = ppi[0] % 2
                    ppi[0] += 1
                    col0 = which * 1024 + ch * 128
                    for c in range(8):
                        kb.op("pe", lambda e, c=c: e.matmul(pp[b][:, 0:NTOK], w_in[:, c, col0:col0 + 128], hnT[:, c, :], start=(c == 0), stop=(c == 7)),
                              reads=("fw_in", "hnT"), writes=(("pp", b),))
                    kb.op("act", lambda e: e.activation(out=ob[b][:], in_=pp[b][:, 0:NTOK], func=AF.Copy, scale=(0.125 if which == 0 else 1.0)),
                          reads=(("pp", b),), writes=(("ob", b),))
                    kb.dma("sp", lambda e: e.dma_start(out=dst[ch * 128:(ch + 1) * 128, g * NTOK:(g + 1) * NTOK], in_=ob[b][:]),
                           reads=(("ob", b),), writes=(("qk_d", which, ch, g),))
            for tt in range(GT):
                t = g * GT + tt
                vbb = vb[t % 2]
                for cg in range(2):
                    b = ppi[0] % 2
                    ppi[0] += 1
                    for c in range(8):
                        kb.op("pe", lambda e, c=c: e.matmul(pp[b][:, :], hnT[:, c, tt * 128:(tt + 1) * 128],
                                                            w_in[:, c, 2048 + cg * 512:2048 + (cg + 1) * 512], start=(c == 0), stop=(c == 7)),
                              reads=("fw_in", "hnT"), writes=(("pp", b),))
                    kb.op("act", lambda e: e.activation(out=vbb[:, cg * 512:(cg + 1) * 512], in_=pp[b][:, :], func=AF.Copy),
                          reads=(("pp", b),), writes=(("vb", t % 2),))
                kb.dma("sp", lambda e: e.dma_start(out=v_d[t * 128:(t + 1) * 128, :], in_=vbb[:]), reads=(("vb", t % 2),), writes=(("v_d", t),))
                for c in range(8):
                    kb.op("pe", lambda e, c=c: e.matmul(pf[:, :], hnT[:, c, tt * 128:(tt + 1) * 128], w_f[:, c, :], start=(c == 0), stop=(c == 7)),
                          reads=("fw_f", "hnT"), writes=("pf",))
                kb.op("dve", lambda e: e.tensor_tensor(out=fz[:], in0=pf[:, :], in1=bfb[:], op=ALU.add), reads=("pf", "bfb"), writes=("fz",))
                kb.op("act", lambda e: e.activation(out=fe[:], in_=fz[:], func=AF.Exp, scale=-1.0), reads=("fz",), writes=("fe",))
                kb.op("dve", lambda e: e.tensor_scalar(out=fe[:], in0=fe[:], scalar1=1.0, scalar2=None, op0=ALU.add), reads=("fe",), writes=("fe",))
                kb.op("act", lambda e: e.activation(out=lf[:], in_=fe[:], func=AF.Ln), reads=("fe",), writes=("lf",))
                kb.op("dve", lambda e: e.tensor_scalar(out=lf[:], in0=lf[:], scalar1=-1.0, scalar2=None, op0=ALU.mult), reads=("lf",), writes=("lf",))
                kb.op("pe", lambda e: e.matmul(pct[:, :], tri[:], lf[:], start=True, stop=True), reads=("lf", "tri"), writes=("pct",))
                kb.op("dve", lambda e: e.tensor_tensor(out=ctok[:], in0=pct[:, :], in1=carry_b[:], op=ALU.add), reads=("pct", "carry_b"), writes=("ctok",))
                kb.op("pe", lambda e: e.matmul(ptot[:, :], onesf[:], lf[:], start=True, stop=True), reads=("lf", "onesf"), writes=("ptot",))
                kb.op("dve", lambda e: e.tensor_tensor(out=carry_b[:], in0=carry_b[:], in1=ptot[:, :], op=ALU.add), reads=("ptot", "carry_b"), writes=("carry_b",))
                kb.op("act", lambda e: e.activation(out=negct[:], in_=ctok[:], func=AF.Copy, scale=-1.0), reads=("ctok",), writes=("negct",))
                kb.dma("sp", lambda e: e.dma_start(out=scr_d["negc"][t * 128:(t + 1) * 128, :], in_=negct[:]), reads=("negct",), writes=(("negc_d", t),))
                kb.op("pe", lambda e: e.transpose(out=pc[:, tt * 128:(tt + 1) * 128], in_=ctok[:], identity=identf[:]), reads=("ctok", "identf"), writes=("pc",))
            kb.op("act", lambda e: e.activation(out=cT[:], in_=pc[:, :], func=AF.Copy), reads=("pc",), writes=("cT",))
            kb.op("dve", lambda e: e.tensor_copy(hi[:], cT[:]), reads=("cT",), writes=("hi",))
            kb.op("dve", lambda e: e.tensor_copy(hi32[:], hi[:]), reads=("hi",), writes=("hi32",))
            kb.op("dve", lambda e: e.tensor_tensor(out=r1[:], in0=cT[:], in1=hi32[:], op=ALU.subtract), reads=("cT", "hi32"), writes=("r1",))
            kb.op("dve", lambda e: e.tensor_copy(mid[:], r1[:]), reads=("r1",), writes=("mid",))
            kb.op("dve", lambda e: e.tensor_copy(mid32[:], mid[:]), reads=("mid",), writes=("mid32",))
            kb.op("dve", lambda e: e.tensor_tensor(out=r2[:], in0=r1[:], in1=mid32[:], op=ALU.subtract), reads=("r1", "mid32"), writes=("r2",))
            kb.op("dve", lambda e: e.tensor_copy(lo[:], r2[:]), reads=("r2",), writes=("lo",))
            for k3, src_t, sk in ((0, hi, "hi"), (1, mid, "mid"), (2, lo, "lo")):
                kb.dma("sp", lambda e: e.dma_start(out=ca_d[:, k3, g * NTOK:(g + 1) * NTOK], in_=src_t[:]), reads=(sk,), writes=(("ca_d", g),))

        kb.barrier()
    allqk = tuple(("qk_d", w, ch, g) for w in range(2) for ch in range(8) for g in range(NT // GT))
    allv = tuple(("v_d", t) for t in range(NT))
    allca = tuple(("ca_d", g) for g in range(NT // GT))
    allnegc = tuple(("negc_d", t) for t in range(NT))
    with contextlib.ExitStack() as es:
        sb = lambda n, s, d: kb.sb(n, s, d, es)
        ps = lambda n, s, d: kb.ps(n, s, d, es)
        q_aug = [sb("q_aug", [67, T], BF16) for _ in range(2)]
        k_aug = [sb("k_aug", [67, T], BF16) for _ in range(2)]
        V_aug = [sb("V_aug", [128, NT, 65], BF16) for _ in range(2)]
        negc = sb("negc", [128, NT, 16], F32)
        identb = cm.ident
        nmask = sb("nmask", [128, 128], BF16)
        PT = [sb("PT", [128, 512], BF16) for _ in range(2)]
        rcp = sb("rcp", [128, 1], F32)
        otk = [sb("otk", [128, 64], BF16) for _ in range(2)]
        pS = [ps("pS", [128, 512], F32) for _ in range(2)]
        pO = [ps("pO", [128, 65], F32) for _ in range(4)]
        kb.dma("pool", lambda e: e.dma_start(out=nmask[:], in_=W["c_negmask"][:, :]), writes=("nmask",))
        kb.dma("sp", lambda e: e.dma_start(out=negc[:], in_=scr_d["negc"][:, :].rearrange("(t p) h -> p t h", p=128)), reads=allnegc, writes=("negc",))
        si = [0]
        for hh in range(16):
            hb_ = hh % 2
            qa, ka, va = q_aug[hb_], k_aug[hb_], V_aug[hb_]
            qk_, kk_, vk_ = ("q_aug", hb_), ("k_aug", hb_), ("V_aug", hb_)
            kb.dma("sp", lambda e: e.dma_start(out=qa[0:64, :], in_=qT_d[hh * 64:(hh + 1) * 64, :]), reads=allqk, writes=(qk_,))
            kb.dma("sp", lambda e: e.dma_start(out=qa[64:67, :], in_=ca_d[hh, :, :]), reads=allca, writes=(qk_,))
            kb.dma("sp", lambda e: e.dma_start(out=ka[0:64, :], in_=kT_d[hh * 64:(hh + 1) * 64, :]), reads=allqk, writes=(kk_,))
            kb.dma("pool", lambda e: e.dma_start(out=ka[64:67, :], in_=W["c_ones3"][:, 0:T]), writes=(kk_,))
            kb.dma("sp", lambda e: e.dma_start(out=va[:, :, 0:64], in_=v_d[:, hh * 64:(hh + 1) * 64].rearrange("(t p) d -> p t d", p=128)),
                   reads=allv, writes=(vk_,))
            kb.dma("pool", lambda e: e.dma_start(out=va[:, :, 64:65], in_=W["c_ones3"][0:1, 0:NT * 128].rearrange("o (t p) -> p t o", p=128),
                                                 allow_slow_non_contiguous=True), writes=(vk_,))
            for i in range(NT // GT):
                nj = GT * i + GT
                for j in range(nj):
                    r = max(0, j - GT * i)
                    b = si[0] % 2
                    si[0] += 1
                    lhs = ka[0:67, j * 128:(j + 1) * 128]
                    c0 = i * NTOK
                    if j >= GT * i:
                        kb.op("pe", lambda e: e.matmul(pS[b][:, r * 128:(r + 1) * 128], lhs, qa[0:67, c0 + r * 128:c0 + (r + 1) * 128], start=True, stop=False),
                              reads=(qk_, kk_), writes=(("pS", b),))
                        kb.op("pe", lambda e: e.matmul(pS[b][:, r * 128:(r + 1) * 128], identb[:], nmask[:], start=False, stop=True),
                              reads=("ident", "nmask"), writes=(("pS", b),))
                        if r < GT - 1:
                            kb.op("pe", lambda e: e.matmul(pS[b][:, (r + 1) * 128:NTOK], lhs, qa[0:67, c0 + (r + 1) * 128:c0 + NTOK], start=True, stop=True),
                                  reads=(qk_, kk_), writes=(("pS", b),))
                    else:
                        kb.op("pe", lambda e: e.matmul(pS[b][:, 0:NTOK], lhs, qa[0:67, c0:c0 + NTOK], start=True, stop=True),
                              reads=(qk_, kk_), writes=(("pS", b),))
                    kb.op("act", lambda e: e.activation(out=PT[b][:, r * 128:NTOK], in_=pS[b][:, r * 128:NTOK], func=AF.Exp,
                                                        bias=negc[:, j, hh:hh + 1], scale=1.0), reads=(("pS", b), "negc"), writes=(("PT", b),))
                    for rr in range(r, GT):
                        kb.op("pe", lambda e, rr=rr: e.matmul(pO[rr][:, :], PT[b][:, rr * 128:(rr + 1) * 128], va[:, j, :], start=(j == 0), stop=(j == GT * i + rr)),
                              reads=(("PT", b), vk_), writes=(("pO", rr),))
                for rr in range(GT):
                    t = GT * i + rr
                    ob_ = otk[t % 2]
                    kb.op("dve", lambda e: e.reciprocal(out=rcp[:], in_=pO[rr][:, 64:65]), reads=(("pO", rr),), writes=("rcp",))
                    kb.op("dve", lambda e: e.tensor_scalar(out=ob_[:], in0=pO[rr][:, 0:64], scalar1=rcp[:, 0:1], scalar2=None, op0=ALU.mult),
                          reads=(("pO", rr), "rcp"), writes=(("otk", t % 2),))
                    kb.dma("sp", lambda e: e.dma_start(out=o_d[t * 128:(t + 1) * 128, hh * 64:(hh + 1) * 64], in_=ob_[:]),
                           reads=(("otk", t % 2),), writes=(("o_d", t, hh),))

        kb.barrier()
    with contextlib.ExitStack() as es:
        sb = lambda n, s, d: kb.sb(n, s, d, es)
        ps = lambda n, s, d: kb.ps(n, s, d, es)
        w_out = sb("fw_out", [128, 8, 1024], BF16)
        hb2 = [sb("hb", [128, D], F32) for _ in range(2)]
        o_t = [sb("o_t", [128, D], BF16) for _ in range(2)]
        oT = sb("oT", [128, 8, 128], BF16)
        pst = ps("pst", [128, 8, 128], BF16)
        pm = ps("pm", [128, 1024], F32)
        for c in range(8):
            kb.dma("pool", lambda e, c=c: e.dma_start(out=w_out[:, c, :], in_=W["fox_w_out"][0, c * 128:(c + 1) * 128, :]), writes=("fw_out",))
        for t in range(NT):
            b = t % 2
            kb.dma("sp", lambda e: e.dma_start(out=o_t[b][:], in_=o_d[t * 128:(t + 1) * 128, :]),
                   reads=tuple(("o_d", t, hh) for hh in range(16)), writes=(("o_t", b),))
            kb.dma("sp", lambda e: e.dma_start(out=hb2[b][:], in_=hsrc[t * 128:(t + 1) * 128, :]), reads=((hsrc_name, t),), writes=(("hb", b),))
            transpose_tile(kb, cm, o_t[b], ("o_t", b), oT[:, :, :], "oT", pst, "pst")
            for cg in range(2):
                for c in range(8):
                    kb.op("pe", lambda e, c=c: e.matmul(pm[:, cg * 512:(cg + 1) * 512], oT[:, c, :], w_out[:, c, cg * 512:(cg + 1) * 512], start=(c == 0), stop=(c == 7)),
                          reads=("oT", "fw_out"), writes=("pm",))
                kb.op("dve", lambda e: e.tensor_tensor(out=hb2[b][:, cg * 512:(cg + 1) * 512], in0=pm[:, cg * 512:(cg + 1) * 512],
                                                       in1=hb2[b][:, cg * 512:(cg + 1) * 512], op=ALU.add), reads=("pm", ("hb", b)), writes=(("hb", b),))
            kb.dma("sp", lambda e: e.dma_start(out=hdst[t * 128:(t + 1) * 128, :], in_=hb2[b][:]), reads=(("hb", b),), writes=((hdst_name, t),))


        kb.barrier()
def ssd_phase(kb, cm, li, hsrc, hsrc_name, hdst, hdst_name, W, y_d):
    NT = cm.NT
    GT = min(2, NT)
    NTOK = GT * 128
    with contextlib.ExitStack() as es:
        sb = lambda n, s, d: kb.sb(n, s, d, es)
        ps = lambda n, s, d: kb.ps(n, s, d, es)
        w_x = sb("w_x", [128, 8, 4096], BF16)
        w_dt = sb("w_dt", [128, 8, 32], BF16)
        gb = sb("gb", [128, D], F32)
        cw = sb("cw", [128, 32, 4], F32)
        cbias = sb("cbias", [128, 32], F32)
        dtb = sb("dtb", [128, 32], F32)
        aneg = sb("aneg", [128, 32], F32)
        Db = sb("Db", [128, 32], F32)
        tri = sb("tri", [128, 128], F32)
        onesf = sb("onesf", [128, 128], F32)
        identf = sb("identf", [128, 128], F32)
        nmaskf = sb("nmaskf", [128, 128], F32)
        cmaskT = sb("cmaskT", [128, 128], F32)
        hb2 = [sb("hb", [128, D], F32) for _ in range(2)]
        hn = sb("hn", [128, D], F32)
        hnb = sb("hnb", [128, D], BF16)
        scr = sb("scr", [128, D], F32)
        stat = sb("stat", [128, 4], F32)
        hnT = sb("hnT", [128, 8, NTOK], BF16)
        xp = [sb("xp", [128, NTOK + 3], F32) for _ in range(2)]
        acc = [sb("acc", [128, NTOK], F32) for _ in range(2)]
        halo = sb("halo", [128, 32, 3], F32)
        xc = sb("xc", [128, 32, NTOK], BF16)
        x_tok = sb("x_tok", [128, GT, 2048], BF16)
        B_tok = sb("B_tok", [128, GT, 1024], BF16)
        xb = sb("xb", [128, 32], F32)
        dtt = sb("dtt", [128, 32], F32)
        dA = sb("dA", [128, 32], F32)
        cum = sb("cum", [128, 32], F32)
        negcum = sb("negcum", [128, 32], F32)
        dd = sb("dd", [128, 32], F32)
        dec_end = sb("dec_end", [128, 32], F32)
        ecum = sb("ecum", [128, 32], F32)
        etot = sb("etot", [128, 32], F32)
        xdt = sb("xdt", [128, 2048], BF16)
        xdd = sb("xdd", [128, 2048], BF16)
        cbm = sb("cbm", [128, 128], F32)
        tsc = sb("tsc", [128, 128], F32)
        LT = sb("LT", [128, 128], F32)
        MT = sb("MT", [128, 128], BF16)
        S = sb("S", [128, 32, 64], F32)
        S_bf = sb("S_bf", [128, 32, 64], BF16)
        yi = sb("yi", [128, 256], F32)
        y_sb = sb("y_sb", [128, 2048], F32)
        tmp = sb("tmp", [128, 2048], F32)
        pst = ps("pst", [128, 8, 128], BF16)
        pp = [ps("pp", [128, 512], F32) for _ in range(2)]
        pdc = ps("pdc", [128, 96], F32)
        pcb = ps("pcb", [128, 128], F32)
        pcr = ps("pcr", [128, 128], F32)
        py = ps("py", [128, 512], F32)
        pSu = ps("pSu", [128, 256], F32)

        for c in range(8):
            kb.dma("pool", lambda e, c=c: e.dma_start(out=w_x[:, c, :], in_=W["ssm_w_in"][0, c * 128:(c + 1) * 128, 2048:6144]), writes=("w_x",))
            kb.dma("pool", lambda e, c=c: e.dma_start(out=w_dt[:, c, :], in_=W["ssm_w_in"][0, c * 128:(c + 1) * 128, 6144:6176]), writes=("w_dt",))
        kb.dma("sp", lambda e: e.dma_start(out=gb[:], in_=W["norm_mix"][li].partition_broadcast(128)), writes=("gb",))
        kb.dma("sp", lambda e: e.dma_start(out=cw[:], in_=W["ssm_conv_wT"][:, :, :]), writes=("cw",))
        kb.dma("sp", lambda e: e.dma_start(out=cbias[:], in_=W["ssm_conv_bT"][:, :]), writes=("cbias",))
        kb.dma("sp", lambda e: e.dma_start(out=dtb[:], in_=W["ssm_dt_bias"][0].partition_broadcast(128)), writes=("dtb",))
        kb.dma("sp", lambda e: e.dma_start(out=aneg[:], in_=W["ssm_a_log"][0].partition_broadcast(128)), writes=("aneg",))
        kb.dma("sp", lambda e: e.dma_start(out=Db[:], in_=W["ssm_d"][0].partition_broadcast(128)), writes=("Db",))
        kb.dma("sp", lambda e: e.dma_start(out=tri[:], in_=W["c_tri"][:, :]), writes=("tri",))
        kb.dma("sp", lambda e: e.dma_start(out=cmaskT[:], in_=W["c_tri"][:, :]), writes=("cmaskT",))
        kb.dma("sp", lambda e: e.dma_start(out=onesf[:], in_=W["c_ones"][:, :]), writes=("onesf",))
        kb.dma("sp", lambda e: e.dma_start(out=identf[:], in_=cm.ident_d[:, :]), writes=("identf",))
        kb.dma("sp", lambda e: e.dma_start(out=nmaskf[:], in_=W["c_negmask"][:, :]), writes=("nmaskf",))
        kb.op("act", lambda e: e.activation(out=aneg[:], in_=aneg[:], func=AF.Exp), reads=("aneg",), writes=("aneg",))
        kb.op("dve", lambda e: e.tensor_scalar(out=aneg[:], in0=aneg[:], scalar1=-1.0, scalar2=None, op0=ALU.mult), reads=("aneg",), writes=("aneg",))
        kb.op("dve", lambda e: e.memset(halo[:], 0.0), writes=("halo",))
        kb.op("dve", lambda e: e.memset(S[:], 0.0), writes=("S",))
        kb.op("dve", lambda e: e.memset(S_bf[:], 0.0), writes=("S_bf",))
        ppi = [0]
        for g2 in range(NT // GT):
            for tt in range(GT):
                t = g2 * GT + tt
                load_norm_T(kb, cm, t, hsrc, hsrc_name, hb2[t % 2], ("hb", t % 2), gb, "gb", hn, hnb, scr, stat, pst,
                            hnT[:, :, tt * 128:(tt + 1) * 128], "hnT")
            for ch in range(32):
                b = ppi[0] % 2
                ppi[0] += 1
                for c in range(8):
                    kb.op("pe", lambda e, c=c: e.matmul(pp[b][:, 0:NTOK], w_x[:, c, ch * 128:(ch + 1) * 128], hnT[:, c, :], start=(c == 0), stop=(c == 7)),
                          reads=("w_x", "hnT"), writes=(("pp", b),))
                xk, ak = ("xp", b), ("acc", b)
                kb.op("act", lambda e: e.activation(out=xp[b][:, 3:3 + NTOK], in_=pp[b][:, 0:NTOK], func=AF.Copy), reads=(("pp", b),), writes=(xk,))
                kb.op("pool", lambda e: e.tensor_copy(xp[b][:, 0:3], halo[:, ch, :]), reads=("halo", xk), writes=(xk,))
                kb.op("dve", lambda e: e.tensor_scalar(out=acc[b][:], in0=xp[b][:, 0:NTOK], scalar1=cw[:, ch, 0:1], scalar2=cbias[:, ch:ch + 1],
                                                       op0=ALU.mult, op1=ALU.add), reads=(xk, "cw", "cbias"), writes=(ak,))
                for k in range(1, 4):
                    kb.op("dve", lambda e, k=k: e.scalar_tensor_tensor(out=acc[b][:], in0=xp[b][:, k:k + NTOK], scalar=cw[:, ch, k:k + 1], in1=acc[b][:],
                                                                       op0=ALU.mult, op1=ALU.add), reads=(xk, "cw", ak), writes=(ak,))
                kb.op("pool", lambda e: e.tensor_copy(halo[:, ch, :], xp[b][:, NTOK:NTOK + 3]), reads=(xk, "halo"), writes=("halo",))
                kb.op("act", lambda e: e.activation(out=xc[:, ch, :], in_=acc[b][:], func=AF.Silu), reads=(ak,), writes=(("xc", ch),))
            allxc = tuple(("xc", ch) for ch in range(32))
            for tt in range(GT):
                for blk in range(3):
                    for cc in range(8):
                        ch = blk * 8 + cc
                        kb.op("pe", lambda e, cc=cc, ch=ch: e.transpose(out=pst[:, cc, :], in_=xc[:, ch, tt * 128:(tt + 1) * 128], identity=cm.ident[:]),
                              reads=(("xc", ch), "ident"), writes=("pst",))
                    if blk < 2:
                        dst = x_tok[:, tt, blk * 1024:(blk + 1) * 1024].rearrange("p (c k) -> p c k", k=128)
                        kb.op("act", lambda e: e.activation(out=dst, in_=pst[:, :, :], func=AF.Copy), reads=("pst",), writes=("x_tok",))
                    else:
                        dst = B_tok[:, tt, :].rearrange("p (c k) -> p c k", k=128)
                        kb.op("act", lambda e: e.activation(out=dst, in_=pst[:, :, :], func=AF.Copy), reads=("pst",), writes=("B_tok",))
            for tt in range(GT):
                t = g2 * GT + tt
                tsl = slice(tt * 128, (tt + 1) * 128)
                for c in range(8):
                    kb.op("pe", lambda e, c=c: e.matmul(pdc[:, 0:32], hnT[:, c, tsl], w_dt[:, c, :], start=(c == 0), stop=(c == 7)),
                          reads=("w_dt", "hnT"), writes=("pdc",))
                kb.op("dve", lambda e: e.tensor_tensor(out=xb[:], in0=pdc[:, 0:32], in1=dtb[:], op=ALU.add), reads=("pdc", "dtb"), writes=("xb",))
                kb.op("act", lambda e: e.activation(out=xb[:], in_=xb[:], func=AF.Exp), reads=("xb",), writes=("xb",))
                kb.op("dve", lambda e: e.tensor_scalar(out=xb[:], in0=xb[:], scalar1=1.0, scalar2=None, op0=ALU.add), reads=("xb",), writes=("xb",))
                kb.op("act", lambda e: e.activation(out=dtt[:], in_=xb[:], func=AF.Ln), reads=("xb",), writes=("dtt",))
                kb.op("dve", lambda e: e.tensor_tensor(out=dA[:], in0=dtt[:], in1=aneg[:], op=ALU.mult), reads=("dtt", "aneg"), writes=("dA",))
                kb.op("pe", lambda e: e.matmul(pdc[:, 32:64], tri[:], dA[:], start=True, stop=True), reads=("tri", "dA"), writes=("pdc",))
                kb.op("pe", lambda e: e.matmul(pdc[:, 64:96], onesf[:], dA[:], start=True, stop=True), reads=("onesf", "dA"), writes=("pdc",))
                kb.op("act", lambda e: e.activation(out=cum[:], in_=pdc[:, 32:64], func=AF.Copy), reads=("pdc",), writes=("cum",))
                kb.op("act", lambda e: e.activation(out=negcum[:], in_=pdc[:, 32:64], func=AF.Copy, scale=-1.0), reads=("pdc",), writes=("negcum",))
                kb.op("act", lambda e: e.activation(out=ecum[:], in_=pdc[:, 32:64], func=AF.Exp), reads=("pdc",), writes=("ecum",))
                kb.op("act", lambda e: e.activation(out=etot[:], in_=pdc[:, 64:96], func=AF.Exp), reads=("pdc",), writes=("etot",))
                kb.op("dve", lambda e: e.tensor_tensor(out=dd[:], in0=pdc[:, 64:96], in1=cum[:], op=ALU.subtract), reads=("pdc", "cum"), writes=("dd",))
                kb.op("act", lambda e: e.activation(out=dec_end[:], in_=dd[:], func=AF.Exp), reads=("dd",), writes=("dec_end",))
                x3 = x_tok[:, tt, :].rearrange("p (h d) -> p h d", d=64)
                kb.op("dve", lambda e: e.tensor_tensor(out=xdt[:].rearrange("p (h d) -> p h d", d=64), in0=x3,
                                                       in1=dtt[:, :].unsqueeze(2).to_broadcast([128, 32, 64]), op=ALU.mult), reads=("x_tok", "dtt"), writes=("xdt",))
                kb.op("pool", lambda e: e.tensor_tensor(out=xdd[:].rearrange("p (h d) -> p h d", d=64), in0=xdt[:].rearrange("p (h d) -> p h d", d=64),
                                                        in1=dec_end[:, :].unsqueeze(2).to_broadcast([128, 32, 64]), op=ALU.mult), reads=("xdt", "dec_end"), writes=("xdd",))
                for g in range(8):
                    BT = xc[:, 16 + g, tsl]
                    CT = xc[:, 24 + g, tsl]
                    kb.op("pe", lambda e: e.matmul(pcb[:, :], BT, CT, start=True, stop=True), reads=allxc, writes=("pcb",))
                    kb.op("dve", lambda e: e.tensor_tensor(out=cbm[:], in0=pcb[:, :], in1=cmaskT[:], op=ALU.mult), reads=("pcb", "cmaskT"), writes=("cbm",))
                    for h4 in range(4):
                        h = 4 * g + h4
                        hs = slice(h * 64, (h + 1) * 64)
                        kb.op("pool", lambda e: e.tensor_scalar(out=tsc[:], in0=tri[:], scalar1=dA[:, h:h + 1], scalar2=None, op0=ALU.mult),
                              reads=("tri", "dA"), writes=("tsc",))
                        kb.op("pe", lambda e: e.matmul(pcr[:, :], onesf[:], tsc[:], start=True, stop=False), reads=("onesf", "tsc"), writes=("pcr",))
                        kb.op("pe", lambda e: e.matmul(pcr[:, :], identf[:], nmaskf[:], start=False, stop=True), reads=("identf", "nmaskf"), writes=("pcr",))
                        kb.op("act", lambda e: e.activation(out=LT[:], in_=pcr[:, :], func=AF.Exp, bias=negcum[:, h:h + 1], scale=1.0),
                              reads=("pcr", "negcum"), writes=("LT",))
                        kb.op("dve", lambda e: e.tensor_tensor(out=MT[:], in0=LT[:], in1=cbm[:], op=ALU.mult), reads=("LT", "cbm"), writes=("MT",))
                        kb.op("pe", lambda e: e.matmul(py[:, h4 * 64:(h4 + 1) * 64], MT[:], xdt[:, hs], start=True, stop=True), reads=("MT", "xdt"), writes=("py",))
                        kb.op("pe", lambda e: e.matmul(py[:, 256 + h4 * 64:256 + (h4 + 1) * 64], CT, S_bf[:, h, :], start=True, stop=True),
                              reads=allxc + ("S_bf",), writes=("py",))
                        kb.op("pe", lambda e: e.matmul(pSu[:, h4 * 64:(h4 + 1) * 64], B_tok[:, tt, g * 128:(g + 1) * 128], xdd[:, hs], start=True, stop=True),
                              reads=("B_tok", "xdd"), writes=("pSu",))
                    kb.op("act", lambda e: e.activation(out=yi[:], in_=py[:, 0:256], func=AF.Copy), reads=("py",), writes=("yi",))
                    for h4 in range(4):
                        h = 4 * g + h4
                        kb.op("dve", lambda e: e.scalar_tensor_tensor(out=y_sb[:, h * 64:(h + 1) * 64], in0=py[:, 256 + h4 * 64:256 + (h4 + 1) * 64],
                                                                      scalar=ecum[:, h:h + 1], in1=yi[:, h4 * 64:(h4 + 1) * 64], op0=ALU.mult, op1=ALU.add),
                              reads=("py", "ecum", "yi"), writes=("y_sb",))
                    Sg = S[:, 4 * g:4 * g + 4, :]
                    kb.op("dve", lambda e: e.tensor_tensor(out=Sg, in0=Sg, in1=etot[:, 4 * g:4 * g + 4].unsqueeze(2).to_broadcast([128, 4, 64]), op=ALU.mult),
                          reads=("S", "etot"), writes=("S",))
                    kb.op("dve", lambda e: e.tensor_tensor(out=Sg, in0=Sg, in1=pSu[:, :].rearrange("p (h d) -> p h d", d=64), op=ALU.add),
                          reads=("S", "pSu"), writes=("S",))
                    kb.op("act", lambda e: e.activation(out=S_bf[:, 4 * g:4 * g + 4, :], in_=Sg, func=AF.Copy), reads=("S",), writes=("S_bf",))
                kb.op("pool", lambda e: e.tensor_tensor(out=tmp[:].rearrange("p (h d) -> p h d", d=64), in0=x3,
                                                        in1=Db[:, :].unsqueeze(2).to_broadcast([128, 32, 64]), op=ALU.mult), reads=("x_tok", "Db"), writes=("tmp",))
                kb.op("dve", lambda e: e.tensor_tensor(out=y_sb[:], in0=y_sb[:], in1=tmp[:], op=ALU.add), reads=("y_sb", "tmp"), writes=("y_sb",))
                kb.dma("sp", lambda e: e.dma_start(out=y_d[t * 128:(t + 1) * 128, :], in_=y_sb[:]), reads=("y_sb",), writes=(("y_d", t),))
        kb.barrier()
    with contextlib.ExitStack() as es:
        sb = lambda n, s, d: kb.sb(n, s, d, es)
        ps = lambda n, s, d: kb.ps(n, s, d, es)
        w_z = sb("w_z", [128, 8, 2048], BF16)
        w_out = sb("sw_out", [128, 16, 1024], BF16)
        gb = sb("gb", [128, D], F32)
        gnb = sb("gnb", [128, 2048], F32)
        hb2 = [sb("hb", [128, D], F32) for _ in range(2)]
        hn = sb("hn", [128, D], F32)
        hnb = sb("hnb", [128, D], BF16)
        scr = sb("scr", [128, D], F32)
        stat = sb("stat", [128, 4], F32)
        hnT = sb("hnT", [128, 8, 128], BF16)
        zs = sb("zs", [128, 2048], F32)
        yb = sb("yb", [128, 2048], F32)
        sq = sb("sq", [128, 2048], F32)
        ss = sb("ss", [128, 8], F32)
        yn = sb("yn", [128, 2048], BF16)
        yT = sb("yT", [128, 16, 128], BF16)
        pst = ps("pst", [128, 8, 128], BF16)
        pp = [ps("pp", [128, 512], F32) for _ in range(2)]
        pm = ps("pm", [128, 1024], F32)
        for c in range(8):
            kb.dma("pool", lambda e, c=c: e.dma_start(out=w_z[:, c, :], in_=W["ssm_w_in"][0, c * 128:(c + 1) * 128, 0:2048]), writes=("w_z",))
        for c in range(16):
            kb.dma("pool", lambda e, c=c: e.dma_start(out=w_out[:, c, :], in_=W["ssm_w_out"][0, c * 128:(c + 1) * 128, :]), writes=("sw_out",))
        kb.dma("sp", lambda e: e.dma_start(out=gb[:], in_=W["norm_mix"][li].partition_broadcast(128)), writes=("gb",))
        kb.dma("sp", lambda e: e.dma_start(out=gnb[:], in_=W["ssm_gnorm"][0].partition_broadcast(128)), writes=("gnb",))
        ppi = [0]
        for t in range(NT):
            hb = hb2[t % 2]
            hk = ("hb", t % 2)
            load_norm_T(kb, cm, t, hsrc, hsrc_name, hb, hk, gb, "gb", hn, hnb, scr, stat, pst, hnT[:, :, :], "hnT")
            kb.dma("sp", lambda e: e.dma_start(out=yb[:], in_=y_d[t * 128:(t + 1) * 128, :]), reads=(("y_d", t),), writes=("yb",))
            for cg in range(4):
                b = ppi[0] % 2
                ppi[0] += 1
                for c in range(8):
                    kb.op("pe", lambda e, c=c: e.matmul(pp[b][:, :], hnT[:, c, :], w_z[:, c, cg * 512:(cg + 1) * 512], start=(c == 0), stop=(c == 7)),
                          reads=("w_z", "hnT"), writes=(("pp", b),))
                kb.op("act", lambda e: e.activation(out=zs[:, cg * 512:(cg + 1) * 512], in_=pp[b][:, :], func=AF.Silu), reads=(("pp", b),), writes=("zs",))
            kb.op("dve", lambda e: e.tensor_tensor(out=yb[:], in0=yb[:], in1=zs[:], op=ALU.mult), reads=("yb", "zs"), writes=("yb",))
            for g in range(8):
                kb.op("act", lambda e, g=g: e.activation(out=sq[:, g * 256:(g + 1) * 256], in_=yb[:, g * 256:(g + 1) * 256], func=AF.Square,
                                                         accum_out=ss[:, g:g + 1]), reads=("yb",), writes=("sq", "ss"))
            kb.op("dve", lambda e: e.tensor_scalar(out=ss[:], in0=ss[:], scalar1=1.0 / 256.0, scalar2=EPS, op0=ALU.mult, op1=ALU.add), reads=("ss",), writes=("ss",))
            kb.op("act", lambda e: e.activation(out=ss[:], in_=ss[:], func=AF.Sqrt), reads=("ss",), writes=("ss",))
            kb.op("dve", lambda e: e.reciprocal(out=ss[:], in_=ss[:]), reads=("ss",), writes=("ss",))
            kb.op("dve", lambda e: e.tensor_tensor(out=yb[:].rearrange("p (g k) -> p g k", k=256), in0=yb[:].rearrange("p (g k) -> p g k", k=256),
                                                   in1=ss[:, :].unsqueeze(2).to_broadcast([128, 8, 256]), op=ALU.mult), reads=("yb", "ss"), writes=("yb",))
            kb.op("dve", lambda e: e.tensor_tensor(out=yn[:], in0=yb[:], in1=gnb[:], op=ALU.mult), reads=("yb", "gnb"), writes=("yn",))
            for blk in range(2):
                for cc in range(8):
                    kb.op("pe", lambda e, cc=cc: e.transpose(out=pst[:, cc, :], in_=yn[:, (blk * 8 + cc) * 128:(blk * 8 + cc + 1) * 128], identity=cm.ident[:]),
                          reads=("yn", "ident"), writes=("pst",))
                kb.op("act", lambda e: e.activation(out=yT[:, blk * 8:(blk + 1) * 8, :], in_=pst[:, :, :], func=AF.Copy), reads=("pst",), writes=("yT",))
            for cg in range(2):
                for c in range(16):
                    kb.op("pe", lambda e, c=c: e.matmul(pm[:, cg * 512:(cg + 1) * 512], yT[:, c, :], w_out[:, c, cg * 512:(cg + 1) * 512], start=(c == 0), stop=(c == 15)),
                          reads=("yT", "sw_out"), writes=("pm",))
                kb.op("dve", lambda e: e.tensor_tensor(out=hb[:, cg * 512:(cg + 1) * 512], in0=pm[:, cg * 512:(cg + 1) * 512],
                                                       in1=hb[:, cg * 512:(cg + 1) * 512], op=ALU.add), reads=("pm", hk), writes=(hk,))
            kb.dma("sp", lambda e: e.dma_start(out=hdst[t * 128:(t + 1) * 128, :], in_=hb[:]), reads=(hk,), writes=((hdst_name, t),))
        kb.barrier()


def final_phase(kb, cm, hsrc, hsrc_name, y, W):
    NT = cm.NT
    with contextlib.ExitStack() as es:
        sb = lambda n, s, d: kb.sb(n, s, d, es)
        gb = sb("gbf", [128, D], F32)
        h_t = [sb("hf", [128, D], F32) for _ in range(2)]
        o_t = [sb("of", [128, D], F32) for _ in range(2)]
        scr = sb("scrf", [128, D], F32)
        stat = [sb("statf", [128, 4], F32) for _ in range(2)]
        kb.dma("sp", lambda e: e.dma_start(out=gb[:], in_=W["norm_final"].partition_broadcast(128)), writes=("gbf",))
        for t in range(NT):
            b = t % 2
            kb.dma("sp", lambda e: e.dma_start(out=h_t[b][:], in_=hsrc[t * 128:(t + 1) * 128, :]),
                   reads=((hsrc_name, t),), writes=(("hf", b),))
            rmsnorm_tile(kb, h_t[b], ("hf", b), gb, "gbf", o_t[b], ("of", b), scr, "scrf", stat[b], ("statf", b))
            kb.dma("sp", lambda e: e.dma_start(out=y[t * 128:(t + 1) * 128, :], in_=o_t[b][:]),
                   reads=(("of", b),), writes=(("y", t),))


        kb.barrier()
WEIGHT_SPECS = {
    "norm_mix": [4, 1024], "norm_ffn": [4, 1024], "norm_final": [1024], "hgrn_lb_logits": [4, 1024],
    "hgrn_w_in": [2, 1024, 4096], "hgrn_gnorm": [2, 128], "hgrn_w_out": [2, 1024, 1024],
    "fox_w_in": [1, 1024, 3088], "fox_b_f": [1, 16], "fox_w_out": [1, 1024, 1024],
    "ssm_w_in": [1, 1024, 6176], "ssm_conv_w": [1, 4, 4096], "ssm_conv_b": [1, 4096],
    "ssm_dt_bias": [1, 32], "ssm_a_log": [1, 32], "ssm_d": [1, 32], "ssm_gnorm": [1, 2048],
    "ssm_w_out": [1, 2048, 1024], "peer_w_q": [4, 1024, 2048], "peer_keysT": [4, 16, 128, 128],
    "peer_u0": [16384, 1024], "peer_v0": [16384, 1024], "peer_u1": [16384, 1024], "peer_v1": [16384, 1024],
    "peer_u2": [16384, 1024], "peer_v2": [16384, 1024], "peer_u3": [16384, 1024], "peer_v3": [16384, 1024],
    "ssm_conv_wT": [128, 32, 4], "ssm_conv_bT": [128, 32],
    "c_tri": [128, 128], "c_negmask": [128, 128], "c_ones3": [3, 8192],
    "hgrn_lbT": [128, 4, 8], "c_ones": [128, 128], "c_rmask": [128, 512], "c_maskT16": [128, 128], "c_cmask": [128, 8],
}


def build(T=SEQ, plan=("peer0", "final"), used=None):
    kb = KB()
    cm = Common(kb, T)
    x = kb.dram("x", [T, D], F32, kind="ExternalInput")
    y = kb.dram("y", [T, D], F32, kind="ExternalOutput")
    hA = kb.dram("hA", [T, D], F32)
    W = {}
    names = set()
    for p in plan:
        if p.startswith("peer"):
            names |= {"peer_w_q", "peer_keysT", "peer_u" + p[4:], "peer_v" + p[4:], "norm_ffn"}
        if p == "final":
            names |= {"norm_final"}
        if p.startswith("ssd"):
            names |= {"ssm_w_in", "ssm_conv_wT", "ssm_conv_bT", "ssm_dt_bias", "ssm_a_log", "ssm_d", "ssm_gnorm", "ssm_w_out", "norm_mix",
                      "c_tri", "c_negmask", "c_ones"}
        if p.startswith("fox"):
            names |= {"fox_w_in", "fox_b_f", "fox_w_out", "norm_mix", "c_tri", "c_negmask", "c_ones3", "c_ones"}
        if p.startswith("hgrn"):
            names |= {"hgrn_w_in", "hgrn_w_out", "hgrn_gnorm", "hgrn_lbT", "norm_mix", "c_ones", "c_rmask", "c_maskT16", "c_cmask"}
    for n in sorted(names):
        W[n] = kb.dram(n, WEIGHT_SPECS[n], F32, kind="ExternalInput")
    cur, cur_name = x, "x"
    for p in plan:
        kb.barrier()
        if p.startswith("peer"):
            li = int(p[4:])
            peer_phase(kb, cm, li, cur, cur_name, hA, "hA", W)
            cur, cur_name = hA, "hA"
        elif p.startswith("hgrn"):
            li = int(p[4:])
            hgrn_phase(kb, cm, li, li // 3, cur, cur_name, hA, "hA", W)
            cur, cur_name = hA, "hA"
        elif p.startswith("ssd"):
            li = int(p[3:])
            y_d = kb.dram("s_y", [T, 2048], F32)
            ssd_phase(kb, cm, li, cur, cur_name, hA, "hA", W, y_d)
            cur, cur_name = hA, "hA"
        elif p.startswith("fox"):
            li = int(p[3:])
            dk = "ExternalOutput" if DEBUG_SCRATCH else "Internal"
            scr_d = {"qT": kb.dram("f_qT", [1024, T], BF16, dk), "kT": kb.dram("f_kT", [1024, T], BF16, dk), "v": kb.dram("f_v", [T, 1024], BF16, dk),
                     "ca": kb.dram("f_ca", [16, 3, T], BF16, dk), "o": kb.dram("f_o", [T, 1024], BF16, dk), "negc": kb.dram("f_negc", [T, 16], F32, dk)}
            fox_phase(kb, cm, li, cur, cur_name, hA, "hA", W, scr_d)
            cur, cur_name = hA, "hA"
        elif p == "final":
            final_phase(kb, cm, cur, cur_name, y, W)
    kb.finish([("y", t) for t in range(cm.NT)])
    kb.es.close()
    return kb, sorted(names)


def host_consts():
    s = np.arange(128)
    c = {"c_ident": np.eye(128, dtype=np.float32)}
    c["c_ones"] = np.ones((128, 128), np.float32)
    c["c_rmask"] = np.tile((np.arange(512) % 16 != 0).astype(np.float32)[None, :], (128, 1))
    c["c_maskT16"] = ((s[:, None] // 16 == s[None, :] // 16) & (s[:, None] <= s[None, :])).astype(np.float32)
    c["c_cmask"] = (s[:, None] // 16 == np.arange(8)[None, :]).astype(np.float32)
    c["c_tri"] = (s[:, None] <= s[None, :]).astype(np.float32)
    c["c_negmask"] = np.where(s[:, None] <= s[None, :], 0.0, -30000.0).astype(np.float32)
    c["c_ones3"] = np.ones((3, 8192), np.float32)
    return c


def layout_weights(inp):
    out = dict(inp)
    for l in range(4):
        if "peer_u" in inp:
            out["peer_u%d" % l] = np.asarray(inp["peer_u"])[l]
            out["peer_v%d" % l] = np.asarray(inp["peer_v"])[l]
    if "peer_keys" in inp:
        pk = np.asarray(inp["peer_keys"])
        out["peer_keysT"] = np.ascontiguousarray(pk.reshape(4, 16, 128, 128).transpose(0, 1, 3, 2))
    if "ssm_conv_w" in inp:
        out["ssm_conv_wT"] = np.ascontiguousarray(np.asarray(inp["ssm_conv_w"])[0].reshape(4, 32, 128).transpose(2, 1, 0))
        out["ssm_conv_bT"] = np.ascontiguousarray(np.asarray(inp["ssm_conv_b"])[0].reshape(32, 128).T)
    if "hgrn_lb_logits" in inp:
        out["hgrn_lbT"] = np.ascontiguousarray(np.asarray(inp["hgrn_lb_logits"]).reshape(4, 8, 128).transpose(2, 0, 1))
    return out


FULL_PLAN = ("hgrn0", "peer0", "fox1", "peer1", "ssd2", "peer2", "hgrn3", "peer3", "final")


def kernel(**inputs):
    inp = layout_weights({k: np.asarray(v) for k, v in inputs.items()})
    inp.update(host_consts())
    kb, names = build(SEQ, plan=FULL_PLAN)
    shared = {n: np.ascontiguousarray(inp[n], dtype=np.float32) for n in names}
    shared["c_ident"] = inp["c_ident"]
    in_maps = []
    for c in range(NCORES):
        m = dict(shared)
        m["x"] = np.ascontiguousarray(inp["x"][c], dtype=np.float32)
        in_maps.append(m)
    res = run_bass_kernel_spmd(kb.nc, in_maps, core_ids=list(range(NCORES)))
    return np.stack([np.asarray(r["y"]) for r in res.results], axis=0).astype(np.float32)
```

```python
import contextlib
import numpy as np
import concourse.bass as bass
import concourse.mybir as mybir
from concourse.bass_utils import run_bass_kernel_spmd

F32 = mybir.dt.float32
BF16 = mybir.dt.bfloat16
U32 = mybir.dt.uint32
AF = mybir.ActivationFunctionType
ALU = mybir.AluOpType
AX = mybir.AxisListType

D = 1024
NCORES = 8
SEQ = 8192
EPS = 1e-6
NEG = -1.0e30
DEBUG_SCRATCH = False


class KB:
    R_DMA = 8

    def __init__(self):
        self.nc = bass.Bass("TRN2", target_bir_lowering=False)
        self.es = contextlib.ExitStack()
        nc = self.nc
        self.eng = {"pe": nc.tensor, "dve": nc.vector, "act": nc.scalar, "pool": nc.gpsimd, "sp": nc.sync}
        self.sems = {}
        for e in ("pe", "dve", "act", "pool"):
            self.sems[e] = self.es.enter_context(nc.semaphore("s_" + e))
        for q in ("sp", "pool", "act"):
            for s in range(self.R_DMA):
                self.sems[("dma", q, s)] = self.es.enter_context(nc.semaphore(f"d_{q}_{s}"))
        self.cnt = {k: 0 for k in self.sems}
        self.dma_i = {"sp": 0, "pool": 0, "act": 0}
        self.waited = {e: {} for e in self.eng}
        self.last_w = {}
        self.readers = {}
        self.n_ins = 0
        self.uid = 0
        self.psum_names = set()

    def sb(self, name, shape, dt, es=None):
        self.uid += 1
        return (es or self.es).enter_context(self.nc.sbuf_tensor(f"{name}_{self.uid}", list(shape), dt))

    def ps(self, name, shape, dt, es=None):
        self.uid += 1
        self.psum_names.add(name)
        shape = list(shape)
        esz = 2 if dt == BF16 else 4
        per_part = esz
        for s in shape[1:]:
            per_part *= s
        if per_part >= 2048 or len(shape) != 2:
            assert per_part % 2048 == 0, (name, shape)
            return (es or self.es).enter_context(self.nc.psum_tensor(f"{name}_{self.uid}", shape, dt))
        t = (es or self.es).enter_context(self.nc.psum_tensor(f"{name}_{self.uid}", [shape[0], 2048 // esz], dt))
        return t[:, 0:shape[1]]

    def dram(self, name, shape, dt, kind="Internal"):
        return self.nc.dram_tensor(name, list(shape), dt, kind=kind).ap()

    def _deps(self, reads, writes):
        deps = {}

        def add(h):
            if h is None:
                return
            k, c = h
            if deps.get(k, 0) < c:
                deps[k] = c

        for k in reads:
            add(self.last_w.get(k))
        for k in writes:
            add(self.last_w.get(k))
            for sk, c in self.readers.get(k, {}).items():
                add((sk, c))
        return deps

    def _waits(self, eng, deps):
        E = self.eng[eng]
        w = self.waited[eng]
        for sk, c in deps.items():
            if eng == "pe" and sk == "pe":
                continue
            if w.get(sk, 0) < c:
                E.wait_ge(self.sems[sk], c)
                w[sk] = c
                self.n_ins += 1

    def _record(self, h, reads, writes):
        sk, c = h
        for k in reads:
            self.readers.setdefault(k, {})[sk] = c
        for k in writes:
            self.last_w[k] = h
            self.readers[k] = {}

    def _excl(self, reads, writes):
        r2, w2 = [], list(writes)
        for k in reads:
            root = k if isinstance(k, str) else k[0]
            (w2 if root in self.psum_names else r2).append(k)
        return r2, w2

    def op(self, eng, fn, reads=(), writes=()):
        reads, writes = self._excl(reads, writes)
        deps = self._deps(reads, writes)
        self._waits(eng, deps)
        ins = fn(self.eng[eng])
        self.cnt[eng] += 1
        ins.then_inc(self.sems[eng], 1)
        self.n_ins += 1
        self._record((eng, self.cnt[eng]), reads, writes)

    def dma(self, q, fn, reads=(), writes=()):
        slot = self.dma_i[q] % self.R_DMA
        self.dma_i[q] += 1
        sk = ("dma", q, slot)
        reads, writes = self._excl(reads, writes)
        deps = self._deps(reads, writes)
        if self.cnt[sk] > 0:
            deps[sk] = max(deps.get(sk, 0), self.cnt[sk])
        self._waits(q, deps)
        ins = fn(self.eng[q])
        self.cnt[sk] += 16
        ins.then_inc(self.sems[sk], 16)
        self.n_ins += 1
        self._record((sk, self.cnt[sk]), reads, writes)

    def barrier(self):
        deps = {sk: c for sk, c in self.cnt.items() if c > 0}
        for eng in self.eng:
            self._waits(eng, dict(deps))

    def finish(self, keys):
        deps = self._deps(keys, ())
        self._waits("sp", deps)


class Common:
    def __init__(self, kb, T):
        self.kb = kb
        self.T = T
        self.NT = T // 128
        nc = kb.nc
        self.ident_d = kb.dram("c_ident", [128, 128], F32, kind="ExternalInput")
        self.ident = kb.sb("ident", [128, 128], BF16)
        kb.dma("pool", lambda e: e.dma_start(out=self.ident[:], in_=self.ident_d[:, :]), reads=(), writes=("ident",))


def rmsnorm_tile(kb, h_t, hk, gb, gk, out_t, ok, scr, sk, stat, stk, es_keys=()):
    kb.op("act", lambda e: e.activation(out=scr[:], in_=h_t[:], func=AF.Square, accum_out=stat[:, 0:1]),
          reads=(hk,), writes=(sk, stk))
    kb.op("dve", lambda e: e.tensor_scalar(out=stat[:, 1:2], in0=stat[:, 0:1], scalar1=1.0 / D, scalar2=EPS,
                                           op0=ALU.mult, op1=ALU.add), reads=(stk,), writes=(stk,))
    kb.op("act", lambda e: e.activation(out=stat[:, 2:3], in_=stat[:, 1:2], func=AF.Sqrt), reads=(stk,), writes=(stk,))
    kb.op("dve", lambda e: e.reciprocal(out=stat[:, 3:4], in_=stat[:, 2:3]), reads=(stk,), writes=(stk,))
    kb.op("dve", lambda e: e.scalar_tensor_tensor(out=out_t[:], in0=h_t[:], scalar=stat[:, 3:4], in1=gb[:],
                                                  op0=ALU.mult, op1=ALU.mult), reads=(hk, stk, gk), writes=(ok,))


def transpose_tile(kb, cm, src_bf, srck, dst_ap, dstk, pst, pstk, nch=8):
    for c in range(nch):
        kb.op("pe", lambda e, c=c: e.transpose(out=pst[:, c, :], in_=src_bf[:, c * 128:(c + 1) * 128], identity=cm.ident[:]),
              reads=(srck, "ident"), writes=(pstk,))
    kb.op("act", lambda e: e.activation(out=dst_ap, in_=pst[:, 0:nch, :], func=AF.Copy), reads=(pstk,), writes=(dstk,))


def peer_phase(kb, cm, li, hsrc, hsrc_name, hdst, hdst_name, W):
    NT = cm.NT
    NG = 6
    with contextlib.ExitStack() as es:
        sb = lambda n, s, d: kb.sb(n, s, d, es)
        ps = lambda n, s, d: kb.ps(n, s, d, es)
        wq = sb("wq", [128, 8, 2048], BF16)
        keysT = sb("keysT", [128, 16, 128], BF16)
        gb = sb("gb", [128, D], F32)
        identf = sb("identf", [128, 128], F32)
        h_t = [sb("h_t", [128, D], F32) for _ in range(2)]
        hn2 = [sb("hn", [128, D], F32) for _ in range(2)]
        hnb = sb("hnb", [128, D], BF16)
        hnT = sb("hnT", [128, 8, 128], BF16)
        stat = sb("stat", [128, 4], F32)
        qT = sb("qT", [128, 16, 128], BF16)
        s_sb = sb("s_sb", [128, 16, 128], F32)
        wk = sb("wk", [128, 16, 128], F32)
        top = sb("top", [128, 16, 16], F32)
        ix = sb("ix", [128, 16, 16], U32)
        ixf = sb("ixf", [128, 16, 16], F32)
        cs = sb("cs", [128, 8, 256], F32)
        ci = sb("ci", [128, 8, 256], F32)
        wk2 = sb("wk2", [128, 8, 256], F32)
        top2 = sb("top2", [128, 8, 16], F32)
        t2c = sb("t2c", [128, 8, 16], F32)
        nmax = sb("nmax", [128, 8], F32)
        ez = sb("ez", [128, 8, 16], F32)
        zz = sb("zz", [128, 8], F32)
        rz = sb("rz", [128, 8], F32)
        eidx = sb("eidx", [128, 128], F32)
        gates2 = [sb("gates", [128, 128], F32) for _ in range(2)]
        eidu2 = [sb("eidu", [128, 128], U32) for _ in range(2)]
        adot2 = [sb("adot", [128, 128], F32) for _ in range(2)]
        aact2 = [sb("aact", [128, 128], F32) for _ in range(2)]
        junk = sb("junk", [128, D], F32)
        junk2 = sb("junk2", [128, 256], F32)
        ug = [sb("ug", [128, D], F32) for _ in range(NG)]
        vg = [sb("vg", [128, D], F32) for _ in range(NG)]
        Ds = [sb("Ds", [128, 128], F32) for _ in range(4)]
        pst = ps("pst", [128, 8, 128], BF16)
        pq = ps("pq", [128, 4, 128], F32)
        psc = [ps("psc", [128, 4, 128], F32) for _ in range(4)]
        pacc = ps("pacc", [128, 1024], F32)

        for c in range(8):
            kb.dma("pool", lambda e, c=c: e.dma_start(out=wq[:, c, :], in_=W["peer_w_q"][li, c * 128:(c + 1) * 128, :]), writes=("wq",))
        kb.dma("pool", lambda e: e.dma_start(out=keysT[:], in_=W["peer_keysT"][li].rearrange("j d k -> d j k")), writes=("keysT",))
        kb.dma("sp", lambda e: e.dma_start(out=gb[:], in_=W["norm_ffn"][li].partition_broadcast(128)), writes=("gb",))
        kb.dma("sp", lambda e: e.dma_start(out=identf[:], in_=cm.ident_d[:, :]), writes=("identf",))
        u_tab = W["peer_u%d" % li]
        v_tab = W["peer_v%d" % li]

        def stage_A(t):
            p2 = t % 2
            hb, hk = h_t[p2], ("h_t", p2)
            hn, hnk = hn2[p2], ("hn", p2)
            gates, gk = gates2[p2], ("gates", p2)
            eidu, ek = eidu2[p2], ("eidu", p2)
            kb.dma("sp", lambda e: e.dma_start(out=hb[:], in_=hsrc[t * 128:(t + 1) * 128, :]), reads=((hsrc_name, t),), writes=(hk,))
            rmsnorm_tile(kb, hb, hk, gb, "gb", hn, hnk, junk, "junk", stat, "stat")
            kb.op("act", lambda e: e.activation(out=hnb[:], in_=hn[:], func=AF.Copy), reads=(hnk,), writes=("hnb",))
            transpose_tile(kb, cm, hnb, "hnb", hnT[:, :, :], "hnT", pst, "pst")
            for jg in range(4):
                for jj in range(4):
                    j = jg * 4 + jj
                    for c in range(8):
                        kb.op("pe", lambda e, c=c: e.matmul(pq[:, jj, :], wq[:, c, j * 128:(j + 1) * 128], hnT[:, c, :], start=(c == 0), stop=(c == 7)),
                              reads=("wq", "hnT"), writes=("pq",))
                kb.op("act", lambda e: e.activation(out=qT[:, jg * 4:(jg + 1) * 4, :], in_=pq[:], func=AF.Copy), reads=("pq",), writes=(("qT", jg),))
            for jg in range(4):
                for jj in range(4):
                    j = jg * 4 + jj
                    kb.op("pe", lambda e: e.matmul(psc[jg][:, jj, :], qT[:, j, :], keysT[:, j, :], start=True, stop=True),
                          reads=(("qT", jg), "keysT"), writes=(("psc", jg),))
                kb.op("act", lambda e: e.activation(out=s_sb[:, jg * 4:(jg + 1) * 4, :], in_=psc[jg][:], func=AF.Copy),
                      reads=(("psc", jg),), writes=(("s_sb", jg),))
            for j in range(16):
                sk = ("s_sb", j // 4)
                tk = ("top", j)
                kb.op("dve", lambda e: e.max(out=top[:, j, 0:8], in_=s_sb[:, j, :]), reads=(sk,), writes=(tk,))
                kb.op("dve", lambda e: e.max_index(out=ix[:, j, 0:8], in_max=top[:, j, 0:8], in_values=s_sb[:, j, :]), reads=(sk, tk), writes=(("ix", j),))
                kb.op("dve", lambda e: e.match_replace(out=wk[:, j, :], in_to_replace=top[:, j, 0:8], in_values=s_sb[:, j, :], imm_value=NEG),
                      reads=(sk, tk), writes=(("wk", j),))
                kb.op("dve", lambda e: e.max(out=top[:, j, 8:16], in_=wk[:, j, :]), reads=(("wk", j),), writes=(tk,))
                kb.op("dve", lambda e: e.max_index(out=ix[:, j, 8:16], in_max=top[:, j, 8:16], in_values=wk[:, j, :]), reads=(("wk", j), tk), writes=(("ix", j),))
            allix = tuple(("ix", j) for j in range(16))
            alltop = tuple(("top", j) for j in range(16))
            kb.op("dve", lambda e: e.tensor_copy(ixf[:], ix[:]), reads=allix, writes=("ixf",))
            for hh in range(8):
                j0, j1 = 2 * hh, 2 * hh + 1
                csv = cs[:, hh, :].rearrange("p (a b) -> p a b", b=16)
                civ = ci[:, hh, :].rearrange("p (a b) -> p a b", b=16)
                kb.op("dve", lambda e: e.tensor_tensor(out=csv, in0=top[:, j0, :].unsqueeze(2).to_broadcast([128, 16, 16]),
                                                       in1=top[:, j1, :].unsqueeze(1).to_broadcast([128, 16, 16]), op=ALU.add),
                      reads=alltop, writes=(("cs", hh),))
                kb.op("dve", lambda e: e.tensor_scalar(out=civ, in0=ixf[:, j0, :].unsqueeze(2).to_broadcast([128, 16, 16]), scalar1=128.0, scalar2=None, op0=ALU.mult),
                      reads=("ixf",), writes=(("ci", hh),))
                kb.op("dve", lambda e: e.tensor_tensor(out=civ, in0=civ, in1=ixf[:, j1, :].unsqueeze(1).to_broadcast([128, 16, 16]), op=ALU.add),
                      reads=("ixf", ("ci", hh)), writes=(("ci", hh),))
                t2k = ("top2", hh)
                kb.op("dve", lambda e: e.max(out=top2[:, hh, 0:8], in_=cs[:, hh, :]), reads=(("cs", hh),), writes=(t2k,))
                kb.op("dve", lambda e: e.match_replace(out=wk2[:, hh, :], in_to_replace=top2[:, hh, 0:8], in_values=cs[:, hh, :], imm_value=NEG),
                      reads=(("cs", hh), t2k), writes=(("wk2", hh),))
                kb.op("dve", lambda e: e.max(out=top2[:, hh, 8:16], in_=wk2[:, hh, :]), reads=(("wk2", hh),), writes=(t2k,))
                for k in range(16):
                    kb.op("dve", lambda e: e.scalar_tensor_tensor(out=junk2[:], in0=cs[:, hh, :], scalar=top2[:, hh, k:k + 1], in1=ci[:, hh, :],
                                                                  op0=ALU.is_equal, op1=ALU.mult, accum_out=eidx[:, hh * 16 + k:hh * 16 + k + 1]),
                          reads=(("cs", hh), ("ci", hh), t2k), writes=("junk2", "eidx"))
            allt2 = tuple(("top2", hh) for hh in range(8))
            kb.op("dve", lambda e: e.tensor_scalar(out=nmax[:], in0=top2[:, :, 0], scalar1=-1.0, scalar2=None, op0=ALU.mult), reads=allt2, writes=("nmax",))
            kb.op("dve", lambda e: e.tensor_copy(t2c[:], top2[:]), reads=allt2, writes=("t2c",))
            kb.op("dve", lambda e: e.tensor_scalar(out=eidx[:], in0=eidx[:], scalar1=0.0, scalar2=16383.0, op0=ALU.max, op1=ALU.min),
                  reads=("eidx",), writes=("eidx",))
            kb.op("dve", lambda e: e.tensor_copy(eidu[:], eidx[:]), reads=("eidx",), writes=(ek,))

        def stage_A2(t):
            p2 = t % 2
            gates, gk = gates2[p2], ("gates", p2)
            for hh in range(8):
                kb.op("act", lambda e: e.activation(out=ez[:, hh, :], in_=t2c[:, hh, :], func=AF.Exp, bias=nmax[:, hh:hh + 1], scale=1.0,
                                                    accum_out=zz[:, hh:hh + 1]), reads=("t2c", "nmax"), writes=("ez", "zz"))
            kb.op("dve", lambda e: e.reciprocal(out=rz[:], in_=zz[:]), reads=("zz",), writes=("rz",))
            kb.op("dve", lambda e: e.tensor_tensor(out=gates[:].rearrange("p (h k) -> p h k", k=16), in0=ez[:],
                                                   in1=rz[:].unsqueeze(2).to_broadcast([128, 8, 16]), op=ALU.mult), reads=("ez", "rz"), writes=(gk,))

        gi = [0, 0]

        def stage_B(t):
            p2 = t % 2
            hn, hnk = hn2[p2], ("hn", p2)
            eidu, ek = eidu2[p2], ("eidu", p2)
            adot, ak = adot2[p2], ("adot", p2)
            aact, aak = aact2[p2], ("aact", p2)
            for s in range(128):
                b = gi[0] % NG
                gi[0] += 1
                kb.dma("pool", lambda e: e.indirect_dma_start(out=ug[b][:], out_offset=None, in_=u_tab,
                                                              in_offset=bass.IndirectOffsetOnAxis(ap=eidu[:, s:s + 1], axis=0)),
                       reads=(ek,), writes=(("ug", b),))
                kb.op("dve", lambda e: e.scalar_tensor_tensor(out=junk[:], in0=ug[b][:], scalar=1.0, in1=hn[:], op0=ALU.mult, op1=ALU.mult,
                                                              accum_out=adot[:, s:s + 1]), reads=(("ug", b), hnk), writes=("junk", ak))
            kb.op("act", lambda e: e.activation(out=aact[:], in_=adot[:], func=AF.Gelu), reads=(ak,), writes=(aak,))
            kb.op("dve", lambda e: e.tensor_tensor(out=aact[:], in0=aact[:], in1=gates2[p2][:], op=ALU.mult), reads=(aak, ("gates", p2)), writes=(aak,))

        def stage_C(t):
            p2 = t % 2
            hb, hk = h_t[p2], ("h_t", p2)
            eidu, ek = eidu2[p2], ("eidu", p2)
            aact, aak = aact2[p2], ("aact", p2)
            for s in range(128):
                b = gi[1] % NG
                d4 = gi[1] % 4
                gi[1] += 1
                kb.dma("pool", lambda e: e.indirect_dma_start(out=vg[b][:], out_offset=None, in_=v_tab,
                                                              in_offset=bass.IndirectOffsetOnAxis(ap=eidu[:, s:s + 1], axis=0)),
                       reads=(ek,), writes=(("vg", b),))
                kb.op("act", lambda e: e.activation(out=Ds[d4][:], in_=identf[:], func=AF.Copy, scale=aact[:, s:s + 1]),
                      reads=("identf", aak), writes=(("Ds", d4),))
                for half in range(2):
                    kb.op("pe", lambda e: e.matmul(pacc[:, half * 512:(half + 1) * 512], Ds[d4][:], vg[b][:, half * 512:(half + 1) * 512],
                                                   start=(s == 0), stop=(s == 127)), reads=(("Ds", d4), ("vg", b)), writes=("pacc",))
            for half in range(2):
                kb.op("dve", lambda e: e.tensor_tensor(out=hb[:, half * 512:(half + 1) * 512], in0=pacc[:, half * 512:(half + 1) * 512],
                                                       in1=hb[:, half * 512:(half + 1) * 512], op=ALU.add), reads=("pacc", hk), writes=(hk,))
            kb.dma("sp", lambda e: e.dma_start(out=hdst[t * 128:(t + 1) * 128, :], in_=hb[:]), reads=(hk,), writes=((hdst_name, t),))

        stage_A(0)
        stage_A2(0)
        for t in range(NT):
            stage_B(t)
            if t + 1 < NT:
                stage_A(t + 1)
            stage_C(t)
            if t + 1 < NT:
                stage_A2(t + 1)
        kb.barrier()


def load_norm_T(kb, cm, t, hsrc, hsrc_name, hb, hk, gb, gk, hn, hnb, scr, stat, pst, dst_ap, dstk):
    kb.dma("sp", lambda e: e.dma_start(out=hb[:], in_=hsrc[t * 128:(t + 1) * 128, :]), reads=((hsrc_name, t),), writes=(hk,))
    rmsnorm_tile(kb, hb, hk, gb, gk, hn, "hn", scr, "scr", stat, "stat")
    kb.op("act", lambda e: e.activation(out=hnb[:], in_=hn[:], func=AF.Copy), reads=("hn",), writes=("hnb",))
    transpose_tile(kb, cm, hnb, "hnb", dst_ap, dstk, pst, "pst")


def hgrn_phase(kb, cm, li, j, hsrc, hsrc_name, hdst, hdst_name, W):
    NT = cm.NT
    GT = min(4, NT)
    NTOK = GT * 128
    NCH = NTOK // 16
    SCALE = 128.0 ** -0.5
    with contextlib.ExitStack() as es:
        sb = lambda n, s, d: kb.sb(n, s, d, es)
        ps = lambda n, s, d: kb.ps(n, s, d, es)
        w_in = sb("w_in", [128, 8, 4096], BF16)
        w_out = sb("w_out", [128, 8, 1024], BF16)
        gb = sb("gb", [128, D], F32)
        gn = sb("gn", [128, 1], F32)
        lbz = sb("lbz", [128, 4, 8], F32)
        lbe = sb("lbe", [128, 4, 8], F32)
        den = sb("den", [128, 8], F32)
        num = sb("num", [128, 8], F32)
        lb = sb("lb", [128, 8], F32)
        oml = sb("oml", [128, 8], F32)
        epst = sb("epst", [128, 1], F32)
        rmask = sb("rmask", [128, 512], F32)
        maskT = sb("maskT", [128, 128], F32)
        cmask = sb("cmask", [128, 8], F32)
        ones = sb("ones", [128, 128], BF16)
        hb2 = [sb("hb", [128, D], F32) for _ in range(2)]
        hn = sb("hn", [128, D], F32)
        hnb = sb("hnb", [128, D], BF16)
        scr = sb("scr", [128, D], F32)
        stat = sb("stat", [128, 4], F32)
        hnT = sb("hnT", [128, 8, NTOK], BF16)
        v_tok = sb("v_tok", [128, GT, 1024], BF16)
        fs = sb("fs", [128, NTOK], F32)
        fT = sb("fT", [128, NTOK], F32)
        lf = sb("lf", [128, NTOK], F32)
        kk = sb("kk", [128, NTOK], F32)
        bT = sb("bT", [128, NTOK], F32)
        dT = sb("dT", [128, NTOK], F32)
        eb = sb("eb", [128, NTOK], F32)
        enb = sb("enb", [128, NTOK], F32)
        ed = sb("ed", [128, NTOK], F32)
        qd = sb("qd", [128, NTOK], BF16)
        kinv = sb("kinv", [128, NTOK], BF16)
        kdT = sb("kdT", [128, NTOK], BF16)
        kd_tok = sb("kd_tok", [128, GT, 128], BF16)
        sg = sb("sg", [128, NTOK], BF16)
        yT = sb("yT", [128, 8, NTOK], BF16)
        carry = [sb("carry", [128, 128], F32) for _ in range(8)]
        Sd = sb("Sd", [128, 128, 9], F32)
        So = sb("So", [128, 128, 9], F32)
        a9 = sb("a9", [128, 128, 9], F32)
        Sbf = sb("Sbf", [128, 8, 128], BF16)
        Vblk = sb("Vblk", [128, 8, 128], BF16)
        AT = sb("AT", [128, 128], BF16)
        osq = sb("osq", [128, 128], BF16)
        sdv = sb("sdv", [128, 128], F32)
        rsv = sb("rsv", [128, 128], F32)
        t1 = sb("t1", [128, 128], F32)
        pst = ps("pst", [128, 8, 128], BF16)
        pp = [ps("pp", [128, 512], F32) for _ in range(2)]
        pA = ps("pA", [128, 128], F32)
        pS = ps("pS", [128, 1024], F32)
        po = ps("po", [128, 128], F32)
        pss = ps("pss", [128, 128], F32)

        for c in range(8):
            kb.dma("pool", lambda e, c=c: e.dma_start(out=w_in[:, c, :], in_=W["hgrn_w_in"][j, c * 128:(c + 1) * 128, :]), writes=("w_in",))
            kb.dma("pool", lambda e, c=c: e.dma_start(out=w_out[:, c, :], in_=W["hgrn_w_out"][j, c * 128:(c + 1) * 128, :]), writes=("w_out",))
        kb.dma("pool", lambda e: e.dma_start(out=ones[:], in_=W["c_ones"][:, :]), writes=("ones",))
        kb.dma("sp", lambda e: e.dma_start(out=gb[:], in_=W["norm_mix"][li].partition_broadcast(128)), writes=("gb",))
        kb.dma("sp", lambda e: e.dma_start(out=gn[:], in_=W["hgrn_gnorm"][j].rearrange("(p o) -> p o", o=1)), writes=("gn",))
        kb.dma("sp", lambda e: e.dma_start(out=lbz[:], in_=W["hgrn_lbT"][:, :, :]), writes=("lbz",))
        kb.dma("sp", lambda e: e.dma_start(out=rmask[:], in_=W["c_rmask"][:, :]), writes=("rmask",))
        kb.dma("sp", lambda e: e.dma_start(out=maskT[:], in_=W["c_maskT16"][:, :]), writes=("maskT",))
        kb.dma("sp", lambda e: e.dma_start(out=cmask[:], in_=W["c_cmask"][:, :]), writes=("cmask",))
        kb.op("dve", lambda e: e.memset(epst[:], EPS), writes=("epst",))
        kb.op("dve", lambda e: e.memset(a9[:], 0.0), writes=("a9",))
        for hh in range(8):
            kb.op("dve", lambda e, hh=hh: e.memset(carry[hh][:], 0.0), writes=(("carry", hh),))
        kb.op("act", lambda e: e.activation(out=lbe[:], in_=lbz[:], func=AF.Exp), reads=("lbz",), writes=("lbe",))
        kb.op("dve", lambda e: e.tensor_tensor(out=den[:], in0=lbe[:, 0, :], in1=lbe[:, 1, :], op=ALU.add), reads=("lbe",), writes=("den",))
        kb.op("dve", lambda e: e.tensor_tensor(out=den[:], in0=den[:], in1=lbe[:, 2, :], op=ALU.add), reads=("lbe", "den"), writes=("den",))
        kb.op("dve", lambda e: e.tensor_tensor(out=den[:], in0=den[:], in1=lbe[:, 3, :], op=ALU.add), reads=("lbe", "den"), writes=("den",))
        kb.op("dve", lambda e: e.memset(num[:], 0.0), writes=("num",))
        for l in range(1, li + 1):
            kb.op("dve", lambda e, l=l: e.tensor_tensor(out=num[:], in0=num[:], in1=lbe[:, l, :], op=ALU.add), reads=("lbe", "num"), writes=("num",))
        kb.op("dve", lambda e: e.reciprocal(out=den[:], in_=den[:]), reads=("den",), writes=("den",))
        kb.op("dve", lambda e: e.tensor_tensor(out=lb[:], in0=num[:], in1=den[:], op=ALU.mult), reads=("num", "den"), writes=("lb",))
        kb.op("dve", lambda e: e.tensor_scalar(out=oml[:], in0=lb[:], scalar1=-1.0, scalar2=1.0, op0=ALU.mult, op1=ALU.add), reads=("lb",), writes=("oml",))

        ppi = [0]

        def proj_fm(col0):
            b = ppi[0] % 2
            ppi[0] += 1
            for c in range(8):
                kb.op("pe", lambda e, c=c: e.matmul(pp[b][:, 0:NTOK], w_in[:, c, col0:col0 + 128], hnT[:, c, :], start=(c == 0), stop=(c == 7)),
                      reads=("w_in", "hnT"), writes=(("pp", b),))
            return pp[b], ("pp", b)

        for g in range(NT // GT):
            for tt in range(GT):
                t = g * GT + tt
                load_norm_T(kb, cm, t, hsrc, hsrc_name, hb2[t % 2], ("hb", t % 2), gb, "gb", hn, hnb, scr, stat, pst,
                            hnT[:, :, tt * 128:(tt + 1) * 128], "hnT")
            for tt in range(GT):
                for cg in range(2):
                    b = ppi[0] % 2
                    ppi[0] += 1
                    for c in range(8):
                        kb.op("pe", lambda e, c=c: e.matmul(pp[b][:, :], hnT[:, c, tt * 128:(tt + 1) * 128],
                                                            w_in[:, c, 2048 + cg * 512:2048 + (cg + 1) * 512], start=(c == 0), stop=(c == 7)),
                              reads=("w_in", "hnT"), writes=(("pp", b),))
                    kb.op("act", lambda e: e.activation(out=v_tok[:, tt, cg * 512:(cg + 1) * 512], in_=pp[b][:, :], func=AF.Copy),
                          reads=(("pp", b),), writes=("v_tok",))
            for hh in range(8):
                p, pk = proj_fm(1024 + hh * 128)
                kb.op("act", lambda e: e.activation(out=fs[:], in_=p[:, 0:NTOK], func=AF.Sigmoid), reads=(pk,), writes=("fs",))
                kb.op("dve", lambda e: e.tensor_scalar(out=fT[:], in0=fs[:], scalar1=oml[:, hh:hh + 1], scalar2=lb[:, hh:hh + 1],
                                                       op0=ALU.mult, op1=ALU.add), reads=("fs", "oml", "lb"), writes=("fT",))
                kb.op("act", lambda e: e.activation(out=lf[:], in_=fT[:], func=AF.Ln), reads=("fT",), writes=("lf",))
                kb.op("pool", lambda e: e.tensor_scalar(out=kk[:], in0=fT[:], scalar1=-1.0, scalar2=1.0, op0=ALU.mult, op1=ALU.add),
                      reads=("fT",), writes=("kk",))
                kb.op("dve", lambda e: e.tensor_tensor_scan(out=bT[:], data0=rmask[:, 0:NTOK], data1=lf[:], initial=0.0,
                                                            op0=ALU.mult, op1=ALU.add), reads=("rmask", "lf"), writes=("bT",))
                b3 = bT[:].rearrange("p (c k) -> p c k", k=16)
                kb.op("dve", lambda e: e.tensor_tensor(out=dT[:].rearrange("p (c k) -> p c k", k=16),
                                                       in0=b3[:, :, 15:16].to_broadcast([128, NCH, 16]), in1=b3, op=ALU.subtract),
                      reads=("bT",), writes=("dT",))
                kb.op("act", lambda e: e.activation(out=eb[:], in_=bT[:], func=AF.Exp), reads=("bT",), writes=("eb",))
                kb.op("act", lambda e: e.activation(out=enb[:], in_=bT[:], func=AF.Exp, scale=-1.0), reads=("bT",), writes=("enb",))
                kb.op("act", lambda e: e.activation(out=ed[:], in_=dT[:], func=AF.Exp), reads=("dT",), writes=("ed",))
                kb.op("pool", lambda e: e.tensor_tensor(out=kinv[:], in0=kk[:], in1=enb[:], op=ALU.mult), reads=("kk", "enb"), writes=("kinv",))
                kb.op("pool", lambda e: e.tensor_tensor(out=kdT[:], in0=kk[:], in1=ed[:], op=ALU.mult), reads=("kk", "ed"), writes=("kdT",))
                p, pk = proj_fm(hh * 128)
                kb.op("dve", lambda e: e.scalar_tensor_tensor(out=qd[:], in0=p[:, 0:NTOK], scalar=SCALE, in1=eb[:], op0=ALU.mult, op1=ALU.mult),
                      reads=(pk, "eb"), writes=("qd",))
                p, pk = proj_fm(3072 + hh * 128)
                kb.op("act", lambda e: e.activation(out=sg[:], in_=p[:, 0:NTOK], func=AF.Silu), reads=(pk,), writes=("sg",))
                for tt in range(GT):
                    kb.op("pe", lambda e, tt=tt: e.transpose(out=pst[:, tt, :], in_=kdT[:, tt * 128:(tt + 1) * 128], identity=cm.ident[:]),
                          reads=("kdT", "ident"), writes=("pst",))
                kb.op("act", lambda e: e.activation(out=kd_tok[:, :, :], in_=pst[:, 0:GT, :], func=AF.Copy), reads=("pst",), writes=("kd_tok",))
                for tt in range(GT):
                    tsl = slice(tt * 128, (tt + 1) * 128)
                    vh = v_tok[:, tt, hh * 128:(hh + 1) * 128]
                    kb.op("pe", lambda e: e.matmul(pA[:, :], kinv[:, tsl], qd[:, tsl], start=True, stop=True), reads=("kinv", "qd"), writes=("pA",))
                    kb.op("dve", lambda e: e.tensor_tensor(out=AT[:], in0=pA[:, :], in1=maskT[:], op=ALU.mult), reads=("pA", "maskT"), writes=("AT",))
                    kb.op("pool", lambda e: e.tensor_tensor(out=Vblk[:], in0=vh.unsqueeze(1).to_broadcast([128, 8, 128]),
                                                            in1=cmask[:, :].unsqueeze(2).to_broadcast([128, 8, 128]), op=ALU.mult),
                          reads=("v_tok", "cmask"), writes=("Vblk",))
                    for half in range(2):
                        kb.op("pe", lambda e, half=half: e.matmul(pS[:, half * 512:(half + 1) * 512], kd_tok[:, tt, :],
                                                                  Vblk[:, half * 4:(half + 1) * 4, :].rearrange("p c v -> p (c v)"), start=True, stop=True),
                              reads=("kd_tok", "Vblk"), writes=("pS",))
                    kb.op("act", lambda e: e.activation(out=Sd[:, :, 1:9], in_=pS[:, :].rearrange("k (c v) -> k v c", v=128), func=AF.Copy),
                          reads=("pS",), writes=("Sd",))
                    kb.op("pool", lambda e: e.tensor_copy(Sd[:, :, 0], carry[hh][:]), reads=(("carry", hh), "Sd"), writes=("Sd",))
                    ebv = eb[:].rearrange("p (c k) -> p c k", k=16)
                    kb.op("act", lambda e: e.activation(out=a9[:, :, 1:9], in_=ebv[:, tt * 8:(tt + 1) * 8, 15].unsqueeze(1).to_broadcast([128, 128, 8]), func=AF.Copy), reads=("eb",), writes=("a9",))
                    kb.op("dve", lambda e: e.tensor_tensor_scan(out=So[:].rearrange("k v c -> k (v c)"), data0=a9[:].rearrange("k v c -> k (v c)"),
                                                                data1=Sd[:].rearrange("k v c -> k (v c)"), initial=0.0, op0=ALU.mult, op1=ALU.add),
                          reads=("a9", "Sd"), writes=("So",))
                    kb.op("act", lambda e: e.activation(out=carry[hh][:], in_=So[:, :, 8], func=AF.Copy), reads=("So",), writes=(("carry", hh),))
                    kb.op("pool", lambda e: e.tensor_copy(Sbf[:].rearrange("k c v -> k v c"), So[:, :, 0:8]), reads=("So",), writes=("Sbf",))
                    for c in range(8):
                        csl = slice(16 * c, 16 * c + 16)
                        kb.op("pe", lambda e, c=c: e.matmul(po[:, csl], Sbf[:, c, :], qd[:, tt * 128 + 16 * c:tt * 128 + 16 * c + 16], start=True, stop=False),
                              reads=("Sbf", "qd"), writes=("po",))
                        kb.op("pe", lambda e, c=c: e.matmul(po[:, csl], vh, AT[:, csl], start=False, stop=True),
                              reads=("v_tok", "AT"), writes=("po",))
                    kb.op("act", lambda e: e.activation(out=osq[:], in_=po[:, :], func=AF.Square), reads=("po",), writes=("osq",))
                    kb.op("pe", lambda e: e.matmul(pss[:, :], ones[:], osq[:], start=True, stop=True), reads=("ones", "osq"), writes=("pss",))
                    kb.op("act", lambda e: e.activation(out=sdv[:], in_=pss[:, :], func=AF.Sqrt, bias=epst[:, 0:1], scale=1.0 / 128.0),
                          reads=("pss", "epst"), writes=("sdv",))
                    kb.op("dve", lambda e: e.reciprocal(out=rsv[:], in_=sdv[:]), reads=("sdv",), writes=("rsv",))
                    kb.op("dve", lambda e: e.tensor_tensor(out=t1[:], in0=po[:, :], in1=rsv[:], op=ALU.mult), reads=("po", "rsv"), writes=("t1",))
                    kb.op("dve", lambda e: e.scalar_tensor_tensor(out=yT[:, hh, tsl], in0=t1[:], scalar=gn[:, 0:1], in1=sg[:, tsl],
                                                                  op0=ALU.mult, op1=ALU.mult), reads=("t1", "gn", "sg"), writes=("yT",))
            for tt in range(GT):
                t = g * GT + tt
                hb = hb2[t % 2]
                hk = ("hb", t % 2)
                for cg in range(2):
                    for hh in range(8):
                        kb.op("pe", lambda e, hh=hh: e.matmul(pS[:, cg * 512:(cg + 1) * 512], yT[:, hh, tt * 128:(tt + 1) * 128],
                                                              w_out[:, hh, cg * 512:(cg + 1) * 512], start=(hh == 0), stop=(hh == 7)),
                              reads=("yT", "w_out"), writes=("pS",))
                kb.dma("sp", lambda e: e.dma_start(out=hb[:], in_=hsrc[t * 128:(t + 1) * 128, :]), reads=((hsrc_name, t),), writes=(hk,))
                for cg in range(2):
                    kb.op("dve", lambda e: e.tensor_tensor(out=hb[:, cg * 512:(cg + 1) * 512], in0=pS[:, cg * 512:(cg + 1) * 512],
                                                           in1=hb[:, cg * 512:(cg + 1) * 512], op=ALU.add), reads=("pS", hk), writes=(hk,))
                kb.dma("sp", lambda e: e.dma_start(out=hdst[t * 128:(t + 1) * 128, :], in_=hb[:]), reads=(hk,), writes=((hdst_name, t),))


        kb.barrier()
def fox_phase(kb, cm, li, hsrc, hsrc_name, hdst, hdst_name, W, scr_d):
    NT = cm.NT
    T = cm.T
    GT = min(4, NT)
    NTOK = GT * 128
    qT_d, kT_d, v_d, ca_d, o_d = scr_d["qT"], scr_d["kT"], scr_d["v"], scr_d["ca"], scr_d["o"]
    with contextlib.ExitStack() as es:
        sb = lambda n, s, d: kb.sb(n, s, d, es)
        ps = lambda n, s, d: kb.ps(n, s, d, es)
        w_in = sb("fw_in", [128, 8, 3072], BF16)
        w_f = sb("fw_f", [128, 8, 16], BF16)
        gb = sb("gb", [128, D], F32)
        bfb = sb("bfb", [128, 16], F32)
        tri = sb("tri", [128, 128], F32)
        onesf = sb("onesf", [128, 128], F32)
        hb2 = [sb("hb", [128, D], F32) for _ in range(2)]
        hn = sb("hn", [128, D], F32)
        hnb = sb("hnb", [128, D], BF16)
        scr = sb("scr", [128, D], F32)
        stat = sb("stat", [128, 4], F32)
        hnT = sb("hnT", [128, 8, NTOK], BF16)
        ob = [sb("ob", [128, NTOK], BF16) for _ in range(2)]
        vb = [sb("vb", [128, 1024], BF16) for _ in range(2)]
        fz = sb("fz", [128, 16], F32)
        fe = sb("fe", [128, 16], F32)
        lf = sb("lf", [128, 16], F32)
        negc = sb("negc", [128, 16], F32)
        carry_b = sb("carry_b", [128, 16], F32)
        ctok = sb("ctok", [128, 16], F32)
        negct = sb("negct", [128, 16], F32)
        identf = sb("identf", [128, 128], F32)
        cT = sb("cT", [16, NTOK], F32)
        hi = sb("hi", [16, NTOK], BF16)
        hi32 = sb("hi32", [16, NTOK], F32)
        r1 = sb("r1", [16, NTOK], F32)
        mid = sb("mid", [16, NTOK], BF16)
        mid32 = sb("mid32", [16, NTOK], F32)
        r2 = sb("r2", [16, NTOK], F32)
        lo = sb("lo", [16, NTOK], BF16)
        pst = ps("pst", [128, 8, 128], BF16)
        pp = [ps("pp", [128, 512], F32) for _ in range(2)]
        pf = ps("pf", [128, 16], F32)
        pc = ps("pc", [16, NTOK], F32)
        pct = ps("pct", [128, 16], F32)
        ptot = ps("ptot", [128, 16], F32)

        for c in range(8):
            kb.dma("pool", lambda e, c=c: e.dma_start(out=w_in[:, c, :], in_=W["fox_w_in"][0, c * 128:(c + 1) * 128, 0:3072]), writes=("fw_in",))
            kb.dma("pool", lambda e, c=c: e.dma_start(out=w_f[:, c, :], in_=W["fox_w_in"][0, c * 128:(c + 1) * 128, 3072:3088]), writes=("fw_f",))
        kb.dma("sp", lambda e: e.dma_start(out=gb[:], in_=W["norm_mix"][li].partition_broadcast(128)), writes=("gb",))
        kb.dma("sp", lambda e: e.dma_start(out=bfb[:], in_=W["fox_b_f"][0].partition_broadcast(128)), writes=("bfb",))
        kb.dma("sp", lambda e: e.dma_start(out=tri[:], in_=W["c_tri"][:, :]), writes=("tri",))
        kb.dma("sp", lambda e: e.dma_start(out=onesf[:], in_=W["c_ones"][:, :]), writes=("onesf",))
        kb.op("dve", lambda e: e.memset(carry_b[:], 0.0), writes=("carry_b",))
        kb.dma("sp", lambda e: e.dma_start(out=identf[:], in_=cm.ident_d[:, :]), writes=("identf",))
        ppi = [0]
        for g in range(NT // GT):
            for tt in range(GT):
                t = g * GT + tt
                load_norm_T(kb, cm, t, hsrc, hsrc_name, hb2[t % 2], ("hb", t % 2), gb, "gb", hn, hnb, scr, stat, pst,
                            hnT[:, :, tt * 128:(tt + 1) * 128], "hnT")
            for which, dst in ((0, qT_d), (1, kT_d)):
                for ch in range(8):
                    b = ppi[0] % 2
                    ppi[0] += 1
                    col0 = which * 1024 + ch * 128
                    for c in range(8):
                        kb.op("pe", lambda e, c=c: e.matmul(pp[b][:, 0:NTOK], w_in[:, c, col0:col0 + 128], hnT[:, c, :], start=(c == 0), stop=(c == 7)),
                              reads=("fw_in", "hnT"), writes=(("pp", b),))
                    kb.op("act", lambda e: e.activation(out=ob[b][:], in_=pp[b][:, 0:NTOK], func=AF.Copy, scale=(0.125 if which == 0 else 1.0)),
                          reads=(("pp", b),), writes=(("ob", b),))
                    kb.dma("sp", lambda e: e.dma_start(out=dst[ch * 128:(ch + 1) * 128, g * NTOK:(g + 1) * NTOK], in_=ob[b][:]),
                           reads=(("ob", b),), writes=(("qk_d", which, ch, g),))
            for tt in range(GT):
                t = g * GT + tt
                vbb = vb[t % 2]
                for cg in range(2):
                    b = ppi[0] % 2
                    ppi[0] += 1
                    for c in range(8):
                        kb.op("pe", lambda e, c=c: e.matmul(pp[b][:, :], hnT[:, c, tt * 128:(tt + 1) * 128],
                                                            w_in[:, c, 2048 + cg * 512:2048 + (cg + 1) * 512], start=(c == 0), stop=(c == 7)),
                              reads=("fw_in", "hnT"), writes=(("pp", b),))
                    kb.op("act", lambda e: e.activation(out=vbb[:, cg * 512:(cg + 1) * 512], in_=pp[b][:, :], func=AF.Copy),
                          reads=(("pp", b),), writes=(("vb", t % 2),))
                kb.dma("sp", lambda e: e.dma_start(out=v_d[t * 128:(t + 1) * 128, :], in_=vbb[:]), reads=(("vb", t % 2),), writes=(("v_d", t),))
                for c in range(8):
                    kb.op("pe", lambda e, c=c: e.matmul(pf[:, :], hnT[:, c, tt * 128:(tt + 1) * 128], w_f[:, c, :], start=(c == 0), stop=(c == 7)),
                          reads=("fw_f", "hnT"), writes=("pf",))
                kb.op("dve", lambda e: e.tensor_tensor(out=fz[:], in0=pf[:, :], in1=bfb[:], op=ALU.add), reads=("pf", "bfb"), writes=("fz",))
                kb.op("act", lambda e: e.activation(out=fe[:], in_=fz[:], func=AF.Exp, scale=-1.0), reads=("fz",), writes=("fe",))
                kb.op("dve", lambda e: e.tensor_scalar(out=fe[:], in0=fe[:], scalar1=1.0, scalar2=None, op0=ALU.add), reads=("fe",), writes=("fe",))
                kb.op("act", lambda e: e.activation(out=lf[:], in_=fe[:], func=AF.Ln), reads=("fe",), writes=("lf",))
                kb.op("dve", lambda e: e.tensor_scalar(out=lf[:], in0=lf[:], scalar1=-1.0, scalar2=None, op0=ALU.mult), reads=("lf",), writes=("lf",))
                kb.op("pe", lambda e: e.matmul(pct[:, :], tri[:], lf[:], start=True, stop=True), reads=("lf", "tri"), writes=("pct",))
                kb.op("dve", lambda e: e.tensor_tensor(out=ctok[:], in0=pct[:, :], in1=carry_b[:], op=ALU.add), reads=("pct", "carry_b"), writes=("ctok",))
                kb.op("pe", lambda e: e.matmul(ptot[:, :], onesf[:], lf[:], start=True, stop=True), reads=("lf", "onesf"), writes=("ptot",))
                kb.op("dve", lambda e: e.tensor_tensor(out=carry_b[:], in0=carry_b[:], in1=ptot[:, :], op=ALU.add), reads=("ptot", "carry_b"), writes=("carry_b",))
                kb.op("act", lambda e: e.activation(out=negct[:], in_=ctok[:], func=AF.Copy, scale=-1.0), reads=("ctok",), writes=("negct",))
                kb.dma("sp", lambda e: e.dma_start(out=scr_d["negc"][t * 128:(t + 1) * 128, :], in_=negct[:]), reads=("negct",), writes=(("negc_d", t),))
                kb.op("pe", lambda e: e.transpose(out=pc[:, tt * 128:(tt + 1) * 128], in_=ctok[:], identity=identf[:]), reads=("ctok", "identf"), writes=("pc",))
            kb.op("act", lambda e: e.activation(out=cT[:], in_=pc[:, :], func=AF.Copy), reads=("pc",), writes=("cT",))
            kb.op("dve", lambda e: e.tensor_copy(hi[:], cT[:]), reads=("cT",), writes=("hi",))
            kb.op("dve", lambda e: e.tensor_copy(hi32[:], hi[:]), reads=("hi",), writes=("hi32",))
            kb.op("dve", lambda e: e.tensor_tensor(out=r1[:], in0=cT[:], in1=hi32[:], op=ALU.subtract), reads=("cT", "hi32"), writes=("r1",))
            kb.op("dve", lambda e: e.tensor_copy(mid[:], r1[:]), reads=("r1",), writes=("mid",))
            kb.op("dve", lambda e: e.tensor_copy(mid32[:], mid[:]), reads=("mid",), writes=("mid32",))
            kb.op("dve", lambda e: e.tensor_tensor(out=r2[:], in0=r1[:], in1=mid32[:], op=ALU.subtract), reads=("r1", "mid32"), writes=("r2",))
            kb.op("dve", lambda e: e.tensor_copy(lo[:], r2[:]), reads=("r2",), writes=("lo",))
            for k3, src_t, sk in ((0, hi, "hi"), (1, mid, "mid"), (2, lo, "lo")):
                kb.dma("sp", lambda e: e.dma_start(out=ca_d[:, k3, g * NTOK:(g + 1) * NTOK], in_=src_t[:]), reads=(sk,), writes=(("ca_d", g),))

        kb.barrier()
    allqk = tuple(("qk_d", w, ch, g) for w in range(2) for ch in range(8) for g in range(NT // GT))
    allv = tuple(("v_d", t) for t in range(NT))
    allca = tuple(("ca_d", g) for g in range(NT // GT))
    allnegc = tuple(("negc_d", t) for t in range(NT))
    with contextlib.ExitStack() as es:
        sb = lambda n, s, d: kb.sb(n, s, d, es)
        ps = lambda n, s, d: kb.ps(n, s, d, es)
        q_aug = [sb("q_aug", [67, T], BF16) for _ in range(2)]
        k_aug = [sb("k_aug", [67, T], BF16) for _ in range(2)]
        V_aug = [sb("V_aug", [128, NT, 65], BF16) for _ in range(2)]
        negc = sb("negc", [128, NT, 16], F32)
        identb = cm.ident
        nmask = sb("nmask", [128, 128], BF16)
        PT = [sb("PT", [128, 512], BF16) for _ in range(2)]
        rcp = sb("rcp", [128, 1], F32)
        otk = [sb("otk", [128, 64], BF16) for _ in range(2)]
        pS = [ps("pS", [128, 512], F32) for _ in range(2)]
        pO = [ps("pO", [128, 65], F32) for _ in range(4)]
        kb.dma("pool", lambda e: e.dma_start(out=nmask[:], in_=W["c_negmask"][:, :]), writes=("nmask",))
        kb.dma("sp", lambda e: e.dma_start(out=negc[:], in_=scr_d["negc"][:, :].rearrange("(t p) h -> p t h", p=128)), reads=allnegc, writes=("negc",))
        si = [0]
        for hh in range(16):
            hb_ = hh % 2
            qa, ka, va = q_aug[hb_], k_aug[hb_], V_aug[hb_]
            qk_, kk_, vk_ = ("q_aug", hb_), ("k_aug", hb_), ("V_aug", hb_)
            kb.dma("sp", lambda e: e.dma_start(out=qa[0:64, :], in_=qT_d[hh * 64:(hh + 1) * 64, :]), reads=allqk, writes=(qk_,))
            kb.dma("sp", lambda e: e.dma_start(out=qa[64:67, :], in_=ca_d[hh, :, :]), reads=allca, writes=(qk_,))
            kb.dma("sp", lambda e: e.dma_start(out=ka[0:64, :], in_=kT_d[hh * 64:(hh + 1) * 64, :]), reads=allqk, writes=(kk_,))
            kb.dma("pool", lambda e: e.dma_start(out=ka[64:67, :], in_=W["c_ones3"][:, 0:T]), writes=(kk_,))
            kb.dma("sp", lambda e: e.dma_start(out=va[:, :, 0:64], in_=v_d[:, hh * 64:(hh + 1) * 64].rearrange("(t p) d -> p t d", p=128)),
                   reads=allv, writes=(vk_,))
            kb.dma("pool", lambda e: e.dma_start(out=va[:, :, 64:65], in_=W["c_ones3"][0:1, 0:NT * 128].rearrange("o (t p) -> p t o", p=128),
                                                 allow_slow_non_contiguous=True), writes=(vk_,))
            for i in range(NT // GT):
                nj = GT * i + GT
                for j in range(nj):
                    r = max(0, j - GT * i)
                    b = si[0] % 2
                    si[0] += 1
                    lhs = ka[0:67, j * 128:(j + 1) * 128]
                    c0 = i * NTOK
                    if j >= GT * i:
                        kb.op("pe", lambda e: e.matmul(pS[b][:, r * 128:(r + 1) * 128], lhs, qa[0:67, c0 + r * 128:c0 + (r + 1) * 128], start=True, stop=False),
                              reads=(qk_, kk_), writes=(("pS", b),))
                        kb.op("pe", lambda e: e.matmul(pS[b][:, r * 128:(r + 1) * 128], identb[:], nmask[:], start=False, stop=True),
                              reads=("ident", "nmask"), writes=(("pS", b),))
                        if r < GT - 1:
                            kb.op("pe", lambda e: e.matmul(pS[b][:, (r + 1) * 128:NTOK], lhs, qa[0:67, c0 + (r + 1) * 128:c0 + NTOK], start=True, stop=True),
                                  reads=(qk_, kk_), writes=(("pS", b),))
                    else:
                        kb.op("pe", lambda e: e.matmul(pS[b][:, 0:NTOK], lhs, qa[0:67, c0:c0 + NTOK], start=True, stop=True),
                              reads=(qk_, kk_), writes=(("pS", b),))
                    kb.op("act", lambda e: e.activation(out=PT[b][:, r * 128:NTOK], in_=pS[b][:, r * 128:NTOK], func=AF.Exp,
                                                        bias=negc[:, j, hh:hh + 1], scale=1.0), reads=(("pS", b), "negc"), writes=(("PT", b),))
                    for rr in range(r, GT):
                        kb.op("pe", lambda e, rr=rr: e.matmul(pO[rr][:, :], PT[b][:, rr * 128:(rr + 1) * 128], va[:, j, :], start=(j == 0), stop=(j == GT * i + rr)),
                              reads=(("PT", b), vk_), writes=(("pO", rr),))
                for rr in range(GT):
                    t = GT * i + rr
                    ob_ = otk[t % 2]
                    kb.op("dve", lambda e: e.reciprocal(out=rcp[:], in_=pO[rr][:, 64:65]), reads=(("pO", rr),), writes=("rcp",))
                    kb.op("dve", lambda e: e.tensor_scalar(out=ob_[:], in0=pO[rr][:, 0:64], scalar1=rcp[:, 0:1], scalar2=None, op0=ALU.mult),
                          reads=(("pO", rr), "rcp"), writes=(("otk", t % 2),))
                    kb.dma("sp", lambda e: e.dma_start(out=o_d[t * 128:(t + 1) * 128, hh * 64:(hh + 1) * 64], in_=ob_[:]),
                           reads=(("otk", t % 2),), writes=(("o_d", t, hh),))

        kb.barrier()
    with contextlib.ExitStack() as es:
        sb = lambda n, s, d: kb.sb(n, s, d, es)
        ps = lambda n, s, d: kb.ps(n, s, d, es)
        w_out = sb("fw_out", [128, 8, 1024], BF16)
        hb2 = [sb("hb", [128, D], F32) for _ in range(2)]
        o_t = [sb("o_t", [128, D], BF16) for _ in range(2)]
        oT = sb("oT", [128, 8, 128], BF16)
        pst = ps("pst", [128, 8, 128], BF16)
        pm = ps("pm", [128, 1024], F32)
        for c in range(8):
            kb.dma("pool", lambda e, c=c: e.dma_start(out=w_out[:, c, :], in_=W["fox_w_out"][0, c * 128:(c + 1) * 128, :]), writes=("fw_out",))
        for t in range(NT):
            b = t % 2
            kb.dma("sp", lambda e: e.dma_start(out=o_t[b][:], in_=o_d[t * 128:(t + 1) * 128, :]),
                   reads=tuple(("o_d", t, hh) for hh in range(16)), writes=(("o_t", b),))
            kb.dma("sp", lambda e: e.dma_start(out=hb2[b][:], in_=hsrc[t * 128:(t + 1) * 128, :]), reads=((hsrc_name, t),), writes=(("hb", b),))
            transpose_tile(kb, cm, o_t[b], ("o_t", b), oT[:, :, :], "oT", pst, "pst")
            for cg in range(2):
                for c in range(8):
                    kb.op("pe", lambda e, c=c: e.matmul(pm[:, cg * 512:(cg + 1) * 512], oT[:, c, :], w_out[:, c, cg * 512:(cg + 1) * 512], start=(c == 0), stop=(c == 7)),
                          reads=("oT", "fw_out"), writes=("pm",))
                kb.op("dve", lambda e: e.tensor_tensor(out=hb2[b][:, cg * 512:(cg + 1) * 512], in0=pm[:, cg * 512:(cg + 1) * 512],
                                                       in1=hb2[b][:, cg * 512:(cg + 1) * 512], op=ALU.add), reads=("pm", ("hb", b)), writes=(("hb", b),))
            kb.dma("sp", lambda e: e.dma_start(out=hdst[t * 128:(t + 1) * 128, :], in_=hb2[b][:]), reads=(("hb", b),), writes=((hdst_name, t),))


        kb.barrier()
def ssd_phase(kb, cm, li, hsrc, hsrc_name, hdst, hdst_name, W, y_d):
    NT = cm.NT
    GT = min(2, NT)
    NTOK = GT * 128
    with contextlib.ExitStack() as es:
        sb = lambda n, s, d: kb.sb(n, s, d, es)
        ps = lambda n, s, d: kb.ps(n, s, d, es)
        w_x = sb("w_x", [128, 8, 4096], BF16)
        w_dt = sb("w_dt", [128, 8, 32], BF16)
        gb = sb("gb", [128, D], F32)
        cw = sb("cw", [128, 32, 4], F32)
        cbias = sb("cbias", [128, 32], F32)
        dtb = sb("dtb", [128, 32], F32)
        aneg = sb("aneg", [128, 32], F32)
        Db = sb("Db", [128, 32], F32)
        tri = sb("tri", [128, 128], F32)
        onesf = sb("onesf", [128, 128], F32)
        identf = sb("identf", [128, 128], F32)
        nmaskf = sb("nmaskf", [128, 128], F32)
        cmaskT = sb("cmaskT", [128, 128], F32)
        hb2 = [sb("hb", [128, D], F32) for _ in range(2)]
        hn = sb("hn", [128, D], F32)
        hnb = sb("hnb", [128, D], BF16)
        scr = sb("scr", [128, D], F32)
        stat = sb("stat", [128, 4], F32)
        hnT = sb("hnT", [128, 8, NTOK], BF16)
        xp = [sb("xp", [128, NTOK + 3], F32) for _ in range(2)]
        acc = [sb("acc", [128, NTOK], F32) for _ in range(2)]
        halo = sb("halo", [128, 32, 3], F32)
        xc = sb("xc", [128, 32, NTOK], BF16)
        x_tok = sb("x_tok", [128, GT, 2048], BF16)
        B_tok = sb("B_tok", [128, GT, 1024], BF16)
        xb = sb("xb", [128, 32], F32)
        dtt = sb("dtt", [128, 32], F32)
        dA = sb("dA", [128, 32], F32)
        cum = sb("cum", [128, 32], F32)
        negcum = sb("negcum", [128, 32], F32)
        dd = sb("dd", [128, 32], F32)
        dec_end = sb("dec_end", [128, 32], F32)
        ecum = sb("ecum", [128, 32], F32)
        etot = sb("etot", [128, 32], F32)
        xdt = sb("xdt", [128, 2048], BF16)
        xdd = sb("xdd", [128, 2048], BF16)
        cbm = sb("cbm", [128, 128], F32)
        tsc = sb("tsc", [128, 128], F32)
        LT = sb("LT", [128, 128], F32)
        MT = sb("MT", [128, 128], BF16)
        S = sb("S", [128, 32, 64], F32)
        S_bf = sb("S_bf", [128, 32, 64], BF16)
        yi = sb("yi", [128, 256], F32)
        y_sb = sb("y_sb", [128, 2048], F32)
        tmp = sb("tmp", [128, 2048], F32)
        pst = ps("pst", [128, 8, 128], BF16)
        pp = [ps("pp", [128, 512], F32) for _ in range(2)]
        pdc = ps("pdc", [128, 96], F32)
        pcb = ps("pcb", [128, 128], F32)
        pcr = ps("pcr", [128, 128], F32)
        py = ps("py", [128, 512], F32)
        pSu = ps("pSu", [128, 256], F32)

        for c in range(8):
            kb.dma("pool", lambda e, c=c: e.dma_start(out=w_x[:, c, :], in_=W["ssm_w_in"][0, c * 128:(c + 1) * 128, 2048:6144]), writes=("w_x",))
            kb.dma("pool", lambda e, c=c: e.dma_start(out=w_dt[:, c, :], in_=W["ssm_w_in"][0, c * 128:(c + 1) * 128, 6144:6176]), writes=("w_dt",))
        kb.dma("sp", lambda e: e.dma_start(out=gb[:], in_=W["norm_mix"][li].partition_broadcast(128)), writes=("gb",))
        kb.dma("sp", lambda e: e.dma_start(out=cw[:], in_=W["ssm_conv_wT"][:, :, :]), writes=("cw",))
        kb.dma("sp", lambda e: e.dma_start(out=cbias[:], in_=W["ssm_conv_bT"][:, :]), writes=("cbias",))
        kb.dma("sp", lambda e: e.dma_start(out=dtb[:], in_=W["ssm_dt_bias"][0].partition_broadcast(128)), writes=("dtb",))
        kb.dma("sp", lambda e: e.dma_start(out=aneg[:], in_=W["ssm_a_log"][0].partition_broadcast(128)), writes=("aneg",))
        kb.dma("sp", lambda e: e.dma_start(out=Db[:], in_=W["ssm_d"][0].partition_broadcast(128)), writes=("Db",))
        kb.dma("sp", lambda e: e.dma_start(out=tri[:], in_=W["c_tri"][:, :]), writes=("tri",))
        kb.dma("sp", lambda e: e.dma_start(out=cmaskT[:], in_=W["c_tri"][:, :]), writes=("cmaskT",))
        kb.dma("sp", lambda e: e.dma_start(out=onesf[:], in_=W["c_ones"][:, :]), writes=("onesf",))
        kb.dma("sp", lambda e: e.dma_start(out=identf[:], in_=cm.ident_d[:, :]), writes=("identf",))
        kb.dma("sp", lambda e: e.dma_start(out=nmaskf[:], in_=W["c_negmask"][:, :]), writes=("nmaskf",))
        kb.op("act", lambda e: e.activation(out=aneg[:], in_=aneg[:], func=AF.Exp), reads=("aneg",), writes=("aneg",))
        kb.op("dve", lambda e: e.tensor_scalar(out=aneg[:], in0=aneg[:], scalar1=-1.0, scalar2=None, op0=ALU.mult), reads=("aneg",), writes=("aneg",))
        kb.op("dve", lambda e: e.memset(halo[:], 0.0), writes=("halo",))
        kb.op("dve", lambda e: e.memset(S[:], 0.0), writes=("S",))
        kb.op("dve", lambda e: e.memset(S_bf[:], 0.0), writes=("S_bf",))
        ppi = [0]
        for g2 in range(NT // GT):
            for tt in range(GT):
                t = g2 * GT + tt
                load_norm_T(kb, cm, t, hsrc, hsrc_name, hb2[t % 2], ("hb", t % 2), gb, "gb", hn, hnb, scr, stat, pst,
                            hnT[:, :, tt * 128:(tt + 1) * 128], "hnT")
            for ch in range(32):
                b = ppi[0] % 2
                ppi[0] += 1
                for c in range(8):
                    kb.op("pe", lambda e, c=c: e.matmul(pp[b][:, 0:NTOK], w_x[:, c, ch * 128:(ch + 1) * 128], hnT[:, c, :], start=(c == 0), stop=(c == 7)),
                          reads=("w_x", "hnT"), writes=(("pp", b),))
                xk, ak = ("xp", b), ("acc", b)
                kb.op("act", lambda e: e.activation(out=xp[b][:, 3:3 + NTOK], in_=pp[b][:, 0:NTOK], func=AF.Copy), reads=(("pp", b),), writes=(xk,))
                kb.op("pool", lambda e: e.tensor_copy(xp[b][:, 0:3], halo[:, ch, :]), reads=("halo", xk), writes=(xk,))
                kb.op("dve", lambda e: e.tensor_scalar(out=acc[b][:], in0=xp[b][:, 0:NTOK], scalar1=cw[:, ch, 0:1], scalar2=cbias[:, ch:ch + 1],
                                                       op0=ALU.mult, op1=ALU.add), reads=(xk, "cw", "cbias"), writes=(ak,))
                for k in range(1, 4):
                    kb.op("dve", lambda e, k=k: e.scalar_tensor_tensor(out=acc[b][:], in0=xp[b][:, k:k + NTOK], scalar=cw[:, ch, k:k + 1], in1=acc[b][:],
                                                                       op0=ALU.mult, op1=ALU.add), reads=(xk, "cw", ak), writes=(ak,))
                kb.op("pool", lambda e: e.tensor_copy(halo[:, ch, :], xp[b][:, NTOK:NTOK + 3]), reads=(xk, "halo"), writes=("halo",))
                kb.op("act", lambda e: e.activation(out=xc[:, ch, :], in_=acc[b][:], func=AF.Silu), reads=(ak,), writes=(("xc", ch),))
            allxc = tuple(("xc", ch) for ch in range(32))
            for tt in range(GT):
                for blk in range(3):
                    for cc in range(8):
                        ch = blk * 8 + cc
                        kb.op("pe", lambda e, cc=cc, ch=ch: e.transpose(out=pst[:, cc, :], in_=xc[:, ch, tt * 128:(tt + 1) * 128], identity=cm.ident[:]),
                              reads=(("xc", ch), "ident"), writes=("pst",))
                    if blk < 2:
                        dst = x_tok[:, tt, blk * 1024:(blk + 1) * 1024].rearrange("p (c k) -> p c k", k=128)
                        kb.op("act", lambda e: e.activation(out=dst, in_=pst[:, :, :], func=AF.Copy), reads=("pst",), writes=("x_tok",))
                    else:
                        dst = B_tok[:, tt, :].rearrange("p (c k) -> p c k", k=128)
                        kb.op("act", lambda e: e.activation(out=dst, in_=pst[:, :, :], func=AF.Copy), reads=("pst",), writes=("B_tok",))
            for tt in range(GT):
                t = g2 * GT + tt
                tsl = slice(tt * 128, (tt + 1) * 128)
                for c in range(8):
                    kb.op("pe", lambda e, c=c: e.matmul(pdc[:, 0:32], hnT[:, c, tsl], w_dt[:, c, :], start=(c == 0), stop=(c == 7)),
                          reads=("w_dt", "hnT"), writes=("pdc",))
                kb.op("dve", lambda e: e.tensor_tensor(out=xb[:], in0=pdc[:, 0:32], in1=dtb[:], op=ALU.add), reads=("pdc", "dtb"), writes=("xb",))
                kb.op("act", lambda e: e.activation(out=xb[:], in_=xb[:], func=AF.Exp), reads=("xb",), writes=("xb",))
                kb.op("dve", lambda e: e.tensor_scalar(out=xb[:], in0=xb[:], scalar1=1.0, scalar2=None, op0=ALU.add), reads=("xb",), writes=("xb",))
                kb.op("act", lambda e: e.activation(out=dtt[:], in_=xb[:], func=AF.Ln), reads=("xb",), writes=("dtt",))
                kb.op("dve", lambda e: e.tensor_tensor(out=dA[:], in0=dtt[:], in1=aneg[:], op=ALU.mult), reads=("dtt", "aneg"), writes=("dA",))
                kb.op("pe", lambda e: e.matmul(pdc[:, 32:64], tri[:], dA[:], start=True, stop=True), reads=("tri", "dA"), writes=("pdc",))
                kb.op("pe", lambda e: e.matmul(pdc[:, 64:96], onesf[:], dA[:], start=True, stop=True), reads=("onesf", "dA"), writes=("pdc",))
                kb.op("act", lambda e: e.activation(out=cum[:], in_=pdc[:, 32:64], func=AF.Copy), reads=("pdc",), writes=("cum",))
                kb.op("act", lambda e: e.activation(out=negcum[:], in_=pdc[:, 32:64], func=AF.Copy, scale=-1.0), reads=("pdc",), writes=("negcum",))
                kb.op("act", lambda e: e.activation(out=ecum[:], in_=pdc[:, 32:64], func=AF.Exp), reads=("pdc",), writes=("ecum",))
                kb.op("act", lambda e: e.activation(out=etot[:], in_=pdc[:, 64:96], func=AF.Exp), reads=("pdc",), writes=("etot",))
                kb.op("dve", lambda e: e.tensor_tensor(out=dd[:], in0=pdc[:, 64:96], in1=cum[:], op=ALU.subtract), reads=("pdc", "cum"), writes=("dd",))
                kb.op("act", lambda e: e.activation(out=dec_end[:], in_=dd[:], func=AF.Exp), reads=("dd",), writes=("dec_end",))
                x3 = x_tok[:, tt, :].rearrange("p (h d) -> p h d", d=64)
                kb.op("dve", lambda e: e.tensor_tensor(out=xdt[:].rearrange("p (h d) -> p h d", d=64), in0=x3,
                                                       in1=dtt[:, :].unsqueeze(2).to_broadcast([128, 32, 64]), op=ALU.mult), reads=("x_tok", "dtt"), writes=("xdt",))
                kb.op("pool", lambda e: e.tensor_tensor(out=xdd[:].rearrange("p (h d) -> p h d", d=64), in0=xdt[:].rearrange("p (h d) -> p h d", d=64),
                                                        in1=dec_end[:, :].unsqueeze(2).to_broadcast([128, 32, 64]), op=ALU.mult), reads=("xdt", "dec_end"), writes=("xdd",))
                for g in range(8):
                    BT = xc[:, 16 + g, tsl]
                    CT = xc[:, 24 + g, tsl]
                    kb.op("pe", lambda e: e.matmul(pcb[:, :], BT, CT, start=True, stop=True), reads=allxc, writes=("pcb",))
                    kb.op("dve", lambda e: e.tensor_tensor(out=cbm[:], in0=pcb[:, :], in1=cmaskT[:], op=ALU.mult), reads=("pcb", "cmaskT"), writes=("cbm",))
                    for h4 in range(4):
                        h = 4 * g + h4
                        hs = slice(h * 64, (h + 1) * 64)
                        kb.op("pool", lambda e: e.tensor_scalar(out=tsc[:], in0=tri[:], scalar1=dA[:, h:h + 1], scalar2=None, op0=ALU.mult),
                              reads=("tri", "dA"), writes=("tsc",))
                        kb.op("pe", lambda e: e.matmul(pcr[:, :], onesf[:], tsc[:], start=True, stop=False), reads=("onesf", "tsc"), writes=("pcr",))
                        kb.op("pe", lambda e: e.matmul(pcr[:, :], identf[:], nmaskf[:], start=False, stop=True), reads=("identf", "nmaskf"), writes=("pcr",))
                        kb.op("act", lambda e: e.activation(out=LT[:], in_=pcr[:, :], func=AF.Exp, bias=negcum[:, h:h + 1], scale=1.0),
                              reads=("pcr", "negcum"), writes=("LT",))
                        kb.op("dve", lambda e: e.tensor_tensor(out=MT[:], in0=LT[:], in1=cbm[:], op=ALU.mult), reads=("LT", "cbm"), writes=("MT",))
                        kb.op("pe", lambda e: e.matmul(py[:, h4 * 64:(h4 + 1) * 64], MT[:], xdt[:, hs], start=True, stop=True), reads=("MT", "xdt"), writes=("py",))
                        kb.op("pe", lambda e: e.matmul(py[:, 256 + h4 * 64:256 + (h4 + 1) * 64], CT, S_bf[:, h, :], start=True, stop=True),
                              reads=allxc + ("S_bf",), writes=("py",))
                        kb.op("pe", lambda e: e.matmul(pSu[:, h4 * 64:(h4 + 1) * 64], B_tok[:, tt, g * 128:(g + 1) * 128], xdd[:, hs], start=True, stop=True),
                              reads=("B_tok", "xdd"), writes=("pSu",))
                    kb.op("act", lambda e: e.activation(out=yi[:], in_=py[:, 0:256], func=AF.Copy), reads=("py",), writes=("yi",))
                    for h4 in range(4):
                        h = 4 * g + h4
                        kb.op("dve", lambda e: e.scalar_tensor_tensor(out=y_sb[:, h * 64:(h + 1) * 64], in0=py[:, 256 + h4 * 64:256 + (h4 + 1) * 64],
                                                                      scalar=ecum[:, h:h + 1], in1=yi[:, h4 * 64:(h4 + 1) * 64], op0=ALU.mult, op1=ALU.add),
                              reads=("py", "ecum", "yi"), writes=("y_sb",))
                    Sg = S[:, 4 * g:4 * g + 4, :]
                    kb.op("dve", lambda e: e.tensor_tensor(out=Sg, in0=Sg, in1=etot[:, 4 * g:4 * g + 4].unsqueeze(2).to_broadcast([128, 4, 64]), op=ALU.mult),
                          reads=("S", "etot"), writes=("S",))
                    kb.op("dve", lambda e: e.tensor_tensor(out=Sg, in0=Sg, in1=pSu[:, :].rearrange("p (h d) -> p h d", d=64), op=ALU.add),
                          reads=("S", "pSu"), writes=("S",))
                    kb.op("act", lambda e: e.activation(out=S_bf[:, 4 * g:4 * g + 4, :], in_=Sg, func=AF.Copy), reads=("S",), writes=("S_bf",))
                kb.op("pool", lambda e: e.tensor_tensor(out=tmp[:].rearrange("p (h d) -> p h d", d=64), in0=x3,
                                                        in1=Db[:, :].unsqueeze(2).to_broadcast([128, 32, 64]), op=ALU.mult), reads=("x_tok", "Db"), writes=("tmp",))
                kb.op("dve", lambda e: e.tensor_tensor(out=y_sb[:], in0=y_sb[:], in1=tmp[:], op=ALU.add), reads=("y_sb", "tmp"), writes=("y_sb",))
                kb.dma("sp", lambda e: e.dma_start(out=y_d[t * 128:(t + 1) * 128, :], in_=y_sb[:]), reads=("y_sb",), writes=(("y_d", t),))
        kb.barrier()
    with contextlib.ExitStack() as es:
        sb = lambda n, s, d: kb.sb(n, s, d, es)
        ps = lambda n, s, d: kb.ps(n, s, d, es)
        w_z = sb("w_z", [128, 8, 2048], BF16)
        w_out = sb("sw_out", [128, 16, 1024], BF16)
        gb = sb("gb", [128, D], F32)
        gnb = sb("gnb", [128, 2048], F32)
        hb2 = [sb("hb", [128, D], F32) for _ in range(2)]
        hn = sb("hn", [128, D], F32)
        hnb = sb("hnb", [128, D], BF16)
        scr = sb("scr", [128, D], F32)
        stat = sb("stat", [128, 4], F32)
        hnT = sb("hnT", [128, 8, 128], BF16)
        zs = sb("zs", [128, 2048], F32)
        yb = sb("yb", [128, 2048], F32)
        sq = sb("sq", [128, 2048], F32)
        ss = sb("ss", [128, 8], F32)
        yn = sb("yn", [128, 2048], BF16)
        yT = sb("yT", [128, 16, 128], BF16)
        pst = ps("pst", [128, 8, 128], BF16)
        pp = [ps("pp", [128, 512], F32) for _ in range(2)]
        pm = ps("pm", [128, 1024], F32)
        for c in range(8):
            kb.dma("pool", lambda e, c=c: e.dma_start(out=w_z[:, c, :], in_=W["ssm_w_in"][0, c * 128:(c + 1) * 128, 0:2048]), writes=("w_z",))
        for c in range(16):
            kb.dma("pool", lambda e, c=c: e.dma_start(out=w_out[:, c, :], in_=W["ssm_w_out"][0, c * 128:(c + 1) * 128, :]), writes=("sw_out",))
        kb.dma("sp", lambda e: e.dma_start(out=gb[:], in_=W["norm_mix"][li].partition_broadcast(128)), writes=("gb",))
        kb.dma("sp", lambda e: e.dma_start(out=gnb[:], in_=W["ssm_gnorm"][0].partition_broadcast(128)), writes=("gnb",))
        ppi = [0]
        for t in range(NT):
            hb = hb2[t % 2]
            hk = ("hb", t % 2)
            load_norm_T(kb, cm, t, hsrc, hsrc_name, hb, hk, gb, "gb", hn, hnb, scr, stat, pst, hnT[:, :, :], "hnT")
            kb.dma("sp", lambda e: e.dma_start(out=yb[:], in_=y_d[t * 128:(t + 1) * 128, :]), reads=(("y_d", t),), writes=("yb",))
            for cg in range(4):
                b = ppi[0] % 2
                ppi[0] += 1
                for c in range(8):
                    kb.op("pe", lambda e, c=c: e.matmul(pp[b][:, :], hnT[:, c, :], w_z[:, c, cg * 512:(cg + 1) * 512], start=(c == 0), stop=(c == 7)),
                          reads=("w_z", "hnT"), writes=(("pp", b),))
                kb.op("act", lambda e: e.activation(out=zs[:, cg * 512:(cg + 1) * 512], in_=pp[b][:, :], func=AF.Silu), reads=(("pp", b),), writes=("zs",))
            kb.op("dve", lambda e: e.tensor_tensor(out=yb[:], in0=yb[:], in1=zs[:], op=ALU.mult), reads=("yb", "zs"), writes=("yb",))
            for g in range(8):
                kb.op("act", lambda e, g=g: e.activation(out=sq[:, g * 256:(g + 1) * 256], in_=yb[:, g * 256:(g + 1) * 256], func=AF.Square,
                                                         accum_out=ss[:, g:g + 1]), reads=("yb",), writes=("sq", "ss"))
            kb.op("dve", lambda e: e.tensor_scalar(out=ss[:], in0=ss[:], scalar1=1.0 / 256.0, scalar2=EPS, op0=ALU.mult, op1=ALU.add), reads=("ss",), writes=("ss",))
            kb.op("act", lambda e: e.activation(out=ss[:], in_=ss[:], func=AF.Sqrt), reads=("ss",), writes=("ss",))
            kb.op("dve", lambda e: e.reciprocal(out=ss[:], in_=ss[:]), reads=("ss",), writes=("ss",))
            kb.op("dve", lambda e: e.tensor_tensor(out=yb[:].rearrange("p (g k) -> p g k", k=256), in0=yb[:].rearrange("p (g k) -> p g k", k=256),
                                                   in1=ss[:, :].unsqueeze(2).to_broadcast([128, 8, 256]), op=ALU.mult), reads=("yb", "ss"), writes=("yb",))
            kb.op("dve", lambda e: e.tensor_tensor(out=yn[:], in0=yb[:], in1=gnb[:], op=ALU.mult), reads=("yb", "gnb"), writes=("yn",))
            for blk in range(2):
                for cc in range(8):
                    kb.op("pe", lambda e, cc=cc: e.transpose(out=pst[:, cc, :], in_=yn[:, (blk * 8 + cc) * 128:(blk * 8 + cc + 1) * 128], identity=cm.ident[:]),
                          reads=("yn", "ident"), writes=("pst",))
                kb.op("act", lambda e: e.activation(out=yT[:, blk * 8:(blk + 1) * 8, :], in_=pst[:, :, :], func=AF.Copy), reads=("pst",), writes=("yT",))
            for cg in range(2):
                for c in range(16):
                    kb.op("pe", lambda e, c=c: e.matmul(pm[:, cg * 512:(cg + 1) * 512], yT[:, c, :], w_out[:, c, cg * 512:(cg + 1) * 512], start=(c == 0), stop=(c == 15)),
                          reads=("yT", "sw_out"), writes=("pm",))
                kb.op("dve", lambda e: e.tensor_tensor(out=hb[:, cg * 512:(cg + 1) * 512], in0=pm[:, cg * 512:(cg + 1) * 512],
                                                       in1=hb[:, cg * 512:(cg + 1) * 512], op=ALU.add), reads=("pm", hk), writes=(hk,))
            kb.dma("sp", lambda e: e.dma_start(out=hdst[t * 128:(t + 1) * 128, :], in_=hb[:]), reads=(hk,), writes=((hdst_name, t),))
        kb.barrier()


def final_phase(kb, cm, hsrc, hsrc_name, y, W):
    NT = cm.NT
    with contextlib.ExitStack() as es:
        sb = lambda n, s, d: kb.sb(n, s, d, es)
        gb = sb("gbf", [128, D], F32)
        h_t = [sb("hf", [128, D], F32) for _ in range(2)]
        o_t = [sb("of", [128, D], F32) for _ in range(2)]
        scr = sb("scrf", [128, D], F32)
        stat = [sb("statf", [128, 4], F32) for _ in range(2)]
        kb.dma("sp", lambda e: e.dma_start(out=gb[:], in_=W["norm_final"].partition_broadcast(128)), writes=("gbf",))
        for t in range(NT):
            b = t % 2
            kb.dma("sp", lambda e: e.dma_start(out=h_t[b][:], in_=hsrc[t * 128:(t + 1) * 128, :]),
                   reads=((hsrc_name, t),), writes=(("hf", b),))
            rmsnorm_tile(kb, h_t[b], ("hf", b), gb, "gbf", o_t[b], ("of", b), scr, "scrf", stat[b], ("statf", b))
            kb.dma("sp", lambda e: e.dma_start(out=y[t * 128:(t + 1) * 128, :], in_=o_t[b][:]),
                   reads=(("of", b),), writes=(("y", t),))


        kb.barrier()
WEIGHT_SPECS = {
    "norm_mix": [4, 1024], "norm_ffn": [4, 1024], "norm_final": [1024], "hgrn_lb_logits": [4, 1024],
    "hgrn_w_in": [2, 1024, 4096], "hgrn_gnorm": [2, 128], "hgrn_w_out": [2, 1024, 1024],
    "fox_w_in": [1, 1024, 3088], "fox_b_f": [1, 16], "fox_w_out": [1, 1024, 1024],
    "ssm_w_in": [1, 1024, 6176], "ssm_conv_w": [1, 4, 4096], "ssm_conv_b": [1, 4096],
    "ssm_dt_bias": [1, 32], "ssm_a_log": [1, 32], "ssm_d": [1, 32], "ssm_gnorm": [1, 2048],
    "ssm_w_out": [1, 2048, 1024], "peer_w_q": [4, 1024, 2048], "peer_keysT": [4, 16, 128, 128],
    "peer_u0": [16384, 1024], "peer_v0": [16384, 1024], "peer_u1": [16384, 1024], "peer_v1": [16384, 1024],
    "peer_u2": [16384, 1024], "peer_v2": [16384, 1024], "peer_u3": [16384, 1024], "peer_v3": [16384, 1024],
    "ssm_conv_wT": [128, 32, 4], "ssm_conv_bT": [128, 32],
    "c_tri": [128, 128], "c_negmask": [128, 128], "c_ones3": [3, 8192],
    "hgrn_lbT": [128, 4, 8], "c_ones": [128, 128], "c_rmask": [128, 512], "c_maskT16": [128, 128], "c_cmask": [128, 8],
}


def build(T=SEQ, plan=("peer0", "final"), used=None):
    kb = KB()
    cm = Common(kb, T)
    x = kb.dram("x", [T, D], F32, kind="ExternalInput")
    y = kb.dram("y", [T, D], F32, kind="ExternalOutput")
    hA = kb.dram("hA", [T, D], F32)
    W = {}
    names = set()
    for p in plan:
        if p.startswith("peer"):
            names |= {"peer_w_q", "peer_keysT", "peer_u" + p[4:], "peer_v" + p[4:], "norm_ffn"}
        if p == "final":
            names |= {"norm_final"}
        if p.startswith("ssd"):
            names |= {"ssm_w_in", "ssm_conv_wT", "ssm_conv_bT", "ssm_dt_bias", "ssm_a_log", "ssm_d", "ssm_gnorm", "ssm_w_out", "norm_mix",
                      "c_tri", "c_negmask", "c_ones"}
        if p.startswith("fox"):
            names |= {"fox_w_in", "fox_b_f", "fox_w_out", "norm_mix", "c_tri", "c_negmask", "c_ones3", "c_ones"}
        if p.startswith("hgrn"):
            names |= {"hgrn_w_in", "hgrn_w_out", "hgrn_gnorm", "hgrn_lbT", "norm_mix", "c_ones", "c_rmask", "c_maskT16", "c_cmask"}
    for n in sorted(names):
        W[n] = kb.dram(n, WEIGHT_SPECS[n], F32, kind="ExternalInput")
    cur, cur_name = x, "x"
    for p in plan:
        kb.barrier()
        if p.startswith("peer"):
            li = int(p[4:])
            peer_phase(kb, cm, li, cur, cur_name, hA, "hA", W)
            cur, cur_name = hA, "hA"
        elif p.startswith("hgrn"):
            li = int(p[4:])
            hgrn_phase(kb, cm, li, li // 3, cur, cur_name, hA, "hA", W)
            cur, cur_name = hA, "hA"
        elif p.startswith("ssd"):
            li = int(p[3:])
            y_d = kb.dram("s_y", [T, 2048], F32)
            ssd_phase(kb, cm, li, cur, cur_name, hA, "hA", W, y_d)
            cur, cur_name = hA, "hA"
        elif p.startswith("fox"):
            li = int(p[3:])
            dk = "ExternalOutput" if DEBUG_SCRATCH else "Internal"
            scr_d = {"qT": kb.dram("f_qT", [1024, T], BF16, dk), "kT": kb.dram("f_kT", [1024, T], BF16, dk), "v": kb.dram("f_v", [T, 1024], BF16, dk),
                     "ca": kb.dram("f_ca", [16, 3, T], BF16, dk), "o": kb.dram("f_o", [T, 1024], BF16, dk), "negc": kb.dram("f_negc", [T, 16], F32, dk)}
            fox_phase(kb, cm, li, cur, cur_name, hA, "hA", W, scr_d)
            cur, cur_name = hA, "hA"
        elif p == "final":
            final_phase(kb, cm, cur, cur_name, y, W)
    kb.finish([("y", t) for t in range(cm.NT)])
    kb.es.close()
    return kb, sorted(names)


def host_consts():
    s = np.arange(128)
    c = {"c_ident": np.eye(128, dtype=np.float32)}
    c["c_ones"] = np.ones((128, 128), np.float32)
    c["c_rmask"] = np.tile((np.arange(512) % 16 != 0).astype(np.float32)[None, :], (128, 1))
    c["c_maskT16"] = ((s[:, None] // 16 == s[None, :] // 16) & (s[:, None] <= s[None, :])).astype(np.float32)
    c["c_cmask"] = (s[:, None] // 16 == np.arange(8)[None, :]).astype(np.float32)
    c["c_tri"] = (s[:, None] <= s[None, :]).astype(np.float32)
    c["c_negmask"] = np.where(s[:, None] <= s[None, :], 0.0, -30000.0).astype(np.float32)
    c["c_ones3"] = np.ones((3, 8192), np.float32)
    return c


def layout_weights(inp):
    out = dict(inp)
    for l in range(4):
        if "peer_u" in inp:
            out["peer_u%d" % l] = np.asarray(inp["peer_u"])[l]
            out["peer_v%d" % l] = np.asarray(inp["peer_v"])[l]
    if "peer_keys" in inp:
        pk = np.asarray(inp["peer_keys"])
        out["peer_keysT"] = np.ascontiguousarray(pk.reshape(4, 16, 128, 128).transpose(0, 1, 3, 2))
    if "ssm_conv_w" in inp:
        out["ssm_conv_wT"] = np.ascontiguousarray(np.asarray(inp["ssm_conv_w"])[0].reshape(4, 32, 128).transpose(2, 1, 0))
        out["ssm_conv_bT"] = np.ascontiguousarray(np.asarray(inp["ssm_conv_b"])[0].reshape(32, 128).T)
    if "hgrn_lb_logits" in inp:
        out["hgrn_lbT"] = np.ascontiguousarray(np.asarray(inp["hgrn_lb_logits"]).reshape(4, 8, 128).transpose(2, 0, 1))
    return out


FULL_PLAN = ("hgrn0", "peer0", "fox1", "peer1", "ssd2", "peer2", "hgrn3", "peer3", "final")


def kernel(**inputs):
    inp = layout_weights({k: np.asarray(v) for k, v in inputs.items()})
    inp.update(host_consts())
    kb, names = build(SEQ, plan=FULL_PLAN)
    shared = {n: np.ascontiguousarray(inp[n], dtype=np.float32) for n in names}
    shared["c_ident"] = inp["c_ident"]
    in_maps = []
    for c in range(NCORES):
        m = dict(shared)
        m["x"] = np.ascontiguousarray(inp["x"][c], dtype=np.float32)
        in_maps.append(m)
    res = run_bass_kernel_spmd(kb.nc, in_maps, core_ids=list(range(NCORES)))
    return np.stack([np.asarray(r["y"]) for r in res.results], axis=0).astype(np.float32)
```

```python
import contextlib
import numpy as np
import concourse.bass as bass
import concourse.mybir as mybir
from concourse.bass_utils import run_bass_kernel_spmd

F32 = mybir.dt.float32
BF16 = mybir.dt.bfloat16
U32 = mybir.dt.uint32
AF = mybir.ActivationFunctionType
ALU = mybir.AluOpType
AX = mybir.AxisListType

D = 1024
NCORES = 8
SEQ = 8192
EPS = 1e-6
NEG = -1.0e30
DEBUG_SCRATCH = False


class KB:
    R_DMA = 8

    def __init__(self):
        self.nc = bass.Bass("TRN2", target_bir_lowering=False)
        self.es = contextlib.ExitStack()
        nc = self.nc
        self.eng = {"pe": nc.tensor, "dve": nc.vector, "act": nc.scalar, "pool": nc.gpsimd, "sp": nc.sync}
        self.sems = {}
        for e in ("pe", "dve", "act", "pool"):
            self.sems[e] = self.es.enter_context(nc.semaphore("s_" + e))
        for q in ("sp", "pool", "act"):
            for s in range(self.R_DMA):
                self.sems[("dma", q, s)] = self.es.enter_context(nc.semaphore(f"d_{q}_{s}"))
        self.cnt = {k: 0 for k in self.sems}
        self.dma_i = {"sp": 0, "pool": 0, "act": 0}
        self.waited = {e: {} for e in self.eng}
        self.last_w = {}
        self.readers = {}
        self.n_ins = 0
        self.uid = 0
        self.psum_names = set()

    def sb(self, name, shape, dt, es=None):
        self.uid += 1
        return (es or self.es).enter_context(self.nc.sbuf_tensor(f"{name}_{self.uid}", list(shape), dt))

    def ps(self, name, shape, dt, es=None):
        self.uid += 1
        self.psum_names.add(name)
        shape = list(shape)
        esz = 2 if dt == BF16 else 4
        per_part = esz
        for s in shape[1:]:
            per_part *= s
        if per_part >= 2048 or len(shape) != 2:
            assert per_part % 2048 == 0, (name, shape)
            return (es or self.es).enter_context(self.nc.psum_tensor(f"{name}_{self.uid}", shape, dt))
        t = (es or self.es).enter_context(self.nc.psum_tensor(f"{name}_{self.uid}", [shape[0], 2048 // esz], dt))
        return t[:, 0:shape[1]]

    def dram(self, name, shape, dt, kind="Internal"):
        return self.nc.dram_tensor(name, list(shape), dt, kind=kind).ap()

    def _deps(self, reads, writes):
        deps = {}

        def add(h):
            if h is None:
                return
            k, c = h
            if deps.get(k, 0) < c:
                deps[k] = c

        for k in reads:
            add(self.last_w.get(k))
        for k in writes:
            add(self.last_w.get(k))
            for sk, c in self.readers.get(k, {}).items():
                add((sk, c))
        return deps

    def _waits(self, eng, deps):
        E = self.eng[eng]
        w = self.waited[eng]
        for sk, c in deps.items():
            if eng == "pe" and sk == "pe":
                continue
            if w.get(sk, 0) < c:
                E.wait_ge(self.sems[sk], c)
                w[sk] = c
                self.n_ins += 1

    def _record(self, h, reads, writes):
        sk, c = h
        for k in reads:
            self.readers.setdefault(k, {})[sk] = c
        for k in writes:
            self.last_w[k] = h
            self.readers[k] = {}

    def _excl(self, reads, writes):
        r2, w2 = [], list(writes)
        for k in reads:
            root = k if isinstance(k, str) else k[0]
            (w2 if root in self.psum_names else r2).append(k)
        return r2, w2

    def op(self, eng, fn, reads=(), writes=()):
        reads, writes = self._excl(reads, writes)
        deps = self._deps(reads, writes)
        self._waits(eng, deps)
        ins = fn(self.eng[eng])
        self.cnt[eng] += 1
        ins.then_inc(self.sems[eng], 1)
        self.n_ins += 1
        self._record((eng, self.cnt[eng]), reads, writes)

    def dma(self, q, fn, reads=(), writes=()):
        slot = self.dma_i[q] % self.R_DMA
        self.dma_i[q] += 1
        sk = ("dma", q, slot)
        reads, writes = self._excl(reads, writes)
        deps = self._deps(reads, writes)
        if self.cnt[sk] > 0:
            deps[sk] = max(deps.get(sk, 0), self.cnt[sk])
        self._waits(q, deps)
        ins = fn(self.eng[q])
        self.cnt[sk] += 16
        ins.then_inc(self.sems[sk], 16)
        self.n_ins += 1
        self._record((sk, self.cnt[sk]), reads, writes)

    def barrier(self):
        deps = {sk: c for sk, c in self.cnt.items() if c > 0}
        for eng in self.eng:
            self._waits(eng, dict(deps))

    def finish(self, keys):
        deps = self._deps(keys, ())
        self._waits("sp", deps)


class Common:
    def __init__(self, kb, T):
        self.kb = kb
        self.T = T
        self.NT = T // 128
        nc = kb.nc
        self.ident_d = kb.dram("c_ident", [128, 128], F32, kind="ExternalInput")
        self.ident = kb.sb("ident", [128, 128], BF16)
        kb.dma("pool", lambda e: e.dma_start(out=self.ident[:], in_=self.ident_d[:, :]), reads=(), writes=("ident",))


def rmsnorm_tile(kb, h_t, hk, gb, gk, out_t, ok, scr, sk, stat, stk, es_keys=()):
    kb.op("act", lambda e: e.activation(out=scr[:], in_=h_t[:], func=AF.Square, accum_out=stat[:, 0:1]),
          reads=(hk,), writes=(sk, stk))
    kb.op("dve", lambda e: e.tensor_scalar(out=stat[:, 1:2], in0=stat[:, 0:1], scalar1=1.0 / D, scalar2=EPS,
                                           op0=ALU.mult, op1=ALU.add), reads=(stk,), writes=(stk,))
    kb.op("act", lambda e: e.activation(out=stat[:, 2:3], in_=stat[:, 1:2], func=AF.Sqrt), reads=(stk,), writes=(stk,))
    kb.op("dve", lambda e: e.reciprocal(out=stat[:, 3:4], in_=stat[:, 2:3]), reads=(stk,), writes=(stk,))
    kb.op("dve", lambda e: e.scalar_tensor_tensor(out=out_t[:], in0=h_t[:], scalar=stat[:, 3:4], in1=gb[:],
                                                  op0=ALU.mult, op1=ALU.mult), reads=(hk, stk, gk), writes=(ok,))


def transpose_tile(kb, cm, src_bf, srck, dst_ap, dstk, pst, pstk, nch=8):
    for c in range(nch):
        kb.op("pe", lambda e, c=c: e.transpose(out=pst[:, c, :], in_=src_bf[:, c * 128:(c + 1) * 128], identity=cm.ident[:]),
              reads=(srck, "ident"), writes=(pstk,))
    kb.op("act", lambda e: e.activation(out=dst_ap, in_=pst[:, 0:nch, :], func=AF.Copy), reads=(pstk,), writes=(dstk,))


def convert_phase(kb, W, layers):
    RC = 512
    with contextlib.ExitStack() as es:
        bufs = [kb.sb("cvt", [128, RC // 128, 1024], BF16, es) for _ in range(4)]
        i = 0
        for l in layers:
            for nm in ("u", "v"):
                src = W["peer_%s%d" % (nm, l)]
                dst = W["peer_%sb%d" % (nm, l)]
                for ch in range(16384 // RC):
                    b = i % 4
                    i += 1
                    rows = slice(ch * RC, (ch + 1) * RC)
                    kb.dma("pool", lambda e: e.dma_start(out=bufs[b][:], in_=src[rows, :].rearrange("(p r) d -> p r d", p=128)), writes=(("cvt", b),))
                    kb.dma("sp", lambda e: e.dma_start(out=dst[rows, :].rearrange("(p r) d -> p r d", p=128), in_=bufs[b][:]),
                           reads=(("cvt", b),), writes=(("tab", nm, l),))
        kb.barrier()


def peer_phase(kb, cm, li, hsrc, hsrc_name, hdst, hdst_name, W):
    NT = cm.NT
    NG = 10
    with contextlib.ExitStack() as es:
        sb = lambda n, s, d: kb.sb(n, s, d, es)
        ps = lambda n, s, d: kb.ps(n, s, d, es)
        wq = sb("wq", [128, 8, 2048], BF16)
        keysT = sb("keysT", [128, 16, 128], BF16)
        gb = sb("gb", [128, D], F32)
        identf = sb("identf", [128, 128], F32)
        h_t = [sb("h_t", [128, D], F32) for _ in range(2)]
        hn2 = [sb("hn", [128, D], F32) for _ in range(2)]
        hnb = sb("hnb", [128, D], BF16)
        hnT = sb("hnT", [128, 8, 128], BF16)
        stat = sb("stat", [128, 4], F32)
        qT = sb("qT", [128, 16, 128], BF16)
        s_sb = sb("s_sb", [128, 16, 128], F32)
        wk = sb("wk", [128, 16, 128], F32)
        top = sb("top", [128, 16, 16], F32)
        ix = sb("ix", [128, 16, 16], U32)
        ixf = sb("ixf", [128, 16, 16], F32)
        cs = sb("cs", [128, 8, 256], F32)
        ci = sb("ci", [128, 8, 256], F32)
        wk2 = sb("wk2", [128, 8, 256], F32)
        top2 = sb("top2", [128, 8, 16], F32)
        t2c = sb("t2c", [128, 8, 16], F32)
        nmax = sb("nmax", [128, 8], F32)
        ez = sb("ez", [128, 8, 16], F32)
        zz = sb("zz", [128, 8], F32)
        rz = sb("rz", [128, 8], F32)
        eidx = sb("eidx", [128, 128], F32)
        gates2 = [sb("gates", [128, 128], F32) for _ in range(2)]
        eidu2 = [sb("eidu", [128, 128], U32) for _ in range(2)]
        adot2 = [sb("adot", [128, 128], F32) for _ in range(2)]
        aact2 = [sb("aact", [128, 128], F32) for _ in range(2)]
        junk = sb("junk", [128, D], F32)
        junk2 = sb("junk2", [128, 256], F32)
        ug = [sb("ug", [128, D], BF16) for _ in range(NG)]
        vg = [sb("vg", [128, D], BF16) for _ in range(NG)]
        Ds = [sb("Ds", [128, 128], BF16) for _ in range(4)]
        pst = ps("pst", [128, 8, 128], BF16)
        pq = ps("pq", [128, 4, 128], F32)
        psc = [ps("psc", [128, 4, 128], F32) for _ in range(4)]
        pacc = ps("pacc", [128, 1024], F32)

        for c in range(8):
            kb.dma("pool", lambda e, c=c: e.dma_start(out=wq[:, c, :], in_=W["peer_w_q"][li, c * 128:(c + 1) * 128, :]), writes=("wq",))
        kb.dma("pool", lambda e: e.dma_start(out=keysT[:], in_=W["peer_keysT"][li].rearrange("j d k -> d j k")), writes=("keysT",))
        kb.dma("sp", lambda e: e.dma_start(out=gb[:], in_=W["norm_ffn"][li].partition_broadcast(128)), writes=("gb",))
        kb.dma("sp", lambda e: e.dma_start(out=identf[:], in_=cm.ident_d[:, :]), writes=("identf",))
        u_tab = W["peer_ub%d" % li]
        v_tab = W["peer_vb%d" % li]

        def stage_A(t):
            p2 = t % 2
            hb, hk = h_t[p2], ("h_t", p2)
            hn, hnk = hn2[p2], ("hn", p2)
            gates, gk = gates2[p2], ("gates", p2)
            eidu, ek = eidu2[p2], ("eidu", p2)
            kb.dma("sp", lambda e: e.dma_start(out=hb[:], in_=hsrc[t * 128:(t + 1) * 128, :]), reads=((hsrc_name, t),), writes=(hk,))
            rmsnorm_tile(kb, hb, hk, gb, "gb", hn, hnk, junk, "junk", stat, "stat")
            kb.op("act", lambda e: e.activation(out=hnb[:], in_=hn[:], func=AF.Copy), reads=(hnk,), writes=("hnb",))
            transpose_tile(kb, cm, hnb, "hnb", hnT[:, :, :], "hnT", pst, "pst")
            for jg in range(4):
                for jj in range(4):
                    j = jg * 4 + jj
                    for c in range(8):
                        kb.op("pe", lambda e, c=c: e.matmul(pq[:, jj, :], wq[:, c, j * 128:(j + 1) * 128], hnT[:, c, :], start=(c == 0), stop=(c == 7)),
                              reads=("wq", "hnT"), writes=("pq",))
                kb.op("act", lambda e: e.activation(out=qT[:, jg * 4:(jg + 1) * 4, :], in_=pq[:], func=AF.Copy), reads=("pq",), writes=(("qT", jg),))
            for jg in range(4):
                for jj in range(4):
                    j = jg * 4 + jj
                    kb.op("pe", lambda e: e.matmul(psc[jg][:, jj, :], qT[:, j, :], keysT[:, j, :], start=True, stop=True),
                          reads=(("qT", jg), "keysT"), writes=(("psc", jg),))
                kb.op("act", lambda e: e.activation(out=s_sb[:, jg * 4:(jg + 1) * 4, :], in_=psc[jg][:], func=AF.Copy),
                      reads=(("psc", jg),), writes=(("s_sb", jg),))
            for j in range(16):
                sk = ("s_sb", j // 4)
                tk = ("top", j)
                kb.op("dve", lambda e: e.max(out=top[:, j, 0:8], in_=s_sb[:, j, :]), reads=(sk,), writes=(tk,))
                kb.op("dve", lambda e: e.max_index(out=ix[:, j, 0:8], in_max=top[:, j, 0:8], in_values=s_sb[:, j, :]), reads=(sk, tk), writes=(("ix", j),))
                kb.op("dve", lambda e: e.match_replace(out=wk[:, j, :], in_to_replace=top[:, j, 0:8], in_values=s_sb[:, j, :], imm_value=NEG),
                      reads=(sk, tk), writes=(("wk", j),))
                kb.op("dve", lambda e: e.max(out=top[:, j, 8:16], in_=wk[:, j, :]), reads=(("wk", j),), writes=(tk,))
                kb.op("dve", lambda e: e.max_index(out=ix[:, j, 8:16], in_max=top[:, j, 8:16], in_values=wk[:, j, :]), reads=(("wk", j), tk), writes=(("ix", j),))
            allix = tuple(("ix", j) for j in range(16))
            alltop = tuple(("top", j) for j in range(16))
            kb.op("dve", lambda e: e.tensor_copy(ixf[:], ix[:]), reads=allix, writes=("ixf",))
            for hh in range(8):
                j0, j1 = 2 * hh, 2 * hh + 1
                csv = cs[:, hh, :].rearrange("p (a b) -> p a b", b=16)
                civ = ci[:, hh, :].rearrange("p (a b) -> p a b", b=16)
                kb.op("dve", lambda e: e.tensor_tensor(out=csv, in0=top[:, j0, :].unsqueeze(2).to_broadcast([128, 16, 16]),
                                                       in1=top[:, j1, :].unsqueeze(1).to_broadcast([128, 16, 16]), op=ALU.add),
                      reads=alltop, writes=(("cs", hh),))
                kb.op("dve", lambda e: e.tensor_scalar(out=civ, in0=ixf[:, j0, :].unsqueeze(2).to_broadcast([128, 16, 16]), scalar1=128.0, scalar2=None, op0=ALU.mult),
                      reads=("ixf",), writes=(("ci", hh),))
                kb.op("dve", lambda e: e.tensor_tensor(out=civ, in0=civ, in1=ixf[:, j1, :].unsqueeze(1).to_broadcast([128, 16, 16]), op=ALU.add),
                      reads=("ixf", ("ci", hh)), writes=(("ci", hh),))
                t2k = ("top2", hh)
                kb.op("dve", lambda e: e.max(out=top2[:, hh, 0:8], in_=cs[:, hh, :]), reads=(("cs", hh),), writes=(t2k,))
                kb.op("dve", lambda e: e.match_replace(out=wk2[:, hh, :], in_to_replace=top2[:, hh, 0:8], in_values=cs[:, hh, :], imm_value=NEG),
                      reads=(("cs", hh), t2k), writes=(("wk2", hh),))
                kb.op("dve", lambda e: e.max(out=top2[:, hh, 8:16], in_=wk2[:, hh, :]), reads=(("wk2", hh),), writes=(t2k,))
                for k in range(16):
                    kb.op("dve", lambda e: e.scalar_tensor_tensor(out=junk2[:], in0=cs[:, hh, :], scalar=top2[:, hh, k:k + 1], in1=ci[:, hh, :],
                                                                  op0=ALU.is_equal, op1=ALU.mult, accum_out=eidx[:, hh * 16 + k:hh * 16 + k + 1]),
                          reads=(("cs", hh), ("ci", hh), t2k), writes=("junk2", "eidx"))
            allt2 = tuple(("top2", hh) for hh in range(8))
            kb.op("dve", lambda e: e.tensor_scalar(out=nmax[:], in0=top2[:, :, 0], scalar1=-1.0, scalar2=None, op0=ALU.mult), reads=allt2, writes=("nmax",))
            kb.op("dve", lambda e: e.tensor_copy(t2c[:], top2[:]), reads=allt2, writes=("t2c",))
            kb.op("dve", lambda e: e.tensor_scalar(out=eidx[:], in0=eidx[:], scalar1=0.0, scalar2=16383.0, op0=ALU.max, op1=ALU.min),
                  reads=("eidx",), writes=("eidx",))
            kb.op("dve", lambda e: e.tensor_copy(eidu[:], eidx[:]), reads=("eidx",), writes=(ek,))

        def stage_A2(t):
            p2 = t % 2
            gates, gk = gates2[p2], ("gates", p2)
            for hh in range(8):
                kb.op("act", lambda e: e.activation(out=ez[:, hh, :], in_=t2c[:, hh, :], func=AF.Exp, bias=nmax[:, hh:hh + 1], scale=1.0,
                                                    accum_out=zz[:, hh:hh + 1]), reads=("t2c", "nmax"), writes=("ez", "zz"))
            kb.op("dve", lambda e: e.reciprocal(out=rz[:], in_=zz[:]), reads=("zz",), writes=("rz",))
            kb.op("dve", lambda e: e.tensor_tensor(out=gates[:].rearrange("p (h k) -> p h k", k=16), in0=ez[:],
                                                   in1=rz[:].unsqueeze(2).to_broadcast([128, 8, 16]), op=ALU.mult), reads=("ez", "rz"), writes=(gk,))

        gi = [0, 0]

        def stage_B(t):
            p2 = t % 2
            hn, hnk = hn2[p2], ("hn", p2)
            eidu, ek = eidu2[p2], ("eidu", p2)
            adot, ak = adot2[p2], ("adot", p2)
            aact, aak = aact2[p2], ("aact", p2)
            for s in range(128):
                b = gi[0] % NG
                gi[0] += 1
                kb.dma("pool", lambda e: e.indirect_dma_start(out=ug[b][:], out_offset=None, in_=u_tab,
                                                              in_offset=bass.IndirectOffsetOnAxis(ap=eidu[:, s:s + 1], axis=0)),
                       reads=(ek, ("tab", "u", li)), writes=(("ug", b),))
                kb.op("dve", lambda e: e.scalar_tensor_tensor(out=junk[:], in0=ug[b][:], scalar=1.0, in1=hn[:], op0=ALU.mult, op1=ALU.mult,
                                                              accum_out=adot[:, s:s + 1]), reads=(("ug", b), hnk), writes=("junk", ak))
            kb.op("act", lambda e: e.activation(out=aact[:], in_=adot[:], func=AF.Gelu), reads=(ak,), writes=(aak,))
            kb.op("dve", lambda e: e.tensor_tensor(out=aact[:], in0=aact[:], in1=gates2[p2][:], op=ALU.mult), reads=(aak, ("gates", p2)), writes=(aak,))

        def stage_C(t):
            p2 = t % 2
            hb, hk = h_t[p2], ("h_t", p2)
            eidu, ek = eidu2[p2], ("eidu", p2)
            aact, aak = aact2[p2], ("aact", p2)
            for s in range(128):
                b = gi[1] % NG
                d4 = gi[1] % 4
                gi[1] += 1
                kb.dma("pool", lambda e: e.indirect_dma_start(out=vg[b][:], out_offset=None, in_=v_tab,
                                                              in_offset=bass.IndirectOffsetOnAxis(ap=eidu[:, s:s + 1], axis=0)),
                       reads=(ek, ("tab", "v", li)), writes=(("vg", b),))
                kb.op("act", lambda e: e.activation(out=Ds[d4][:], in_=identf[:], func=AF.Copy, scale=aact[:, s:s + 1]),
                      reads=("identf", aak), writes=(("Ds", d4),))
                for half in range(2):
                    kb.op("pe", lambda e: e.matmul(pacc[:, half * 512:(half + 1) * 512], Ds[d4][:], vg[b][:, half * 512:(half + 1) * 512],
                                                   start=(s == 0), stop=(s == 127)), reads=(("Ds", d4), ("vg", b)), writes=("pacc",))
            for half in range(2):
                kb.op("dve", lambda e: e.tensor_tensor(out=hb[:, half * 512:(half + 1) * 512], in0=pacc[:, half * 512:(half + 1) * 512],
                                                       in1=hb[:, half * 512:(half + 1) * 512], op=ALU.add), reads=("pacc", hk), writes=(hk,))
            kb.dma("sp", lambda e: e.dma_start(out=hdst[t * 128:(t + 1) * 128, :], in_=hb[:]), reads=(hk,), writes=((hdst_name, t),))

        stage_A(0)
        stage_A2(0)
        for t in range(NT):
            stage_B(t)
            if t + 1 < NT:
                stage_A(t + 1)
            stage_C(t)
            if t + 1 < NT:
                stage_A2(t + 1)
        kb.barrier()


def load_norm_T(kb, cm, t, hsrc, hsrc_name, hb, hk, gb, gk, hn, hnb, scr, stat, pst, dst_ap, dstk):
    kb.dma("sp", lambda e: e.dma_start(out=hb[:], in_=hsrc[t * 128:(t + 1) * 128, :]), reads=((hsrc_name, t),), writes=(hk,))
    rmsnorm_tile(kb, hb, hk, gb, gk, hn, "hn", scr, "scr", stat, "stat")
    kb.op("act", lambda e: e.activation(out=hnb[:], in_=hn[:], func=AF.Copy), reads=("hn",), writes=("hnb",))
    transpose_tile(kb, cm, hnb, "hnb", dst_ap, dstk, pst, "pst")


def hgrn_phase(kb, cm, li, j, hsrc, hsrc_name, hdst, hdst_name, W):
    NT = cm.NT
    GT = min(4, NT)
    NTOK = GT * 128
    NCH = NTOK // 16
    SCALE = 128.0 ** -0.5
    with contextlib.ExitStack() as es:
        sb = lambda n, s, d: kb.sb(n, s, d, es)
        ps = lambda n, s, d: kb.ps(n, s, d, es)
        w_in = sb("w_in", [128, 8, 4096], BF16)
        w_out = sb("w_out", [128, 8, 1024], BF16)
        gb = sb("gb", [128, D], F32)
        gn = sb("gn", [128, 1], F32)
        lbz = sb("lbz", [128, 4, 8], F32)
        lbe = sb("lbe", [128, 4, 8], F32)
        den = sb("den", [128, 8], F32)
        num = sb("num", [128, 8], F32)
        lb = sb("lb", [128, 8], F32)
        oml = sb("oml", [128, 8], F32)
        epst = sb("epst", [128, 1], F32)
        rmask = sb("rmask", [128, 512], F32)
        maskT = sb("maskT", [128, 128], F32)
        cmask = sb("cmask", [128, 8], F32)
        ones = sb("ones", [128, 128], BF16)
        hb2 = [sb("hb", [128, D], F32) for _ in range(2)]
        hn = sb("hn", [128, D], F32)
        hnb = sb("hnb", [128, D], BF16)
        scr = sb("scr", [128, D], F32)
        stat = sb("stat", [128, 4], F32)
        hnT = sb("hnT", [128, 8, NTOK], BF16)
        v_tok = sb("v_tok", [128, GT, 1024], BF16)
        fs = sb("fs", [128, NTOK], F32)
        fT = sb("fT", [128, NTOK], F32)
        lf = sb("lf", [128, NTOK], F32)
        kk = sb("kk", [128, NTOK], F32)
        bT = sb("bT", [128, NTOK], F32)
        dT = sb("dT", [128, NTOK], F32)
        eb = sb("eb", [128, NTOK], F32)
        enb = sb("enb", [128, NTOK], F32)
        ed = sb("ed", [128, NTOK], F32)
        qd = sb("qd", [128, NTOK], BF16)
        kinv = sb("kinv", [128, NTOK], BF16)
        kdT = sb("kdT", [128, NTOK], BF16)
        kd_tok = sb("kd_tok", [128, GT, 128], BF16)
        sg = sb("sg", [128, NTOK], BF16)
        yT = sb("yT", [128, 8, NTOK], BF16)
        carry = [sb("carry", [128, 128], F32) for _ in range(8)]
        Sd = sb("Sd", [128, 128, 9], F32)
        So = sb("So", [128, 128, 9], F32)
        a9 = sb("a9", [128, 128, 9], F32)
        Sbf = sb("Sbf", [128, 8, 128], BF16)
        Vblk = sb("Vblk", [128, 8, 128], BF16)
        AT = sb("AT", [128, 128], BF16)
        osq = sb("osq", [128, 128], BF16)
        sdv = sb("sdv", [128, 128], F32)
        rsv = sb("rsv", [128, 128], F32)
        t1 = sb("t1", [128, 128], F32)
        pst = ps("pst", [128, 8, 128], BF16)
        pp = [ps("pp", [128, 512], F32) for _ in range(2)]
        pA = ps("pA", [128, 128], F32)
        pS = ps("pS", [128, 1024], F32)
        po = ps("po", [128, 128], F32)
        pss = ps("pss", [128, 128], F32)

        for c in range(8):
            kb.dma("pool", lambda e, c=c: e.dma_start(out=w_in[:, c, :], in_=W["hgrn_w_in"][j, c * 128:(c + 1) * 128, :]), writes=("w_in",))
            kb.dma("pool", lambda e, c=c: e.dma_start(out=w_out[:, c, :], in_=W["hgrn_w_out"][j, c * 128:(c + 1) * 128, :]), writes=("w_out",))
        kb.dma("pool", lambda e: e.dma_start(out=ones[:], in_=W["c_ones"][:, :]), writes=("ones",))
        kb.dma("sp", lambda e: e.dma_start(out=gb[:], in_=W["norm_mix"][li].partition_broadcast(128)), writes=("gb",))
        kb.dma("sp", lambda e: e.dma_start(out=gn[:], in_=W["hgrn_gnorm"][j].rearrange("(p o) -> p o", o=1)), writes=("gn",))
        kb.dma("sp", lambda e: e.dma_start(out=lbz[:], in_=W["hgrn_lbT"][:, :, :]), writes=("lbz",))
        kb.dma("sp", lambda e: e.dma_start(out=rmask[:], in_=W["c_rmask"][:, :]), writes=("rmask",))
        kb.dma("sp", lambda e: e.dma_start(out=maskT[:], in_=W["c_maskT16"][:, :]), writes=("maskT",))
        kb.dma("sp", lambda e: e.dma_start(out=cmask[:], in_=W["c_cmask"][:, :]), writes=("cmask",))
        kb.op("dve", lambda e: e.memset(epst[:], EPS), writes=("epst",))
        kb.op("dve", lambda e: e.memset(a9[:], 0.0), writes=("a9",))
        for hh in range(8):
            kb.op("dve", lambda e, hh=hh: e.memset(carry[hh][:], 0.0), writes=(("carry", hh),))
        kb.op("act", lambda e: e.activation(out=lbe[:], in_=lbz[:], func=AF.Exp), reads=("lbz",), writes=("lbe",))
        kb.op("dve", lambda e: e.tensor_tensor(out=den[:], in0=lbe[:, 0, :], in1=lbe[:, 1, :], op=ALU.add), reads=("lbe",), writes=("den",))
        kb.op("dve", lambda e: e.tensor_tensor(out=den[:], in0=den[:], in1=lbe[:, 2, :], op=ALU.add), reads=("lbe", "den"), writes=("den",))
        kb.op("dve", lambda e: e.tensor_tensor(out=den[:], in0=den[:], in1=lbe[:, 3, :], op=ALU.add), reads=("lbe", "den"), writes=("den",))
        kb.op("dve", lambda e: e.memset(num[:], 0.0), writes=("num",))
        for l in range(1, li + 1):
            kb.op("dve", lambda e, l=l: e.tensor_tensor(out=num[:], in0=num[:], in1=lbe[:, l, :], op=ALU.add), reads=("lbe", "num"), writes=("num",))
        kb.op("dve", lambda e: e.reciprocal(out=den[:], in_=den[:]), reads=("den",), writes=("den",))
        kb.op("dve", lambda e: e.tensor_tensor(out=lb[:], in0=num[:], in1=den[:], op=ALU.mult), reads=("num", "den"), writes=("lb",))
        kb.op("dve", lambda e: e.tensor_scalar(out=oml[:], in0=lb[:], scalar1=-1.0, scalar2=1.0, op0=ALU.mult, op1=ALU.add), reads=("lb",), writes=("oml",))

        ppi = [0]

        def proj_fm(col0):
            b = ppi[0] % 2
            ppi[0] += 1
            for c in range(8):
                kb.op("pe", lambda e, c=c: e.matmul(pp[b][:, 0:NTOK], w_in[:, c, col0:col0 + 128], hnT[:, c, :], start=(c == 0), stop=(c == 7)),
                      reads=("w_in", "hnT"), writes=(("pp", b),))
            return pp[b], ("pp", b)

        for g in range(NT // GT):
            for tt in range(GT):
                t = g * GT + tt
                load_norm_T(kb, cm, t, hsrc, hsrc_name, hb2[t % 2], ("hb", t % 2), gb, "gb", hn, hnb, scr, stat, pst,
                            hnT[:, :, tt * 128:(tt + 1) * 128], "hnT")
            for tt in range(GT):
                for cg in range(2):
                    b = ppi[0] % 2
                    ppi[0] += 1
                    for c in range(8):
                        kb.op("pe", lambda e, c=c: e.matmul(pp[b][:, :], hnT[:, c, tt * 128:(tt + 1) * 128],
                                                            w_in[:, c, 2048 + cg * 512:2048 + (cg + 1) * 512], start=(c == 0), stop=(c == 7)),
                              reads=("w_in", "hnT"), writes=(("pp", b),))
                    kb.op("act", lambda e: e.activation(out=v_tok[:, tt, cg * 512:(cg + 1) * 512], in_=pp[b][:, :], func=AF.Copy),
                          reads=(("pp", b),), writes=("v_tok",))
            for hh in range(8):
                p, pk = proj_fm(1024 + hh * 128)
                kb.op("act", lambda e: e.activation(out=fs[:], in_=p[:, 0:NTOK], func=AF.Sigmoid), reads=(pk,), writes=("fs",))
                kb.op("dve", lambda e: e.tensor_scalar(out=fT[:], in0=fs[:], scalar1=oml[:, hh:hh + 1], scalar2=lb[:, hh:hh + 1],
                                                       op0=ALU.mult, op1=ALU.add), reads=("fs", "oml", "lb"), writes=("fT",))
                kb.op("act", lambda e: e.activation(out=lf[:], in_=fT[:], func=AF.Ln), reads=("fT",), writes=("lf",))
                kb.op("pool", lambda e: e.tensor_scalar(out=kk[:], in0=fT[:], scalar1=-1.0, scalar2=1.0, op0=ALU.mult, op1=ALU.add),
                      reads=("fT",), writes=("kk",))
                kb.op("dve", lambda e: e.tensor_tensor_scan(out=bT[:], data0=rmask[:, 0:NTOK], data1=lf[:], initial=0.0,
                                                            op0=ALU.mult, op1=ALU.add), reads=("rmask", "lf"), writes=("bT",))
                b3 = bT[:].rearrange("p (c k) -> p c k", k=16)
                kb.op("dve", lambda e: e.tensor_tensor(out=dT[:].rearrange("p (c k) -> p c k", k=16),
                                                       in0=b3[:, :, 15:16].to_broadcast([128, NCH, 16]), in1=b3, op=ALU.subtract),
                      reads=("bT",), writes=("dT",))
                kb.op("act", lambda e: e.activation(out=eb[:], in_=bT[:], func=AF.Exp), reads=("bT",), writes=("eb",))
                kb.op("act", lambda e: e.activation(out=enb[:], in_=bT[:], func=AF.Exp, scale=-1.0), reads=("bT",), writes=("enb",))
                kb.op("act", lambda e: e.activation(out=ed[:], in_=dT[:], func=AF.Exp), reads=("dT",), writes=("ed",))
                kb.op("pool", lambda e: e.tensor_tensor(out=kinv[:], in0=kk[:], in1=enb[:], op=ALU.mult), reads=("kk", "enb"), writes=("kinv",))
                kb.op("pool", lambda e: e.tensor_tensor(out=kdT[:], in0=kk[:], in1=ed[:], op=ALU.mult), reads=("kk", "ed"), writes=("kdT",))
                p, pk = proj_fm(hh * 128)
                kb.op("dve", lambda e: e.scalar_tensor_tensor(out=qd[:], in0=p[:, 0:NTOK], scalar=SCALE, in1=eb[:], op0=ALU.mult, op1=ALU.mult),
                      reads=(pk, "eb"), writes=("qd",))
                p, pk = proj_fm(3072 + hh * 128)
                kb.op("act", lambda e: e.activation(out=sg[:], in_=p[:, 0:NTOK], func=AF.Silu), reads=(pk,), writes=("sg",))
                for tt in range(GT):
                    kb.op("pe", lambda e, tt=tt: e.transpose(out=pst[:, tt, :], in_=kdT[:, tt * 128:(tt + 1) * 128], identity=cm.ident[:]),
                          reads=("kdT", "ident"), writes=("pst",))
                kb.op("act", lambda e: e.activation(out=kd_tok[:, :, :], in_=pst[:, 0:GT, :], func=AF.Copy), reads=("pst",), writes=("kd_tok",))
                for tt in range(GT):
                    tsl = slice(tt * 128, (tt + 1) * 128)
                    vh = v_tok[:, tt, hh * 128:(hh + 1) * 128]
                    kb.op("pe", lambda e: e.matmul(pA[:, :], kinv[:, tsl], qd[:, tsl], start=True, stop=True), reads=("kinv", "qd"), writes=("pA",))
                    kb.op("dve", lambda e: e.tensor_tensor(out=AT[:], in0=pA[:, :], in1=maskT[:], op=ALU.mult), reads=("pA", "maskT"), writes=("AT",))
                    kb.op("pool", lambda e: e.tensor_tensor(out=Vblk[:], in0=vh.unsqueeze(1).to_broadcast([128, 8, 128]),
                                                            in1=cmask[:, :].unsqueeze(2).to_broadcast([128, 8, 128]), op=ALU.mult),
                          reads=("v_tok", "cmask"), writes=("Vblk",))
                    for half in range(2):
                        kb.op("pe", lambda e, half=half: e.matmul(pS[:, half * 512:(half + 1) * 512], kd_tok[:, tt, :],
                                                                  Vblk[:, half * 4:(half + 1) * 4, :].rearrange("p c v -> p (c v)"), start=True, stop=True),
                              reads=("kd_tok", "Vblk"), writes=("pS",))
                    kb.op("act", lambda e: e.activation(out=Sd[:, :, 1:9], in_=pS[:, :].rearrange("k (c v) -> k v c", v=128), func=AF.Copy),
                          reads=("pS",), writes=("Sd",))
                    kb.op("pool", lambda e: e.tensor_copy(Sd[:, :, 0], carry[hh][:]), reads=(("carry", hh), "Sd"), writes=("Sd",))
                    ebv = eb[:].rearrange("p (c k) -> p c k", k=16)
                    kb.op("act", lambda e: e.activation(out=a9[:, :, 1:9], in_=ebv[:, tt * 8:(tt + 1) * 8, 15].unsqueeze(1).to_broadcast([128, 128, 8]), func=AF.Copy), reads=("eb",), writes=("a9",))
                    kb.op("dve", lambda e: e.tensor_tensor_scan(out=So[:].rearrange("k v c -> k (v c)"), data0=a9[:].rearrange("k v c -> k (v c)"),
                                                                data1=Sd[:].rearrange("k v c -> k (v c)"), initial=0.0, op0=ALU.mult, op1=ALU.add),
                          reads=("a9", "Sd"), writes=("So",))
                    kb.op("act", lambda e: e.activation(out=carry[hh][:], in_=So[:, :, 8], func=AF.Copy), reads=("So",), writes=(("carry", hh),))
                    kb.op("pool", lambda e: e.tensor_copy(Sbf[:].rearrange("k c v -> k v c"), So[:, :, 0:8]), reads=("So",), writes=("Sbf",))
                    for c in range(8):
                        csl = slice(16 * c, 16 * c + 16)
                        kb.op("pe", lambda e, c=c: e.matmul(po[:, csl], Sbf[:, c, :], qd[:, tt * 128 + 16 * c:tt * 128 + 16 * c + 16], start=True, stop=False),
                              reads=("Sbf", "qd"), writes=("po",))
                        kb.op("pe", lambda e, c=c: e.matmul(po[:, csl], vh, AT[:, csl], start=False, stop=True),
                              reads=("v_tok", "AT"), writes=("po",))
                    kb.op("act", lambda e: e.activation(out=osq[:], in_=po[:, :], func=AF.Square), reads=("po",), writes=("osq",))
                    kb.op("pe", lambda e: e.matmul(pss[:, :], ones[:], osq[:], start=True, stop=True), reads=("ones", "osq"), writes=("pss",))
                    kb.op("act", lambda e: e.activation(out=sdv[:], in_=pss[:, :], func=AF.Sqrt, bias=epst[:, 0:1], scale=1.0 / 128.0),
                          reads=("pss", "epst"), writes=("sdv",))
                    kb.op("dve", lambda e: e.reciprocal(out=rsv[:], in_=sdv[:]), reads=("sdv",), writes=("rsv",))
                    kb.op("dve", lambda e: e.tensor_tensor(out=t1[:], in0=po[:, :], in1=rsv[:], op=ALU.mult), reads=("po", "rsv"), writes=("t1",))
                    kb.op("dve", lambda e: e.scalar_tensor_tensor(out=yT[:, hh, tsl], in0=t1[:], scalar=gn[:, 0:1], in1=sg[:, tsl],
                                                                  op0=ALU.mult, op1=ALU.mult), reads=("t1", "gn", "sg"), writes=("yT",))
            for tt in range(GT):
                t = g * GT + tt
                hb = hb2[t % 2]
                hk = ("hb", t % 2)
                for cg in range(2):
                    for hh in range(8):
                        kb.op("pe", lambda e, hh=hh: e.matmul(pS[:, cg * 512:(cg + 1) * 512], yT[:, hh, tt * 128:(tt + 1) * 128],
                                                              w_out[:, hh, cg * 512:(cg + 1) * 512], start=(hh == 0), stop=(hh == 7)),
                              reads=("yT", "w_out"), writes=("pS",))
                kb.dma("sp", lambda e: e.dma_start(out=hb[:], in_=hsrc[t * 128:(t + 1) * 128, :]), reads=((hsrc_name, t),), writes=(hk,))
                for cg in range(2):
                    kb.op("dve", lambda e: e.tensor_tensor(out=hb[:, cg * 512:(cg + 1) * 512], in0=pS[:, cg * 512:(cg + 1) * 512],
                                                           in1=hb[:, cg * 512:(cg + 1) * 512], op=ALU.add), reads=("pS", hk), writes=(hk,))
                kb.dma("sp", lambda e: e.dma_start(out=hdst[t * 128:(t + 1) * 128, :], in_=hb[:]), reads=(hk,), writes=((hdst_name, t),))


        kb.barrier()
def fox_phase(kb, cm, li, hsrc, hsrc_name, hdst, hdst_name, W, scr_d):
    NT = cm.NT
    T = cm.T
    GT = min(4, NT)
    NTOK = GT * 128
    qT_d, kT_d, v_d, ca_d, o_d = scr_d["qT"], scr_d["kT"], scr_d["v"], scr_d["ca"], scr_d["o"]
    with contextlib.ExitStack() as es:
        sb = lambda n, s, d: kb.sb(n, s, d, es)
        ps = lambda n, s, d: kb.ps(n, s, d, es)
        w_in = sb("fw_in", [128, 8, 3072], BF16)
        w_f = sb("fw_f", [128, 8, 16], BF16)
        gb = sb("gb", [128, D], F32)
        bfb = sb("bfb", [128, 16], F32)
        tri = sb("tri", [128, 128], F32)
        onesf = sb("onesf", [128, 128], F32)
        hb2 = [sb("hb", [128, D], F32) for _ in range(2)]
        hn = sb("hn", [128, D], F32)
        hnb = sb("hnb", [128, D], BF16)
        scr = sb("scr", [128, D], F32)
        stat = sb("stat", [128, 4], F32)
        hnT = sb("hnT", [128, 8, NTOK], BF16)
        ob = [sb("ob", [128, NTOK], BF16) for _ in range(2)]
        vb = [sb("vb", [128, 1024], BF16) for _ in range(2)]
        fz = sb("fz", [128, 16], F32)
        fe = sb("fe", [128, 16], F32)
        lf = sb("lf", [128, 16], F32)
        negc = sb("negc", [128, 16], F32)
        carry_b = sb("carry_b", [128, 16], F32)
        ctok = sb("ctok", [128, 16], F32)
        negct = sb("negct", [128, 16], F32)
        identf = sb("identf", [128, 128], F32)
        cT = sb("cT", [16, NTOK], F32)
        hi = sb("hi", [16, NTOK], BF16)
        hi32 = sb("hi32", [16, NTOK], F32)
        r1 = sb("r1", [16, NTOK], F32)
        mid = sb("mid", [16, NTOK], BF16)
        mid32 = sb("mid32", [16, NTOK], F32)
        r2 = sb("r2", [16, NTOK], F32)
        lo = sb("lo", [16, NTOK], BF16)
        pst = ps("pst", [128, 8, 128], BF16)
        pp = [ps("pp", [128, 512], F32) for _ in range(2)]
        pf = ps("pf", [128, 16], F32)
        pc = ps("pc", [16, NTOK], F32)
        pct = ps("pct", [128, 16], F32)
        ptot = ps("ptot", [128, 16], F32)

        for c in range(8):
            kb.dma("pool", lambda e, c=c: e.dma_start(out=w_in[:, c, :], in_=W["fox_w_in"][0, c * 128:(c + 1) * 128, 0:3072]), writes=("fw_in",))
            kb.dma("pool", lambda e, c=c: e.dma_start(out=w_f[:, c, :], in_=W["fox_w_in"][0, c * 128:(c + 1) * 128, 3072:3088]), writes=("fw_f",))
        kb.dma("sp", lambda e: e.dma_start(out=gb[:], in_=W["norm_mix"][li].partition_broadcast(128)), writes=("gb",))
        kb.dma("sp", lambda e: e.dma_start(out=bfb[:], in_=W["fox_b_f"][0].partition_broadcast(128)), writes=("bfb",))
        kb.dma("sp", lambda e: e.dma_start(out=tri[:], in_=W["c_tri"][:, :]), writes=("tri",))
        kb.dma("sp", lambda e: e.dma_start(out=onesf[:], in_=W["c_ones"][:, :]), writes=("onesf",))
        kb.op("dve", lambda e: e.memset(carry_b[:], 0.0), writes=("carry_b",))
        kb.dma("sp", lambda e: e.dma_start(out=identf[:], in_=cm.ident_d[:, :]), writes=("identf",))
        ppi = [0]
        for g in range(NT // GT):
            for tt in range(GT):
                t = g * GT + tt
                load_norm_T(kb, cm, t, hsrc, hsrc_name, hb2[t % 2], ("hb", t % 2), gb, "gb", hn, hnb, scr, stat, pst,
                            hnT[:, :, tt * 128:(tt + 1) * 128], "hnT")
            for which, dst in ((0, qT_d), (1, kT_d)):
                for ch in range(8):
                    b = ppi[0] % 2
                    ppi[0] += 1
                    col0 = which * 1024 + ch * 128
                    for c in range(8):
                        kb.op("pe", lambda e, c=c: e.matmul(pp[b][:, 0:NTOK], w_in[:, c, col0:col0 + 128], hnT[:, c, :], start=(c == 0), stop=(c == 7)),
                              reads=("fw_in", "hnT"), writes=(("pp", b),))
                    kb.op("act", lambda e: e.activation(out=ob[b][:], in_=pp[b][:, 0:NTOK], func=AF.Copy, scale=(0.125 if which == 0 else 1.0)),
                          reads=(("pp", b),), writes=(("ob", b),))
                    kb.dma("sp", lambda e: e.dma_start(out=dst[ch * 128:(ch + 1) * 128, g * NTOK:(g + 1) * NTOK], in_=ob[b][:]),
                           reads=(("ob", b),), writes=(("qk_d", which, ch, g),))
            for tt in range(GT):
                t = g * GT + tt
                vbb = vb[t % 2]
                for cg in range(2):
                    b = ppi[0] % 2
                    ppi[0] += 1
                    for c in range(8):
                        kb.op("pe", lambda e, c=c: e.matmul(pp[b][:, :], hnT[:, c, tt * 128:(tt + 1) * 128],
                                                            w_in[:, c, 2048 + cg * 512:2048 + (cg + 1) * 512], start=(c == 0), stop=(c == 7)),
                              reads=("fw_in", "hnT"), writes=(("pp", b),))
                    kb.op("act", lambda e: e.activation(out=vbb[:, cg * 512:(cg + 1) * 512], in_=pp[b][:, :], func=AF.Copy),
                          reads=(("pp", b),), writes=(("vb", t % 2),))
                kb.dma("sp", lambda e: e.dma_start(out=v_d[t * 128:(t + 1) * 128, :], in_=vbb[:]), reads=(("vb", t % 2),), writes=(("v_d", t),))
                for c in range(8):
                    kb.op("pe", lambda e, c=c: e.matmul(pf[:, :], hnT[:, c, tt * 128:(tt + 1) * 128], w_f[:, c, :], start=(c == 0), stop=(c == 7)),
                          reads=("fw_f", "hnT"), writes=("pf",))
                kb.op("dve", lambda e: e.tensor_tensor(out=fz[:], in0=pf[:, :], in1=bfb[:], op=ALU.add), reads=("pf", "bfb"), writes=("fz",))
                kb.op("act", lambda e: e.activation(out=fe[:], in_=fz[:], func=AF.Exp, scale=-1.0), reads=("fz",), writes=("fe",))
                kb.op("dve", lambda e: e.tensor_scalar(out=fe[:], in0=fe[:], scalar1=1.0, scalar2=None, op0=ALU.add), reads=("fe",), writes=("fe",))
                kb.op("act", lambda e: e.activation(out=lf[:], in_=fe[:], func=AF.Ln), reads=("fe",), writes=("lf",))
                kb.op("dve", lambda e: e.tensor_scalar(out=lf[:], in0=lf[:], scalar1=-1.0, scalar2=None, op0=ALU.mult), reads=("lf",), writes=("lf",))
                kb.op("pe", lambda e: e.matmul(pct[:, :], tri[:], lf[:], start=True, stop=True), reads=("lf", "tri"), writes=("pct",))
                kb.op("dve", lambda e: e.tensor_tensor(out=ctok[:], in0=pct[:, :], in1=carry_b[:], op=ALU.add), reads=("pct", "carry_b"), writes=("ctok",))
                kb.op("pe", lambda e: e.matmul(ptot[:, :], onesf[:], lf[:], start=True, stop=True), reads=("lf", "onesf"), writes=("ptot",))
                kb.op("dve", lambda e: e.tensor_tensor(out=carry_b[:], in0=carry_b[:], in1=ptot[:, :], op=ALU.add), reads=("ptot", "carry_b"), writes=("carry_b",))
                kb.op("act", lambda e: e.activation(out=negct[:], in_=ctok[:], func=AF.Copy, scale=-1.0), reads=("ctok",), writes=("negct",))
                kb.dma("sp", lambda e: e.dma_start(out=scr_d["negc"][t * 128:(t + 1) * 128, :], in_=negct[:]), reads=("negct",), writes=(("negc_d", t),))
                kb.op("pe", lambda e: e.transpose(out=pc[:, tt * 128:(tt + 1) * 128], in_=ctok[:], identity=identf[:]), reads=("ctok", "identf"), writes=("pc",))
            kb.op("act", lambda e: e.activation(out=cT[:], in_=pc[:, :], func=AF.Copy), reads=("pc",), writes=("cT",))
            kb.op("dve", lambda e: e.tensor_copy(hi[:], cT[:]), reads=("cT",), writes=("hi",))
            kb.op("dve", lambda e: e.tensor_copy(hi32[:], hi[:]), reads=("hi",), writes=("hi32",))
            kb.op("dve", lambda e: e.tensor_tensor(out=r1[:], in0=cT[:], in1=hi32[:], op=ALU.subtract), reads=("cT", "hi32"), writes=("r1",))
            kb.op("dve", lambda e: e.tensor_copy(mid[:], r1[:]), reads=("r1",), writes=("mid",))
            kb.op("dve", lambda e: e.tensor_copy(mid32[:], mid[:]), reads=("mid",), writes=("mid32",))
            kb.op("dve", lambda e: e.tensor_tensor(out=r2[:], in0=r1[:], in1=mid32[:], op=ALU.subtract), reads=("r1", "mid32"), writes=("r2",))
            kb.op("dve", lambda e: e.tensor_copy(lo[:], r2[:]), reads=("r2",), writes=("lo",))
            for k3, src_t, sk in ((0, hi, "hi"), (1, mid, "mid"), (2, lo, "lo")):
                kb.dma("sp", lambda e: e.dma_start(out=ca_d[:, k3, g * NTOK:(g + 1) * NTOK], in_=src_t[:]), reads=(sk,), writes=(("ca_d", g),))

        kb.barrier()
    allqk = tuple(("qk_d", w, ch, g) for w in range(2) for ch in range(8) for g in range(NT // GT))
    allv = tuple(("v_d", t) for t in range(NT))
    allca = tuple(("ca_d", g) for g in range(NT // GT))
    allnegc = tuple(("negc_d", t) for t in range(NT))
    with contextlib.ExitStack() as es:
        sb = lambda n, s, d: kb.sb(n, s, d, es)
        ps = lambda n, s, d: kb.ps(n, s, d, es)
        q_aug = [sb("q_aug", [67, T], BF16) for _ in range(2)]
        k_aug = [sb("k_aug", [67, T], BF16) for _ in range(2)]
        V_aug = [sb("V_aug", [128, NT, 65], BF16) for _ in range(2)]
        negc = sb("negc", [128, NT, 16], F32)
        identb = cm.ident
        nmask = sb("nmask", [128, 128], BF16)
        PT = [sb("PT", [128, 512], BF16) for _ in range(2)]
        rcp = sb("rcp", [128, 1], F32)
        otk = [sb("otk", [128, 64], BF16) for _ in range(2)]
        pS = [ps("pS", [128, 512], F32) for _ in range(2)]
        pO = [ps("pO", [128, 65], F32) for _ in range(4)]
        kb.dma("pool", lambda e: e.dma_start(out=nmask[:], in_=W["c_negmask"][:, :]), writes=("nmask",))
        kb.dma("sp", lambda e: e.dma_start(out=negc[:], in_=scr_d["negc"][:, :].rearrange("(t p) h -> p t h", p=128)), reads=allnegc, writes=("negc",))
        si = [0]
        for hh in range(16):
            hb_ = hh % 2
            qa, ka, va = q_aug[hb_], k_aug[hb_], V_aug[hb_]
            qk_, kk_, vk_ = ("q_aug", hb_), ("k_aug", hb_), ("V_aug", hb_)
            kb.dma("sp", lambda e: e.dma_start(out=qa[0:64, :], in_=qT_d[hh * 64:(hh + 1) * 64, :]), reads=allqk, writes=(qk_,))
            kb.dma("sp", lambda e: e.dma_start(out=qa[64:67, :], in_=ca_d[hh, :, :]), reads=allca, writes=(qk_,))
            kb.dma("sp", lambda e: e.dma_start(out=ka[0:64, :], in_=kT_d[hh * 64:(hh + 1) * 64, :]), reads=allqk, writes=(kk_,))
            kb.dma("pool", lambda e: e.dma_start(out=ka[64:67, :], in_=W["c_ones3"][:, 0:T]), writes=(kk_,))
            kb.dma("sp", lambda e: e.dma_start(out=va[:, :, 0:64], in_=v_d[:, hh * 64:(hh + 1) * 64].rearrange("(t p) d -> p t d", p=128)),
                   reads=allv, writes=(vk_,))
            kb.dma("pool", lambda e: e.dma_start(out=va[:, :, 64:65], in_=W["c_ones3"][0:1, 0:NT * 128].rearrange("o (t p) -> p t o", p=128),
                                                 allow_slow_non_contiguous=True), writes=(vk_,))
            for i in range(NT // GT):
                nj = GT * i + GT
                for j in range(nj):
                    r = max(0, j - GT * i)
                    b = si[0] % 2
                    si[0] += 1
                    lhs = ka[0:67, j * 128:(j + 1) * 128]
                    c0 = i * NTOK
                    if j >= GT * i:
                        kb.op("pe", lambda e: e.matmul(pS[b][:, r * 128:(r + 1) * 128], lhs, qa[0:67, c0 + r * 128:c0 + (r + 1) * 128], start=True, stop=False),
                              reads=(qk_, kk_), writes=(("pS", b),))
                        kb.op("pe", lambda e: e.matmul(pS[b][:, r * 128:(r + 1) * 128], identb[:], nmask[:], start=False, stop=True),
                              reads=("ident", "nmask"), writes=(("pS", b),))
                        if r < GT - 1:
                            kb.op("pe", lambda e: e.matmul(pS[b][:, (r + 1) * 128:NTOK], lhs, qa[0:67, c0 + (r + 1) * 128:c0 + NTOK], start=True, stop=True),
                                  reads=(qk_, kk_), writes=(("pS", b),))
                    else:
                        kb.op("pe", lambda e: e.matmul(pS[b][:, 0:NTOK], lhs, qa[0:67, c0:c0 + NTOK], start=True, stop=True),
                              reads=(qk_, kk_), writes=(("pS", b),))
                    kb.op("act", lambda e: e.activation(out=PT[b][:, r * 128:NTOK], in_=pS[b][:, r * 128:NTOK], func=AF.Exp,
                                                        bias=negc[:, j, hh:hh + 1], scale=1.0), reads=(("pS", b), "negc"), writes=(("PT", b),))
                    for rr in range(r, GT):
                        kb.op("pe", lambda e, rr=rr: e.matmul(pO[rr][:, :], PT[b][:, rr * 128:(rr + 1) * 128], va[:, j, :], start=(j == 0), stop=(j == GT * i + rr)),
                              reads=(("PT", b), vk_), writes=(("pO", rr),))
                for rr in range(GT):
                    t = GT * i + rr
                    ob_ = otk[t % 2]
                    kb.op("dve", lambda e: e.reciprocal(out=rcp[:], in_=pO[rr][:, 64:65]), reads=(("pO", rr),), writes=("rcp",))
                    kb.op("dve", lambda e: e.tensor_scalar(out=ob_[:], in0=pO[rr][:, 0:64], scalar1=rcp[:, 0:1], scalar2=None, op0=ALU.mult),
                          reads=(("pO", rr), "rcp"), writes=(("otk", t % 2),))
                    kb.dma("sp", lambda e: e.dma_start(out=o_d[t * 128:(t + 1) * 128, hh * 64:(hh + 1) * 64], in_=ob_[:]),
                           reads=(("otk", t % 2),), writes=(("o_d", t, hh),))

        kb.barrier()
    with contextlib.ExitStack() as es:
        sb = lambda n, s, d: kb.sb(n, s, d, es)
        ps = lambda n, s, d: kb.ps(n, s, d, es)
        w_out = sb("fw_out", [128, 8, 1024], BF16)
        hb2 = [sb("hb", [128, D], F32) for _ in range(2)]
        o_t = [sb("o_t", [128, D], BF16) for _ in range(2)]
        oT = sb("oT", [128, 8, 128], BF16)
        pst = ps("pst", [128, 8, 128], BF16)
        pm = ps("pm", [128, 1024], F32)
        for c in range(8):
            kb.dma("pool", lambda e, c=c: e.dma_start(out=w_out[:, c, :], in_=W["fox_w_out"][0, c * 128:(c + 1) * 128, :]), writes=("fw_out",))
        for t in range(NT):
            b = t % 2
            kb.dma("sp", lambda e: e.dma_start(out=o_t[b][:], in_=o_d[t * 128:(t + 1) * 128, :]),
                   reads=tuple(("o_d", t, hh) for hh in range(16)), writes=(("o_t", b),))
            kb.dma("sp", lambda e: e.dma_start(out=hb2[b][:], in_=hsrc[t * 128:(t + 1) * 128, :]), reads=((hsrc_name, t),), writes=(("hb", b),))
            transpose_tile(kb, cm, o_t[b], ("o_t", b), oT[:, :, :], "oT", pst, "pst")
            for cg in range(2):
                for c in range(8):
                    kb.op("pe", lambda e, c=c: e.matmul(pm[:, cg * 512:(cg + 1) * 512], oT[:, c, :], w_out[:, c, cg * 512:(cg + 1) * 512], start=(c == 0), stop=(c == 7)),
                          reads=("oT", "fw_out"), writes=("pm",))
                kb.op("dve", lambda e: e.tensor_tensor(out=hb2[b][:, cg * 512:(cg + 1) * 512], in0=pm[:, cg * 512:(cg + 1) * 512],
                                                       in1=hb2[b][:, cg * 512:(cg + 1) * 512], op=ALU.add), reads=("pm", ("hb", b)), writes=(("hb", b),))
            kb.dma("sp", lambda e: e.dma_start(out=hdst[t * 128:(t + 1) * 128, :], in_=hb2[b][:]), reads=(("hb", b),), writes=((hdst_name, t),))


        kb.barrier()
def ssd_phase(kb, cm, li, hsrc, hsrc_name, hdst, hdst_name, W, y_d):
    NT = cm.NT
    GT = min(2, NT)
    NTOK = GT * 128
    with contextlib.ExitStack() as es:
        sb = lambda n, s, d: kb.sb(n, s, d, es)
        ps = lambda n, s, d: kb.ps(n, s, d, es)
        w_x = sb("w_x", [128, 8, 4096], BF16)
        w_dt = sb("w_dt", [128, 8, 32], BF16)
        gb = sb("gb", [128, D], F32)
        cw = sb("cw", [128, 32, 4], F32)
        cbias = sb("cbias", [128, 32], F32)
        dtb = sb("dtb", [128, 32], F32)
        aneg = sb("aneg", [128, 32], F32)
        Db = sb("Db", [128, 32], F32)
        tri = sb("tri", [128, 128], F32)
        onesf = sb("onesf", [128, 128], F32)
        identf = sb("identf", [128, 128], F32)
        nmaskf = sb("nmaskf", [128, 128], F32)
        cmaskT = sb("cmaskT", [128, 128], F32)
        hb2 = [sb("hb", [128, D], F32) for _ in range(2)]
        hn = sb("hn", [128, D], F32)
        hnb = sb("hnb", [128, D], BF16)
        scr = sb("scr", [128, D], F32)
        stat = sb("stat", [128, 4], F32)
        hnT = sb("hnT", [128, 8, NTOK], BF16)
        xp = [sb("xp", [128, NTOK + 3], F32) for _ in range(2)]
        acc = [sb("acc", [128, NTOK], F32) for _ in range(2)]
        halo = sb("halo", [128, 32, 3], F32)
        xc = sb("xc", [128, 32, NTOK], BF16)
        x_tok = sb("x_tok", [128, GT, 2048], BF16)
        B_tok = sb("B_tok", [128, GT, 1024], BF16)
        xb = sb("xb", [128, 32], F32)
        dtt = sb("dtt", [128, 32], F32)
        dA = sb("dA", [128, 32], F32)
        cum = sb("cum", [128, 32], F32)
        negcum = sb("negcum", [128, 32], F32)
        dd = sb("dd", [128, 32], F32)
        dec_end = sb("dec_end", [128, 32], F32)
        ecum = sb("ecum", [128, 32], F32)
        etot = sb("etot", [128, 32], F32)
        xdt = sb("xdt", [128, 2048], BF16)
        xdd = sb("xdd", [128, 2048], BF16)
        cbm = sb("cbm", [128, 128], F32)
        tsc = sb("tsc", [128, 128], F32)
        LT = sb("LT", [128, 128], F32)
        MT = sb("MT", [128, 128], BF16)
        S = sb("S", [128, 32, 64], F32)
        S_bf = sb("S_bf", [128, 32, 64], BF16)
        yi = sb("yi", [128, 256], F32)
        y_sb = sb("y_sb", [128, 2048], F32)
        tmp = sb("tmp", [128, 2048], F32)
        pst = ps("pst", [128, 8, 128], BF16)
        pp = [ps("pp", [128, 512], F32) for _ in range(2)]
        pdc = ps("pdc", [128, 96], F32)
        pcb = ps("pcb", [128, 128], F32)
        pcr = ps("pcr", [128, 128], F32)
        py = ps("py", [128, 512], F32)
        pSu = ps("pSu", [128, 256], F32)

        for c in range(8):
            kb.dma("pool", lambda e, c=c: e.dma_start(out=w_x[:, c, :], in_=W["ssm_w_in"][0, c * 128:(c + 1) * 128, 2048:6144]), writes=("w_x",))
            kb.dma("pool", lambda e, c=c: e.dma_start(out=w_dt[:, c, :], in_=W["ssm_w_in"][0, c * 128:(c + 1) * 128, 6144:6176]), writes=("w_dt",))
        kb.dma("sp", lambda e: e.dma_start(out=gb[:], in_=W["norm_mix"][li].partition_broadcast(128)), writes=("gb",))
        kb.dma("sp", lambda e: e.dma_start(out=cw[:], in_=W["ssm_conv_wT"][:, :, :]), writes=("cw",))
        kb.dma("sp", lambda e: e.dma_start(out=cbias[:], in_=W["ssm_conv_bT"][:, :]), writes=("cbias",))
        kb.dma("sp", lambda e: e.dma_start(out=dtb[:], in_=W["ssm_dt_bias"][0].partition_broadcast(128)), writes=("dtb",))
        kb.dma("sp", lambda e: e.dma_start(out=aneg[:], in_=W["ssm_a_log"][0].partition_broadcast(128)), writes=("aneg",))
        kb.dma("sp", lambda e: e.dma_start(out=Db[:], in_=W["ssm_d"][0].partition_broadcast(128)), writes=("Db",))
        kb.dma("sp", lambda e: e.dma_start(out=tri[:], in_=W["c_tri"][:, :]), writes=("tri",))
        kb.dma("sp", lambda e: e.dma_start(out=cmaskT[:], in_=W["c_tri"][:, :]), writes=("cmaskT",))
        kb.dma("sp", lambda e: e.dma_start(out=onesf[:], in_=W["c_ones"][:, :]), writes=("onesf",))
        kb.dma("sp", lambda e: e.dma_start(out=identf[:], in_=cm.ident_d[:, :]), writes=("identf",))
        kb.dma("sp", lambda e: e.dma_start(out=nmaskf[:], in_=W["c_negmask"][:, :]), writes=("nmaskf",))
        kb.op("act", lambda e: e.activation(out=aneg[:], in_=aneg[:], func=AF.Exp), reads=("aneg",), writes=("aneg",))
        kb.op("dve", lambda e: e.tensor_scalar(out=aneg[:], in0=aneg[:], scalar1=-1.0, scalar2=None, op0=ALU.mult), reads=("aneg",), writes=("aneg",))
        kb.op("dve", lambda e: e.memset(halo[:], 0.0), writes=("halo",))
        kb.op("dve", lambda e: e.memset(S[:], 0.0), writes=("S",))
        kb.op("dve", lambda e: e.memset(S_bf[:], 0.0), writes=("S_bf",))
        ppi = [0]
        for g2 in range(NT // GT):
            for tt in range(GT):
                t = g2 * GT + tt
                load_norm_T(kb, cm, t, hsrc, hsrc_name, hb2[t % 2], ("hb", t % 2), gb, "gb", hn, hnb, scr, stat, pst,
                            hnT[:, :, tt * 128:(tt + 1) * 128], "hnT")
            for ch in range(32):
                b = ppi[0] % 2
                ppi[0] += 1
                for c in range(8):
                    kb.op("pe", lambda e, c=c: e.matmul(pp[b][:, 0:NTOK], w_x[:, c, ch * 128:(ch + 1) * 128], hnT[:, c, :], start=(c == 0), stop=(c == 7)),
                          reads=("w_x", "hnT"), writes=(("pp", b),))
                xk, ak = ("xp", b), ("acc", b)
                kb.op("act", lambda e: e.activation(out=xp[b][:, 3:3 + NTOK], in_=pp[b][:, 0:NTOK], func=AF.Copy), reads=(("pp", b),), writes=(xk,))
                kb.op("pool", lambda e: e.tensor_copy(xp[b][:, 0:3], halo[:, ch, :]), reads=("halo", xk), writes=(xk,))
                kb.op("dve", lambda e: e.tensor_scalar(out=acc[b][:], in0=xp[b][:, 0:NTOK], scalar1=cw[:, ch, 0:1], scalar2=cbias[:, ch:ch + 1],
                                                       op0=ALU.mult, op1=ALU.add), reads=(xk, "cw", "cbias"), writes=(ak,))
                for k in range(1, 4):
                    kb.op("dve", lambda e, k=k: e.scalar_tensor_tensor(out=acc[b][:], in0=xp[b][:, k:k + NTOK], scalar=cw[:, ch, k:k + 1], in1=acc[b][:],
                                                                       op0=ALU.mult, op1=ALU.add), reads=(xk, "cw", ak), writes=(ak,))
                kb.op("pool", lambda e: e.tensor_copy(halo[:, ch, :], xp[b][:, NTOK:NTOK + 3]), reads=(xk, "halo"), writes=("halo",))
                kb.op("act", lambda e: e.activation(out=xc[:, ch, :], in_=acc[b][:], func=AF.Silu), reads=(ak,), writes=(("xc", ch),))
            allxc = tuple(("xc", ch) for ch in range(32))
            for tt in range(GT):
                for blk in range(3):
                    for cc in range(8):
                        ch = blk * 8 + cc
                        kb.op("pe", lambda e, cc=cc, ch=ch: e.transpose(out=pst[:, cc, :], in_=xc[:, ch, tt * 128:(tt + 1) * 128], identity=cm.ident[:]),
                              reads=(("xc", ch), "ident"), writes=("pst",))
                    if blk < 2:
                        dst = x_tok[:, tt, blk * 1024:(blk + 1) * 1024].rearrange("p (c k) -> p c k", k=128)
                        kb.op("act", lambda e: e.activation(out=dst, in_=pst[:, :, :], func=AF.Copy), reads=("pst",), writes=("x_tok",))
                    else:
                        dst = B_tok[:, tt, :].rearrange("p (c k) -> p c k", k=128)
                        kb.op("act", lambda e: e.activation(out=dst, in_=pst[:, :, :], func=AF.Copy), reads=("pst",), writes=("B_tok",))
            for tt in range(GT):
                t = g2 * GT + tt
                tsl = slice(tt * 128, (tt + 1) * 128)
                for c in range(8):
                    kb.op("pe", lambda e, c=c: e.matmul(pdc[:, 0:32], hnT[:, c, tsl], w_dt[:, c, :], start=(c == 0), stop=(c == 7)),
                          reads=("w_dt", "hnT"), writes=("pdc",))
                kb.op("dve", lambda e: e.tensor_tensor(out=xb[:], in0=pdc[:, 0:32], in1=dtb[:], op=ALU.add), reads=("pdc", "dtb"), writes=("xb",))
                kb.op("act", lambda e: e.activation(out=xb[:], in_=xb[:], func=AF.Exp), reads=("xb",), writes=("xb",))
                kb.op("dve", lambda e: e.tensor_scalar(out=xb[:], in0=xb[:], scalar1=1.0, scalar2=None, op0=ALU.add), reads=("xb",), writes=("xb",))
                kb.op("act", lambda e: e.activation(out=dtt[:], in_=xb[:], func=AF.Ln), reads=("xb",), writes=("dtt",))
                kb.op("dve", lambda e: e.tensor_tensor(out=dA[:], in0=dtt[:], in1=aneg[:], op=ALU.mult), reads=("dtt", "aneg"), writes=("dA",))
                kb.op("pe", lambda e: e.matmul(pdc[:, 32:64], tri[:], dA[:], start=True, stop=True), reads=("tri", "dA"), writes=("pdc",))
                kb.op("pe", lambda e: e.matmul(pdc[:, 64:96], onesf[:], dA[:], start=True, stop=True), reads=("onesf", "dA"), writes=("pdc",))
                kb.op("act", lambda e: e.activation(out=cum[:], in_=pdc[:, 32:64], func=AF.Copy), reads=("pdc",), writes=("cum",))
                kb.op("act", lambda e: e.activation(out=negcum[:], in_=pdc[:, 32:64], func=AF.Copy, scale=-1.0), reads=("pdc",), writes=("negcum",))
                kb.op("act", lambda e: e.activation(out=ecum[:], in_=pdc[:, 32:64], func=AF.Exp), reads=("pdc",), writes=("ecum",))
                kb.op("act", lambda e: e.activation(out=etot[:], in_=pdc[:, 64:96], func=AF.Exp), reads=("pdc",), writes=("etot",))
                kb.op("dve", lambda e: e.tensor_tensor(out=dd[:], in0=pdc[:, 64:96], in1=cum[:], op=ALU.subtract), reads=("pdc", "cum"), writes=("dd",))
                kb.op("act", lambda e: e.activation(out=dec_end[:], in_=dd[:], func=AF.Exp), reads=("dd",), writes=("dec_end",))
                x3 = x_tok[:, tt, :].rearrange("p (h d) -> p h d", d=64)
                kb.op("dve", lambda e: e.tensor_tensor(out=xdt[:].rearrange("p (h d) -> p h d", d=64), in0=x3,
                                                       in1=dtt[:, :].unsqueeze(2).to_broadcast([128, 32, 64]), op=ALU.mult), reads=("x_tok", "dtt"), writes=("xdt",))
                kb.op("pool", lambda e: e.tensor_tensor(out=xdd[:].rearrange("p (h d) -> p h d", d=64), in0=xdt[:].rearrange("p (h d) -> p h d", d=64),
                                                        in1=dec_end[:, :].unsqueeze(2).to_broadcast([128, 32, 64]), op=ALU.mult), reads=("xdt", "dec_end"), writes=("xdd",))
                for g in range(8):
                    BT = xc[:, 16 + g, tsl]
                    CT = xc[:, 24 + g, tsl]
                    kb.op("pe", lambda e: e.matmul(pcb[:, :], BT, CT, start=True, stop=True), reads=allxc, writes=("pcb",))
                    kb.op("dve", lambda e: e.tensor_tensor(out=cbm[:], in0=pcb[:, :], in1=cmaskT[:], op=ALU.mult), reads=("pcb", "cmaskT"), writes=("cbm",))
                    for h4 in range(4):
                        h = 4 * g + h4
                        hs = slice(h * 64, (h + 1) * 64)
                        kb.op("pool", lambda e: e.tensor_scalar(out=tsc[:], in0=tri[:], scalar1=dA[:, h:h + 1], scalar2=None, op0=ALU.mult),
                              reads=("tri", "dA"), writes=("tsc",))
                        kb.op("pe", lambda e: e.matmul(pcr[:, :], onesf[:], tsc[:], start=True, stop=False), reads=("onesf", "tsc"), writes=("pcr",))
                        kb.op("pe", lambda e: e.matmul(pcr[:, :], identf[:], nmaskf[:], start=False, stop=True), reads=("identf", "nmaskf"), writes=("pcr",))
                        kb.op("act", lambda e: e.activation(out=LT[:], in_=pcr[:, :], func=AF.Exp, bias=negcum[:, h:h + 1], scale=1.0),
                              reads=("pcr", "negcum"), writes=("LT",))
                        kb.op("dve", lambda e: e.tensor_tensor(out=MT[:], in0=LT[:], in1=cbm[:], op=ALU.mult), reads=("LT", "cbm"), writes=("MT",))
                        kb.op("pe", lambda e: e.matmul(py[:, h4 * 64:(h4 + 1) * 64], MT[:], xdt[:, hs], start=True, stop=True), reads=("MT", "xdt"), writes=("py",))
                        kb.op("pe", lambda e: e.matmul(py[:, 256 + h4 * 64:256 + (h4 + 1) * 64], CT, S_bf[:, h, :], start=True, stop=True),
                              reads=allxc + ("S_bf",), writes=("py",))
                        kb.op("pe", lambda e: e.matmul(pSu[:, h4 * 64:(h4 + 1) * 64], B_tok[:, tt, g * 128:(g + 1) * 128], xdd[:, hs], start=True, stop=True),
                              reads=("B_tok", "xdd"), writes=("pSu",))
                    kb.op("act", lambda e: e.activation(out=yi[:], in_=py[:, 0:256], func=AF.Copy), reads=("py",), writes=("yi",))
                    for h4 in range(4):
                        h = 4 * g + h4
                        kb.op("dve", lambda e: e.scalar_tensor_tensor(out=y_sb[:, h * 64:(h + 1) * 64], in0=py[:, 256 + h4 * 64:256 + (h4 + 1) * 64],
                                                                      scalar=ecum[:, h:h + 1], in1=yi[:, h4 * 64:(h4 + 1) * 64], op0=ALU.mult, op1=ALU.add),
                              reads=("py", "ecum", "yi"), writes=("y_sb",))
                    Sg = S[:, 4 * g:4 * g + 4, :]
                    kb.op("dve", lambda e: e.tensor_tensor(out=Sg, in0=Sg, in1=etot[:, 4 * g:4 * g + 4].unsqueeze(2).to_broadcast([128, 4, 64]), op=ALU.mult),
                          reads=("S", "etot"), writes=("S",))
                    kb.op("dve", lambda e: e.tensor_tensor(out=Sg, in0=Sg, in1=pSu[:, :].rearrange("p (h d) -> p h d", d=64), op=ALU.add),
                          reads=("S", "pSu"), writes=("S",))
                    kb.op("act", lambda e: e.activation(out=S_bf[:, 4 * g:4 * g + 4, :], in_=Sg, func=AF.Copy), reads=("S",), writes=("S_bf",))
                kb.op("pool", lambda e: e.tensor_tensor(out=tmp[:].rearrange("p (h d) -> p h d", d=64), in0=x3,
                                                        in1=Db[:, :].unsqueeze(2).to_broadcast([128, 32, 64]), op=ALU.mult), reads=("x_tok", "Db"), writes=("tmp",))
                kb.op("dve", lambda e: e.tensor_tensor(out=y_sb[:], in0=y_sb[:], in1=tmp[:], op=ALU.add), reads=("y_sb", "tmp"), writes=("y_sb",))
                kb.dma("sp", lambda e: e.dma_start(out=y_d[t * 128:(t + 1) * 128, :], in_=y_sb[:]), reads=("y_sb",), writes=(("y_d", t),))
        kb.barrier()
    with contextlib.ExitStack() as es:
        sb = lambda n, s, d: kb.sb(n, s, d, es)
        ps = lambda n, s, d: kb.ps(n, s, d, es)
        w_z = sb("w_z", [128, 8, 2048], BF16)
        w_out = sb("sw_out", [128, 16, 1024], BF16)
        gb = sb("gb", [128, D], F32)
        gnb = sb("gnb", [128, 2048], F32)
        hb2 = [sb("hb", [128, D], F32) for _ in range(2)]
        hn = sb("hn", [128, D], F32)
        hnb = sb("hnb", [128, D], BF16)
        scr = sb("scr", [128, D], F32)
        stat = sb("stat", [128, 4], F32)
        hnT = sb("hnT", [128, 8, 128], BF16)
        zs = sb("zs", [128, 2048], F32)
        yb = sb("yb", [128, 2048], F32)
        sq = sb("sq", [128, 2048], F32)
        ss = sb("ss", [128, 8], F32)
        yn = sb("yn", [128, 2048], BF16)
        yT = sb("yT", [128, 16, 128], BF16)
        pst = ps("pst", [128, 8, 128], BF16)
        pp = [ps("pp", [128, 512], F32) for _ in range(2)]
        pm = ps("pm", [128, 1024], F32)
        for c in range(8):
            kb.dma("pool", lambda e, c=c: e.dma_start(out=w_z[:, c, :], in_=W["ssm_w_in"][0, c * 128:(c + 1) * 128, 0:2048]), writes=("w_z",))
        for c in range(16):
            kb.dma("pool", lambda e, c=c: e.dma_start(out=w_out[:, c, :], in_=W["ssm_w_out"][0, c * 128:(c + 1) * 128, :]), writes=("sw_out",))
        kb.dma("sp", lambda e: e.dma_start(out=gb[:], in_=W["norm_mix"][li].partition_broadcast(128)), writes=("gb",))
        kb.dma("sp", lambda e: e.dma_start(out=gnb[:], in_=W["ssm_gnorm"][0].partition_broadcast(128)), writes=("gnb",))
        ppi = [0]
        for t in range(NT):
            hb = hb2[t % 2]
            hk = ("hb", t % 2)
            load_norm_T(kb, cm, t, hsrc, hsrc_name, hb, hk, gb, "gb", hn, hnb, scr, stat, pst, hnT[:, :, :], "hnT")
            kb.dma("sp", lambda e: e.dma_start(out=yb[:], in_=y_d[t * 128:(t + 1) * 128, :]), reads=(("y_d", t),), writes=("yb",))
            for cg in range(4):
                b = ppi[0] % 2
                ppi[0] += 1
                for c in range(8):
                    kb.op("pe", lambda e, c=c: e.matmul(pp[b][:, :], hnT[:, c, :], w_z[:, c, cg * 512:(cg + 1) * 512], start=(c == 0), stop=(c == 7)),
                          reads=("w_z", "hnT"), writes=(("pp", b),))
                kb.op("act", lambda e: e.activation(out=zs[:, cg * 512:(cg + 1) * 512], in_=pp[b][:, :], func=AF.Silu), reads=(("pp", b),), writes=("zs",))
            kb.op("dve", lambda e: e.tensor_tensor(out=yb[:], in0=yb[:], in1=zs[:], op=ALU.mult), reads=("yb", "zs"), writes=("yb",))
            for g in range(8):
                kb.op("act", lambda e, g=g: e.activation(out=sq[:, g * 256:(g + 1) * 256], in_=yb[:, g * 256:(g + 1) * 256], func=AF.Square,
                                                         accum_out=ss[:, g:g + 1]), reads=("yb",), writes=("sq", "ss"))
            kb.op("dve", lambda e: e.tensor_scalar(out=ss[:], in0=ss[:], scalar1=1.0 / 256.0, scalar2=EPS, op0=ALU.mult, op1=ALU.add), reads=("ss",), writes=("ss",))
            kb.op("act", lambda e: e.activation(out=ss[:], in_=ss[:], func=AF.Sqrt), reads=("ss",), writes=("ss",))
            kb.op("dve", lambda e: e.reciprocal(out=ss[:], in_=ss[:]), reads=("ss",), writes=("ss",))
            kb.op("dve", lambda e: e.tensor_tensor(out=yb[:].rearrange("p (g k) -> p g k", k=256), in0=yb[:].rearrange("p (g k) -> p g k", k=256),
                                                   in1=ss[:, :].unsqueeze(2).to_broadcast([128, 8, 256]), op=ALU.mult), reads=("yb", "ss"), writes=("yb",))
            kb.op("dve", lambda e: e.tensor_tensor(out=yn[:], in0=yb[:], in1=gnb[:], op=ALU.mult), reads=("yb", "gnb"), writes=("yn",))
            for blk in range(2):
                for cc in range(8):
                    kb.op("pe", lambda e, cc=cc: e.transpose(out=pst[:, cc, :], in_=yn[:, (blk * 8 + cc) * 128:(blk * 8 + cc + 1) * 128], identity=cm.ident[:]),
                          reads=("yn", "ident"), writes=("pst",))
                kb.op("act", lambda e: e.activation(out=yT[:, blk * 8:(blk + 1) * 8, :], in_=pst[:, :, :], func=AF.Copy), reads=("pst",), writes=("yT",))
            for cg in range(2):
                for c in range(16):
                    kb.op("pe", lambda e, c=c: e.matmul(pm[:, cg * 512:(cg + 1) * 512], yT[:, c, :], w_out[:, c, cg * 512:(cg + 1) * 512], start=(c == 0), stop=(c == 15)),
                          reads=("yT", "sw_out"), writes=("pm",))
                kb.op("dve", lambda e: e.tensor_tensor(out=hb[:, cg * 512:(cg + 1) * 512], in0=pm[:, cg * 512:(cg + 1) * 512],
                                                       in1=hb[:, cg * 512:(cg + 1) * 512], op=ALU.add), reads=("pm", hk), writes=(hk,))
            kb.dma("sp", lambda e: e.dma_start(out=hdst[t * 128:(t + 1) * 128, :], in_=hb[:]), reads=(hk,), writes=((hdst_name, t),))
        kb.barrier()


def final_phase(kb, cm, hsrc, hsrc_name, y, W):
    NT = cm.NT
    with contextlib.ExitStack() as es:
        sb = lambda n, s, d: kb.sb(n, s, d, es)
        gb = sb("gbf", [128, D], F32)
        h_t = [sb("hf", [128, D], F32) for _ in range(2)]
        o_t = [sb("of", [128, D], F32) for _ in range(2)]
        scr = sb("scrf", [128, D], F32)
        stat = [sb("statf", [128, 4], F32) for _ in range(2)]
        kb.dma("sp", lambda e: e.dma_start(out=gb[:], in_=W["norm_final"].partition_broadcast(128)), writes=("gbf",))
        for t in range(NT):
            b = t % 2
            kb.dma("sp", lambda e: e.dma_start(out=h_t[b][:], in_=hsrc[t * 128:(t + 1) * 128, :]),
                   reads=((hsrc_name, t),), writes=(("hf", b),))
            rmsnorm_tile(kb, h_t[b], ("hf", b), gb, "gbf", o_t[b], ("of", b), scr, "scrf", stat[b], ("statf", b))
            kb.dma("sp", lambda e: e.dma_start(out=y[t * 128:(t + 1) * 128, :], in_=o_t[b][:]),
                   reads=(("of", b),), writes=(("y", t),))


        kb.barrier()
WEIGHT_SPECS = {
    "norm_mix": [4, 1024], "norm_ffn": [4, 1024], "norm_final": [1024], "hgrn_lb_logits": [4, 1024],
    "hgrn_w_in": [2, 1024, 4096], "hgrn_gnorm": [2, 128], "hgrn_w_out": [2, 1024, 1024],
    "fox_w_in": [1, 1024, 3088], "fox_b_f": [1, 16], "fox_w_out": [1, 1024, 1024],
    "ssm_w_in": [1, 1024, 6176], "ssm_conv_w": [1, 4, 4096], "ssm_conv_b": [1, 4096],
    "ssm_dt_bias": [1, 32], "ssm_a_log": [1, 32], "ssm_d": [1, 32], "ssm_gnorm": [1, 2048],
    "ssm_w_out": [1, 2048, 1024], "peer_w_q": [4, 1024, 2048], "peer_keysT": [4, 16, 128, 128],
    "peer_u0": [16384, 1024], "peer_v0": [16384, 1024], "peer_u1": [16384, 1024], "peer_v1": [16384, 1024],
    "peer_u2": [16384, 1024], "peer_v2": [16384, 1024], "peer_u3": [16384, 1024], "peer_v3": [16384, 1024],
    "ssm_conv_wT": [128, 32, 4], "ssm_conv_bT": [128, 32],
    "c_tri": [128, 128], "c_negmask": [128, 128], "c_ones3": [3, 8192],
    "hgrn_lbT": [128, 4, 8], "c_ones": [128, 128], "c_rmask": [128, 512], "c_maskT16": [128, 128], "c_cmask": [128, 8],
}


def build(T=SEQ, plan=("peer0", "final"), used=None):
    kb = KB()
    cm = Common(kb, T)
    x = kb.dram("x", [T, D], F32, kind="ExternalInput")
    y = kb.dram("y", [T, D], F32, kind="ExternalOutput")
    hA = kb.dram("hA", [T, D], F32)
    W = {}
    names = set()
    for p in plan:
        if p.startswith("peer"):
            names |= {"peer_w_q", "peer_keysT", "peer_u" + p[4:], "peer_v" + p[4:], "norm_ffn"}
        if p == "final":
            names |= {"norm_final"}
        if p.startswith("ssd"):
            names |= {"ssm_w_in", "ssm_conv_wT", "ssm_conv_bT", "ssm_dt_bias", "ssm_a_log", "ssm_d", "ssm_gnorm", "ssm_w_out", "norm_mix",
                      "c_tri", "c_negmask", "c_ones"}
        if p.startswith("fox"):
            names |= {"fox_w_in", "fox_b_f", "fox_w_out", "norm_mix", "c_tri", "c_negmask", "c_ones3", "c_ones"}
        if p.startswith("hgrn"):
            names |= {"hgrn_w_in", "hgrn_w_out", "hgrn_gnorm", "hgrn_lbT", "norm_mix", "c_ones", "c_rmask", "c_maskT16", "c_cmask"}
    for n in sorted(names):
        W[n] = kb.dram(n, WEIGHT_SPECS[n], F32, kind="ExternalInput")
    peer_layers = [int(p[4:]) for p in plan if p.startswith("peer")]
    for l in peer_layers:
        W["peer_ub%d" % l] = kb.dram("peer_ub%d" % l, [16384, 1024], BF16)
        W["peer_vb%d" % l] = kb.dram("peer_vb%d" % l, [16384, 1024], BF16)
    if peer_layers:
        convert_phase(kb, W, peer_layers)
    cur, cur_name = x, "x"
    for p in plan:
        kb.barrier()
        if p.startswith("peer"):
            li = int(p[4:])
            peer_phase(kb, cm, li, cur, cur_name, hA, "hA", W)
            cur, cur_name = hA, "hA"
        elif p.startswith("hgrn"):
            li = int(p[4:])
            hgrn_phase(kb, cm, li, li // 3, cur, cur_name, hA, "hA", W)
            cur, cur_name = hA, "hA"
        elif p.startswith("ssd"):
            li = int(p[3:])
            y_d = kb.dram("s_y", [T, 2048], F32)
            ssd_phase(kb, cm, li, cur, cur_name, hA, "hA", W, y_d)
            cur, cur_name = hA, "hA"
        elif p.startswith("fox"):
            li = int(p[3:])
            dk = "ExternalOutput" if DEBUG_SCRATCH else "Internal"
            scr_d = {"qT": kb.dram("f_qT", [1024, T], BF16, dk), "kT": kb.dram("f_kT", [1024, T], BF16, dk), "v": kb.dram("f_v", [T, 1024], BF16, dk),
                     "ca": kb.dram("f_ca", [16, 3, T], BF16, dk), "o": kb.dram("f_o", [T, 1024], BF16, dk), "negc": kb.dram("f_negc", [T, 16], F32, dk)}
            fox_phase(kb, cm, li, cur, cur_name, hA, "hA", W, scr_d)
            cur, cur_name = hA, "hA"
        elif p == "final":
            final_phase(kb, cm, cur, cur_name, y, W)
    kb.finish([("y", t) for t in range(cm.NT)])
    kb.es.close()
    return kb, sorted(names)


def host_consts():
    s = np.arange(128)
    c = {"c_ident": np.eye(128, dtype=np.float32)}
    c["c_ones"] = np.ones((128, 128), np.float32)
    c["c_rmask"] = np.tile((np.arange(512) % 16 != 0).astype(np.float32)[None, :], (128, 1))
    c["c_maskT16"] = ((s[:, None] // 16 == s[None, :] // 16) & (s[:, None] <= s[None, :])).astype(np.float32)
    c["c_cmask"] = (s[:, None] // 16 == np.arange(8)[None, :]).astype(np.float32)
    c["c_tri"] = (s[:, None] <= s[None, :]).astype(np.float32)
    c["c_negmask"] = np.where(s[:, None] <= s[None, :], 0.0, -30000.0).astype(np.float32)
    c["c_ones3"] = np.ones((3, 8192), np.float32)
    return c


def layout_weights(inp):
    out = dict(inp)
    for l in range(4):
        if "peer_u" in inp:
            out["peer_u%d" % l] = np.asarray(inp["peer_u"])[l]
            out["peer_v%d" % l] = np.asarray(inp["peer_v"])[l]
    if "peer_keys" in inp:
        pk = np.asarray(inp["peer_keys"])
        out["peer_keysT"] = np.ascontiguousarray(pk.reshape(4, 16, 128, 128).transpose(0, 1, 3, 2))
    if "ssm_conv_w" in inp:
        out["ssm_conv_wT"] = np.ascontiguousarray(np.asarray(inp["ssm_conv_w"])[0].reshape(4, 32, 128).transpose(2, 1, 0))
        out["ssm_conv_bT"] = np.ascontiguousarray(np.asarray(inp["ssm_conv_b"])[0].reshape(32, 128).T)
    if "hgrn_lb_logits" in inp:
        out["hgrn_lbT"] = np.ascontiguousarray(np.asarray(inp["hgrn_lb_logits"]).reshape(4, 8, 128).transpose(2, 0, 1))
    return out


FULL_PLAN = ("hgrn0", "peer0", "fox1", "peer1", "ssd2", "peer2", "hgrn3", "peer3", "final")


def kernel(**inputs):
    inp = layout_weights({k: np.asarray(v) for k, v in inputs.items()})
    inp.update(host_consts())
    kb, names = build(SEQ, plan=FULL_PLAN)
    shared = {n: np.ascontiguousarray(inp[n], dtype=np.float32) for n in names}
    shared["c_ident"] = inp["c_ident"]
    in_maps = []
    for c in range(NCORES):
        m = dict(shared)
        m["x"] = np.ascontiguousarray(inp["x"][c], dtype=np.float32)
        in_maps.append(m)
    res = run_bass_kernel_spmd(kb.nc, in_maps, core_ids=list(range(NCORES)))
    return np.stack([np.asarray(r["y"]) for r in res.results], axis=0).astype(np.float32)
```

```python
import contextlib
import numpy as np
import concourse.bass as bass
import concourse.mybir as mybir
from concourse.bass_utils import run_bass_kernel_spmd

F32 = mybir.dt.float32
BF16 = mybir.dt.bfloat16
U32 = mybir.dt.uint32
AF = mybir.ActivationFunctionType
ALU = mybir.AluOpType
AX = mybir.AxisListType

D = 1024
NCORES = 8
SEQ = 8192
EPS = 1e-6
NEG = -1.0e30
DEBUG_SCRATCH = False


class KB:
    R_DMA = 8

    def __init__(self):
        self.nc = bass.Bass("TRN2", target_bir_lowering=False)
        self.es = contextlib.ExitStack()
        nc = self.nc
        self.eng = {"pe": nc.tensor, "dve": nc.vector, "act": nc.scalar, "pool": nc.gpsimd, "sp": nc.sync}
        self.sems = {}
        for e in ("pe", "dve", "act", "pool"):
            self.sems[e] = self.es.enter_context(nc.semaphore("s_" + e))
        for q in ("sp", "pool", "act"):
            for s in range(self.R_DMA):
                self.sems[("dma", q, s)] = self.es.enter_context(nc.semaphore(f"d_{q}_{s}"))
        self.cnt = {k: 0 for k in self.sems}
        self.dma_i = {"sp": 0, "pool": 0, "act": 0}
        self.waited = {e: {} for e in self.eng}
        self.last_w = {}
        self.readers = {}
        self.n_ins = 0
        self.uid = 0
        self.psum_names = set()

    def sb(self, name, shape, dt, es=None):
        self.uid += 1
        return (es or self.es).enter_context(self.nc.sbuf_tensor(f"{name}_{self.uid}", list(shape), dt))

    def ps(self, name, shape, dt, es=None):
        self.uid += 1
        self.psum_names.add(name)
        shape = list(shape)
        esz = 2 if dt == BF16 else 4
        per_part = esz
        for s in shape[1:]:
            per_part *= s
        if per_part >= 2048 or len(shape) != 2:
            assert per_part % 2048 == 0, (name, shape)
            return (es or self.es).enter_context(self.nc.psum_tensor(f"{name}_{self.uid}", shape, dt))
        t = (es or self.es).enter_context(self.nc.psum_tensor(f"{name}_{self.uid}", [shape[0], 2048 // esz], dt))
        return t[:, 0:shape[1]]

    def dram(self, name, shape, dt, kind="Internal"):
        return self.nc.dram_tensor(name, list(shape), dt, kind=kind).ap()

    def _deps(self, reads, writes):
        deps = {}

        def add(h):
            if h is None:
                return
            k, c = h
            if deps.get(k, 0) < c:
                deps[k] = c

        for k in reads:
            add(self.last_w.get(k))
        for k in writes:
            add(self.last_w.get(k))
            for sk, c in self.readers.get(k, {}).items():
                add((sk, c))
        return deps

    def _waits(self, eng, deps):
        E = self.eng[eng]
        w = self.waited[eng]
        for sk, c in deps.items():
            if eng == "pe" and sk == "pe":
                continue
            if w.get(sk, 0) < c:
                E.wait_ge(self.sems[sk], c)
                w[sk] = c
                self.n_ins += 1

    def _record(self, h, reads, writes):
        sk, c = h
        for k in reads:
            self.readers.setdefault(k, {})[sk] = c
        for k in writes:
            self.last_w[k] = h
            self.readers[k] = {}

    def _excl(self, reads, writes):
        r2, w2 = [], list(writes)
        for k in reads:
            root = k if isinstance(k, str) else k[0]
            (w2 if root in self.psum_names else r2).append(k)
        return r2, w2

    def op(self, eng, fn, reads=(), writes=()):
        reads, writes = self._excl(reads, writes)
        deps = self._deps(reads, writes)
        self._waits(eng, deps)
        ins = fn(self.eng[eng])
        self.cnt[eng] += 1
        ins.then_inc(self.sems[eng], 1)
        self.n_ins += 1
        self._record((eng, self.cnt[eng]), reads, writes)

    def dma(self, q, fn, reads=(), writes=()):
        slot = self.dma_i[q] % self.R_DMA
        self.dma_i[q] += 1
        sk = ("dma", q, slot)
        reads, writes = self._excl(reads, writes)
        deps = self._deps(reads, writes)
        if self.cnt[sk] > 0:
            deps[sk] = max(deps.get(sk, 0), self.cnt[sk])
        self._waits(q, deps)
        ins = fn(self.eng[q])
        self.cnt[sk] += 16
        ins.then_inc(self.sems[sk], 16)
        self.n_ins += 1
        self._record((sk, self.cnt[sk]), reads, writes)

    def barrier(self):
        deps = {sk: c for sk, c in self.cnt.items() if c > 0}
        for eng in self.eng:
            self._waits(eng, dict(deps))

    def finish(self, keys):
        deps = self._deps(keys, ())
        self._waits("sp", deps)


class Common:
    def __init__(self, kb, T):
        self.kb = kb
        self.T = T
        self.NT = T // 128
        nc = kb.nc
        self.ident_d = kb.dram("c_ident", [128, 128], F32, kind="ExternalInput")
        self.ident = kb.sb("ident", [128, 128], BF16)
        kb.dma("pool", lambda e: e.dma_start(out=self.ident[:], in_=self.ident_d[:, :]), reads=(), writes=("ident",))


def rmsnorm_tile(kb, h_t, hk, gb, gk, out_t, ok, scr, sk, stat, stk, es_keys=()):
    kb.op("act", lambda e: e.activation(out=scr[:], in_=h_t[:], func=AF.Square, accum_out=stat[:, 0:1]),
          reads=(hk,), writes=(sk, stk))
    kb.op("dve", lambda e: e.tensor_scalar(out=stat[:, 1:2], in0=stat[:, 0:1], scalar1=1.0 / D, scalar2=EPS,
                                           op0=ALU.mult, op1=ALU.add), reads=(stk,), writes=(stk,))
    kb.op("act", lambda e: e.activation(out=stat[:, 2:3], in_=stat[:, 1:2], func=AF.Sqrt), reads=(stk,), writes=(stk,))
    kb.op("dve", lambda e: e.reciprocal(out=stat[:, 3:4], in_=stat[:, 2:3]), reads=(stk,), writes=(stk,))
    kb.op("dve", lambda e: e.scalar_tensor_tensor(out=out_t[:], in0=h_t[:], scalar=stat[:, 3:4], in1=gb[:],
                                                  op0=ALU.mult, op1=ALU.mult), reads=(hk, stk, gk), writes=(ok,))


def transpose_tile(kb, cm, src_bf, srck, dst_ap, dstk, pst, pstk, nch=8):
    for c in range(nch):
        kb.op("pe", lambda e, c=c: e.transpose(out=pst[:, c, :], in_=src_bf[:, c * 128:(c + 1) * 128], identity=cm.ident[:]),
              reads=(srck, "ident"), writes=(pstk,))
    kb.op("act", lambda e: e.activation(out=dst_ap, in_=pst[:, 0:nch, :], func=AF.Copy), reads=(pstk,), writes=(dstk,))


def convert_phase(kb, W, layers):
    RC = 512
    with contextlib.ExitStack() as es:
        bufs = [kb.sb("cvt", [128, RC // 128, 1024], BF16, es) for _ in range(4)]
        i = 0
        for l in layers:
            for k, nm in enumerate(("u", "v")):
                src = W["peer_%s%d" % (nm, l)]
                dst = W["peer_uvb%d" % l]
                for ch in range(16384 // RC):
                    b = i % 4
                    i += 1
                    rows = slice(ch * RC, (ch + 1) * RC)
                    kb.dma("pool", lambda e: e.dma_start(out=bufs[b][:], in_=src[rows, :].rearrange("(p r) d -> p r d", p=128)), writes=(("cvt", b),))
                    kb.dma("sp", lambda e: e.dma_start(out=dst[rows, k * 1024:(k + 1) * 1024].rearrange("(p r) d -> p r d", p=128), in_=bufs[b][:]),
                           reads=(("cvt", b),), writes=(("tab", nm, l),))
        kb.barrier()


def peer_phase(kb, cm, li, hsrc, hsrc_name, hdst, hdst_name, W):
    NT = cm.NT
    NG = 10
    with contextlib.ExitStack() as es:
        sb = lambda n, s, d: kb.sb(n, s, d, es)
        ps = lambda n, s, d: kb.ps(n, s, d, es)
        wq = sb("wq", [128, 8, 2048], BF16)
        keysT = sb("keysT", [128, 16, 128], BF16)
        gb = sb("gb", [128, D], F32)
        identf = sb("identf", [128, 128], F32)
        h_t = [sb("h_t", [128, D], F32) for _ in range(2)]
        hn2 = [sb("hn", [128, D], F32) for _ in range(2)]
        hnb = sb("hnb", [128, D], BF16)
        hnT = sb("hnT", [128, 8, 128], BF16)
        stat = sb("stat", [128, 4], F32)
        qT = sb("qT", [128, 16, 128], BF16)
        s_sb = sb("s_sb", [128, 16, 128], F32)
        wk = sb("wk", [128, 16, 128], F32)
        top = sb("top", [128, 16, 16], F32)
        ix = sb("ix", [128, 16, 16], U32)
        ixf = sb("ixf", [128, 16, 16], F32)
        cs = sb("cs", [128, 8, 256], F32)
        ci = sb("ci", [128, 8, 256], F32)
        wk2 = sb("wk2", [128, 8, 256], F32)
        top2 = sb("top2", [128, 8, 16], F32)
        t2c = sb("t2c", [128, 8, 16], F32)
        nmax = sb("nmax", [128, 8], F32)
        ez = sb("ez", [128, 8, 16], F32)
        zz = sb("zz", [128, 8], F32)
        rz = sb("rz", [128, 8], F32)
        eidx = sb("eidx", [128, 128], F32)
        gates2 = [sb("gates", [128, 128], F32) for _ in range(2)]
        eidu2 = [sb("eidu", [128, 128], U32) for _ in range(2)]
        junk = sb("junk", [128, D], F32)
        junk2 = sb("junk2", [128, 256], F32)
        uvg = [sb("uvg", [128, 2 * D], BF16) for _ in range(NG)]
        sc4 = [sb("sc4", [128, 4], F32) for _ in range(8)]
        Ds = [sb("Ds", [128, 128], BF16) for _ in range(8)]
        pst = ps("pst", [128, 8, 128], BF16)
        pq = ps("pq", [128, 4, 128], F32)
        psc = [ps("psc", [128, 4, 128], F32) for _ in range(4)]
        pacc = ps("pacc", [128, 1024], F32)

        for c in range(8):
            kb.dma("pool", lambda e, c=c: e.dma_start(out=wq[:, c, :], in_=W["peer_w_q"][li, c * 128:(c + 1) * 128, :]), writes=("wq",))
        kb.dma("pool", lambda e: e.dma_start(out=keysT[:], in_=W["peer_keysT"][li].rearrange("j d k -> d j k")), writes=("keysT",))
        kb.dma("sp", lambda e: e.dma_start(out=gb[:], in_=W["norm_ffn"][li].partition_broadcast(128)), writes=("gb",))
        kb.dma("sp", lambda e: e.dma_start(out=identf[:], in_=cm.ident_d[:, :]), writes=("identf",))
        uv_tab = W["peer_uvb%d" % li]

        def stage_A(t):
            p2 = t % 2
            hb, hk = h_t[p2], ("h_t", p2)
            hn, hnk = hn2[p2], ("hn", p2)
            gates, gk = gates2[p2], ("gates", p2)
            eidu, ek = eidu2[p2], ("eidu", p2)
            kb.dma("sp", lambda e: e.dma_start(out=hb[:], in_=hsrc[t * 128:(t + 1) * 128, :]), reads=((hsrc_name, t),), writes=(hk,))
            rmsnorm_tile(kb, hb, hk, gb, "gb", hn, hnk, junk, "junk", stat, "stat")
            kb.op("act", lambda e: e.activation(out=hnb[:], in_=hn[:], func=AF.Copy), reads=(hnk,), writes=("hnb",))
            transpose_tile(kb, cm, hnb, "hnb", hnT[:, :, :], "hnT", pst, "pst")
            for jg in range(4):
                for jj in range(4):
                    j = jg * 4 + jj
                    for c in range(8):
                        kb.op("pe", lambda e, c=c: e.matmul(pq[:, jj, :], wq[:, c, j * 128:(j + 1) * 128], hnT[:, c, :], start=(c == 0), stop=(c == 7)),
                              reads=("wq", "hnT"), writes=("pq",))
                kb.op("act", lambda e: e.activation(out=qT[:, jg * 4:(jg + 1) * 4, :], in_=pq[:], func=AF.Copy), reads=("pq",), writes=(("qT", jg),))
            for jg in range(4):
                for jj in range(4):
                    j = jg * 4 + jj
                    kb.op("pe", lambda e: e.matmul(psc[jg][:, jj, :], qT[:, j, :], keysT[:, j, :], start=True, stop=True),
                          reads=(("qT", jg), "keysT"), writes=(("psc", jg),))
                kb.op("act", lambda e: e.activation(out=s_sb[:, jg * 4:(jg + 1) * 4, :], in_=psc[jg][:], func=AF.Copy),
                      reads=(("psc", jg),), writes=(("s_sb", jg),))

        def stage_A1(t):
            p2 = t % 2
            eidu, ek = eidu2[p2], ("eidu", p2)
            for j in range(16):
                sk = ("s_sb", j // 4)
                tk = ("top", j)
                kb.op("dve", lambda e: e.max(out=top[:, j, 0:8], in_=s_sb[:, j, :]), reads=(sk,), writes=(tk,))
                kb.op("dve", lambda e: e.max_index(out=ix[:, j, 0:8], in_max=top[:, j, 0:8], in_values=s_sb[:, j, :]), reads=(sk, tk), writes=(("ix", j),))
                kb.op("dve", lambda e: e.match_replace(out=wk[:, j, :], in_to_replace=top[:, j, 0:8], in_values=s_sb[:, j, :], imm_value=NEG),
                      reads=(sk, tk), writes=(("wk", j),))
                kb.op("dve", lambda e: e.max(out=top[:, j, 8:16], in_=wk[:, j, :]), reads=(("wk", j),), writes=(tk,))
                kb.op("dve", lambda e: e.max_index(out=ix[:, j, 8:16], in_max=top[:, j, 8:16], in_values=wk[:, j, :]), reads=(("wk", j), tk), writes=(("ix", j),))
            allix = tuple(("ix", j) for j in range(16))
            alltop = tuple(("top", j) for j in range(16))
            kb.op("dve", lambda e: e.tensor_copy(ixf[:], ix[:]), reads=allix, writes=("ixf",))
            for hh in range(8):
                j0, j1 = 2 * hh, 2 * hh + 1
                csv = cs[:, hh, :].rearrange("p (a b) -> p a b", b=16)
                civ = ci[:, hh, :].rearrange("p (a b) -> p a b", b=16)
                kb.op("dve", lambda e: e.tensor_tensor(out=csv, in0=top[:, j0, :].unsqueeze(2).to_broadcast([128, 16, 16]),
                                                       in1=top[:, j1, :].unsqueeze(1).to_broadcast([128, 16, 16]), op=ALU.add),
                      reads=alltop, writes=(("cs", hh),))
                kb.op("dve", lambda e: e.tensor_scalar(out=civ, in0=ixf[:, j0, :].unsqueeze(2).to_broadcast([128, 16, 16]), scalar1=128.0, scalar2=None, op0=ALU.mult),
                      reads=("ixf",), writes=(("ci", hh),))
                kb.op("dve", lambda e: e.tensor_tensor(out=civ, in0=civ, in1=ixf[:, j1, :].unsqueeze(1).to_broadcast([128, 16, 16]), op=ALU.add),
                      reads=("ixf", ("ci", hh)), writes=(("ci", hh),))
                t2k = ("top2", hh)
                kb.op("dve", lambda e: e.max(out=top2[:, hh, 0:8], in_=cs[:, hh, :]), reads=(("cs", hh),), writes=(t2k,))
                kb.op("dve", lambda e: e.match_replace(out=wk2[:, hh, :], in_to_replace=top2[:, hh, 0:8], in_values=cs[:, hh, :], imm_value=NEG),
                      reads=(("cs", hh), t2k), writes=(("wk2", hh),))
                kb.op("dve", lambda e: e.max(out=top2[:, hh, 8:16], in_=wk2[:, hh, :]), reads=(("wk2", hh),), writes=(t2k,))
                for k in range(16):
                    kb.op("dve", lambda e: e.scalar_tensor_tensor(out=junk2[:], in0=cs[:, hh, :], scalar=top2[:, hh, k:k + 1], in1=ci[:, hh, :],
                                                                  op0=ALU.is_equal, op1=ALU.mult, accum_out=eidx[:, hh * 16 + k:hh * 16 + k + 1]),
                          reads=(("cs", hh), ("ci", hh), t2k), writes=("junk2", "eidx"))
            allt2 = tuple(("top2", hh) for hh in range(8))
            kb.op("dve", lambda e: e.tensor_scalar(out=nmax[:], in0=top2[:, :, 0], scalar1=-1.0, scalar2=None, op0=ALU.mult), reads=allt2, writes=("nmax",))
            kb.op("dve", lambda e: e.tensor_copy(t2c[:], top2[:]), reads=allt2, writes=("t2c",))
            kb.op("dve", lambda e: e.tensor_scalar(out=eidx[:], in0=eidx[:], scalar1=0.0, scalar2=16383.0, op0=ALU.max, op1=ALU.min),
                  reads=("eidx",), writes=("eidx",))
            kb.op("dve", lambda e: e.tensor_copy(eidu[:], eidx[:]), reads=("eidx",), writes=(ek,))

        def stage_A2(t):
            p2 = t % 2
            gates, gk = gates2[p2], ("gates", p2)
            for hh in range(8):
                kb.op("act", lambda e: e.activation(out=ez[:, hh, :], in_=t2c[:, hh, :], func=AF.Exp, bias=nmax[:, hh:hh + 1], scale=1.0,
                                                    accum_out=zz[:, hh:hh + 1]), reads=("t2c", "nmax"), writes=("ez", "zz"))
            kb.op("dve", lambda e: e.reciprocal(out=rz[:], in_=zz[:]), reads=("zz",), writes=("rz",))
            kb.op("dve", lambda e: e.tensor_tensor(out=gates[:].rearrange("p (h k) -> p h k", k=16), in0=ez[:],
                                                   in1=rz[:].unsqueeze(2).to_broadcast([128, 8, 16]), op=ALU.mult), reads=("ez", "rz"), writes=(gk,))

        gi = [0]

        def stage_BC(t):
            p2 = t % 2
            hb, hk = h_t[p2], ("h_t", p2)
            hn, hnk = hn2[p2], ("hn", p2)
            eidu, ek = eidu2[p2], ("eidu", p2)
            gates, gk = gates2[p2], ("gates", p2)
            for s in range(128):
                b = gi[0] % NG
                d4 = gi[0] % 8
                gi[0] += 1
                kb.dma("pool", lambda e: e.indirect_dma_start(out=uvg[b][:], out_offset=None, in_=uv_tab,
                                                              in_offset=bass.IndirectOffsetOnAxis(ap=eidu[:, s:s + 1], axis=0)),
                       reads=(ek, ("tab", "u", li), ("tab", "v", li)), writes=(("uvg", b),))
                kb.op("dve", lambda e: e.scalar_tensor_tensor(out=junk[:], in0=uvg[b][:, 0:1024], scalar=1.0, in1=hn[:], op0=ALU.mult, op1=ALU.mult,
                                                              accum_out=sc4[d4][:, 0:1]), reads=(("uvg", b), hnk), writes=("junk", ("sc4", d4)))
                kb.op("act", lambda e: e.activation(out=sc4[d4][:, 1:2], in_=sc4[d4][:, 0:1], func=AF.Gelu), reads=(("sc4", d4),), writes=(("sc4", d4),))
                kb.op("act", lambda e: e.activation(out=sc4[d4][:, 2:3], in_=sc4[d4][:, 1:2], func=AF.Copy, scale=gates[:, s:s + 1]),
                      reads=(("sc4", d4), gk), writes=(("sc4", d4),))
                kb.op("act", lambda e: e.activation(out=Ds[d4][:], in_=identf[:], func=AF.Copy, scale=sc4[d4][:, 2:3]),
                      reads=("identf", ("sc4", d4)), writes=(("Ds", d4),))
                for half in range(2):
                    kb.op("pe", lambda e: e.matmul(pacc[:, half * 512:(half + 1) * 512], Ds[d4][:], uvg[b][:, 1024 + half * 512:1024 + (half + 1) * 512],
                                                   start=(s == 0), stop=(s == 127)), reads=(("Ds", d4), ("uvg", b)), writes=("pacc",))
            for half in range(2):
                kb.op("dve", lambda e: e.tensor_tensor(out=hb[:, half * 512:(half + 1) * 512], in0=pacc[:, half * 512:(half + 1) * 512],
                                                       in1=hb[:, half * 512:(half + 1) * 512], op=ALU.add), reads=("pacc", hk), writes=(hk,))
            kb.dma("sp", lambda e: e.dma_start(out=hdst[t * 128:(t + 1) * 128, :], in_=hb[:]), reads=(hk,), writes=((hdst_name, t),))

        stage_A(0)
        stage_A1(0)
        stage_A2(0)
        for t in range(NT):
            if t + 1 < NT:
                stage_A(t + 1)
            stage_BC(t)
            if t + 1 < NT:
                stage_A1(t + 1)
                stage_A2(t + 1)
        kb.barrier()


def load_norm_T(kb, cm, t, hsrc, hsrc_name, hb, hk, gb, gk, hn, hnb, scr, stat, pst, dst_ap, dstk):
    kb.dma("sp", lambda e: e.dma_start(out=hb[:], in_=hsrc[t * 128:(t + 1) * 128, :]), reads=((hsrc_name, t),), writes=(hk,))
    rmsnorm_tile(kb, hb, hk, gb, gk, hn, "hn", scr, "scr", stat, "stat")
    kb.op("act", lambda e: e.activation(out=hnb[:], in_=hn[:], func=AF.Copy), reads=("hn",), writes=("hnb",))
    transpose_tile(kb, cm, hnb, "hnb", dst_ap, dstk, pst, "pst")


def hgrn_phase(kb, cm, li, j, hsrc, hsrc_name, hdst, hdst_name, W):
    NT = cm.NT
    GT = min(4, NT)
    NTOK = GT * 128
    NCH = NTOK // 16
    SCALE = 128.0 ** -0.5
    with contextlib.ExitStack() as es:
        sb = lambda n, s, d: kb.sb(n, s, d, es)
        ps = lambda n, s, d: kb.ps(n, s, d, es)
        w_in = sb("w_in", [128, 8, 4096], BF16)
        w_out = sb("w_out", [128, 8, 1024], BF16)
        gb = sb("gb", [128, D], F32)
        gn = sb("gn", [128, 1], F32)
        lbz = sb("lbz", [128, 4, 8], F32)
        lbe = sb("lbe", [128, 4, 8], F32)
        den = sb("den", [128, 8], F32)
        num = sb("num", [128, 8], F32)
        lb = sb("lb", [128, 8], F32)
        oml = sb("oml", [128, 8], F32)
        epst = sb("epst", [128, 1], F32)
        rmask = sb("rmask", [128, 512], F32)
        maskT = sb("maskT", [128, 128], F32)
        cmask = sb("cmask", [128, 8], F32)
        ones = sb("ones", [128, 128], BF16)
        hb2 = [sb("hb", [128, D], F32) for _ in range(2)]
        hn = sb("hn", [128, D], F32)
        hnb = sb("hnb", [128, D], BF16)
        scr = sb("scr", [128, D], F32)
        stat = sb("stat", [128, 4], F32)
        hnT = sb("hnT", [128, 8, NTOK], BF16)
        v_tok = sb("v_tok", [128, GT, 1024], BF16)
        fs = sb("fs", [128, NTOK], F32)
        fT = sb("fT", [128, NTOK], F32)
        lf = sb("lf", [128, NTOK], F32)
        kk = sb("kk", [128, NTOK], F32)
        bT = sb("bT", [128, NTOK], F32)
        dT = sb("dT", [128, NTOK], F32)
        eb = sb("eb", [128, NTOK], F32)
        enb = sb("enb", [128, NTOK], F32)
        ed = sb("ed", [128, NTOK], F32)
        qd = sb("qd", [128, NTOK], BF16)
        kinv = sb("kinv", [128, NTOK], BF16)
        kdT = sb("kdT", [128, NTOK], BF16)
        kd_tok = sb("kd_tok", [128, GT, 128], BF16)
        sg = sb("sg", [128, NTOK], BF16)
        yT = sb("yT", [128, 8, NTOK], BF16)
        carry = [sb("carry", [128, 128], F32) for _ in range(8)]
        Sd = sb("Sd", [128, 128, 9], F32)
        So = sb("So", [128, 128, 9], F32)
        a9 = sb("a9", [128, 128, 9], F32)
        Sbf = sb("Sbf", [128, 8, 128], BF16)
        Vblk = sb("Vblk", [128, 8, 128], BF16)
        AT = sb("AT", [128, 128], BF16)
        osq = sb("osq", [128, 128], BF16)
        sdv = sb("sdv", [128, 128], F32)
        rsv = sb("rsv", [128, 128], F32)
        t1 = sb("t1", [128, 128], F32)
        pst = ps("pst", [128, 8, 128], BF16)
        pp = [ps("pp", [128, 512], F32) for _ in range(2)]
        pA = ps("pA", [128, 128], F32)
        pS = ps("pS", [128, 1024], F32)
        po = ps("po", [128, 128], F32)
        pss = ps("pss", [128, 128], F32)

        for c in range(8):
            kb.dma("pool", lambda e, c=c: e.dma_start(out=w_in[:, c, :], in_=W["hgrn_w_in"][j, c * 128:(c + 1) * 128, :]), writes=("w_in",))
            kb.dma("pool", lambda e, c=c: e.dma_start(out=w_out[:, c, :], in_=W["hgrn_w_out"][j, c * 128:(c + 1) * 128, :]), writes=("w_out",))
        kb.dma("pool", lambda e: e.dma_start(out=ones[:], in_=W["c_ones"][:, :]), writes=("ones",))
        kb.dma("sp", lambda e: e.dma_start(out=gb[:], in_=W["norm_mix"][li].partition_broadcast(128)), writes=("gb",))
        kb.dma("sp", lambda e: e.dma_start(out=gn[:], in_=W["hgrn_gnorm"][j].rearrange("(p o) -> p o", o=1)), writes=("gn",))
        kb.dma("sp", lambda e: e.dma_start(out=lbz[:], in_=W["hgrn_lbT"][:, :, :]), writes=("lbz",))
        kb.dma("sp", lambda e: e.dma_start(out=rmask[:], in_=W["c_rmask"][:, :]), writes=("rmask",))
        kb.dma("sp", lambda e: e.dma_start(out=maskT[:], in_=W["c_maskT16"][:, :]), writes=("maskT",))
        kb.dma("sp", lambda e: e.dma_start(out=cmask[:], in_=W["c_cmask"][:, :]), writes=("cmask",))
        kb.op("dve", lambda e: e.memset(epst[:], EPS), writes=("epst",))
        kb.op("dve", lambda e: e.memset(a9[:], 0.0), writes=("a9",))
        for hh in range(8):
            kb.op("dve", lambda e, hh=hh: e.memset(carry[hh][:], 0.0), writes=(("carry", hh),))
        kb.op("act", lambda e: e.activation(out=lbe[:], in_=lbz[:], func=AF.Exp), reads=("lbz",), writes=("lbe",))
        kb.op("dve", lambda e: e.tensor_tensor(out=den[:], in0=lbe[:, 0, :], in1=lbe[:, 1, :], op=ALU.add), reads=("lbe",), writes=("den",))
        kb.op("dve", lambda e: e.tensor_tensor(out=den[:], in0=den[:], in1=lbe[:, 2, :], op=ALU.add), reads=("lbe", "den"), writes=("den",))
        kb.op("dve", lambda e: e.tensor_tensor(out=den[:], in0=den[:], in1=lbe[:, 3, :], op=ALU.add), reads=("lbe", "den"), writes=("den",))
        kb.op("dve", lambda e: e.memset(num[:], 0.0), writes=("num",))
        for l in range(1, li + 1):
            kb.op("dve", lambda e, l=l: e.tensor_tensor(out=num[:], in0=num[:], in1=lbe[:, l, :], op=ALU.add), reads=("lbe", "num"), writes=("num",))
        kb.op("dve", lambda e: e.reciprocal(out=den[:], in_=den[:]), reads=("den",), writes=("den",))
        kb.op("dve", lambda e: e.tensor_tensor(out=lb[:], in0=num[:], in1=den[:], op=ALU.mult), reads=("num", "den"), writes=("lb",))
        kb.op("dve", lambda e: e.tensor_scalar(out=oml[:], in0=lb[:], scalar1=-1.0, scalar2=1.0, op0=ALU.mult, op1=ALU.add), reads=("lb",), writes=("oml",))

        ppi = [0]

        def proj_fm(col0):
            b = ppi[0] % 2
            ppi[0] += 1
            for c in range(8):
                kb.op("pe", lambda e, c=c: e.matmul(pp[b][:, 0:NTOK], w_in[:, c, col0:col0 + 128], hnT[:, c, :], start=(c == 0), stop=(c == 7)),
                      reads=("w_in", "hnT"), writes=(("pp", b),))
            return pp[b], ("pp", b)

        for g in range(NT // GT):
            for tt in range(GT):
                t = g * GT + tt
                load_norm_T(kb, cm, t, hsrc, hsrc_name, hb2[t % 2], ("hb", t % 2), gb, "gb", hn, hnb, scr, stat, pst,
                            hnT[:, :, tt * 128:(tt + 1) * 128], "hnT")
            for tt in range(GT):
                for cg in range(2):
                    b = ppi[0] % 2
                    ppi[0] += 1
                    for c in range(8):
                        kb.op("pe", lambda e, c=c: e.matmul(pp[b][:, :], hnT[:, c, tt * 128:(tt + 1) * 128],
                                                            w_in[:, c, 2048 + cg * 512:2048 + (cg + 1) * 512], start=(c == 0), stop=(c == 7)),
                              reads=("w_in", "hnT"), writes=(("pp", b),))
                    kb.op("act", lambda e: e.activation(out=v_tok[:, tt, cg * 512:(cg + 1) * 512], in_=pp[b][:, :], func=AF.Copy),
                          reads=(("pp", b),), writes=("v_tok",))
            for hh in range(8):
                p, pk = proj_fm(1024 + hh * 128)
                kb.op("act", lambda e: e.activation(out=fs[:], in_=p[:, 0:NTOK], func=AF.Sigmoid), reads=(pk,), writes=("fs",))
                kb.op("dve", lambda e: e.tensor_scalar(out=fT[:], in0=fs[:], scalar1=oml[:, hh:hh + 1], scalar2=lb[:, hh:hh + 1],
                                                       op0=ALU.mult, op1=ALU.add), reads=("fs", "oml", "lb"), writes=("fT",))
                kb.op("act", lambda e: e.activation(out=lf[:], in_=fT[:], func=AF.Ln), reads=("fT",), writes=("lf",))
                kb.op("pool", lambda e: e.tensor_scalar(out=kk[:], in0=fT[:], scalar1=-1.0, scalar2=1.0, op0=ALU.mult, op1=ALU.add),
                      reads=("fT",), writes=("kk",))
                kb.op("dve", lambda e: e.tensor_tensor_scan(out=bT[:], data0=rmask[:, 0:NTOK], data1=lf[:], initial=0.0,
                                                            op0=ALU.mult, op1=ALU.add), reads=("rmask", "lf"), writes=("bT",))
                b3 = bT[:].rearrange("p (c k) -> p c k", k=16)
                kb.op("dve", lambda e: e.tensor_tensor(out=dT[:].rearrange("p (c k) -> p c k", k=16),
                                                       in0=b3[:, :, 15:16].to_broadcast([128, NCH, 16]), in1=b3, op=ALU.subtract),
                      reads=("bT",), writes=("dT",))
                kb.op("act", lambda e: e.activation(out=eb[:], in_=bT[:], func=AF.Exp), reads=("bT",), writes=("eb",))
                kb.op("act", lambda e: e.activation(out=enb[:], in_=bT[:], func=AF.Exp, scale=-1.0), reads=("bT",), writes=("enb",))
                kb.op("act", lambda e: e.activation(out=ed[:], in_=dT[:], func=AF.Exp), reads=("dT",), writes=("ed",))
                kb.op("pool", lambda e: e.tensor_tensor(out=kinv[:], in0=kk[:], in1=enb[:], op=ALU.mult), reads=("kk", "enb"), writes=("kinv",))
                kb.op("pool", lambda e: e.tensor_tensor(out=kdT[:], in0=kk[:], in1=ed[:], op=ALU.mult), reads=("kk", "ed"), writes=("kdT",))
                p, pk = proj_fm(hh * 128)
                kb.op("dve", lambda e: e.scalar_tensor_tensor(out=qd[:], in0=p[:, 0:NTOK], scalar=SCALE, in1=eb[:], op0=ALU.mult, op1=ALU.mult),
                      reads=(pk, "eb"), writes=("qd",))
                p, pk = proj_fm(3072 + hh * 128)
                kb.op("act", lambda e: e.activation(out=sg[:], in_=p[:, 0:NTOK], func=AF.Silu), reads=(pk,), writes=("sg",))
                for tt in range(GT):
                    kb.op("pe", lambda e, tt=tt: e.transpose(out=pst[:, tt, :], in_=kdT[:, tt * 128:(tt + 1) * 128], identity=cm.ident[:]),
                          reads=("kdT", "ident"), writes=("pst",))
                kb.op("act", lambda e: e.activation(out=kd_tok[:, :, :], in_=pst[:, 0:GT, :], func=AF.Copy), reads=("pst",), writes=("kd_tok",))
                for tt in range(GT):
                    tsl = slice(tt * 128, (tt + 1) * 128)
                    vh = v_tok[:, tt, hh * 128:(hh + 1) * 128]
                    kb.op("pe", lambda e: e.matmul(pA[:, :], kinv[:, tsl], qd[:, tsl], start=True, stop=True), reads=("kinv", "qd"), writes=("pA",))
                    kb.op("dve", lambda e: e.tensor_tensor(out=AT[:], in0=pA[:, :], in1=maskT[:], op=ALU.mult), reads=("pA", "maskT"), writes=("AT",))
                    kb.op("pool", lambda e: e.tensor_tensor(out=Vblk[:], in0=vh.unsqueeze(1).to_broadcast([128, 8, 128]),
                                                            in1=cmask[:, :].unsqueeze(2).to_broadcast([128, 8, 128]), op=ALU.mult),
                          reads=("v_tok", "cmask"), writes=("Vblk",))
                    for half in range(2):
                        kb.op("pe", lambda e, half=half: e.matmul(pS[:, half * 512:(half + 1) * 512], kd_tok[:, tt, :],
                                                                  Vblk[:, half * 4:(half + 1) * 4, :].rearrange("p c v -> p (c v)"), start=True, stop=True),
                              reads=("kd_tok", "Vblk"), writes=("pS",))
                    kb.op("act", lambda e: e.activation(out=Sd[:, :, 1:9], in_=pS[:, :].rearrange("k (c v) -> k v c", v=128), func=AF.Copy),
                          reads=("pS",), writes=("Sd",))
                    kb.op("pool", lambda e: e.tensor_copy(Sd[:, :, 0], carry[hh][:]), reads=(("carry", hh), "Sd"), writes=("Sd",))
                    ebv = eb[:].rearrange("p (c k) -> p c k", k=16)
                    kb.op("act", lambda e: e.activation(out=a9[:, :, 1:9], in_=ebv[:, tt * 8:(tt + 1) * 8, 15].unsqueeze(1).to_broadcast([128, 128, 8]), func=AF.Copy), reads=("eb",), writes=("a9",))
                    kb.op("dve", lambda e: e.tensor_tensor_scan(out=So[:].rearrange("k v c -> k (v c)"), data0=a9[:].rearrange("k v c -> k (v c)"),
                                                                data1=Sd[:].rearrange("k v c -> k (v c)"), initial=0.0, op0=ALU.mult, op1=ALU.add),
                          reads=("a9", "Sd"), writes=("So",))
                    kb.op("act", lambda e: e.activation(out=carry[hh][:], in_=So[:, :, 8], func=AF.Copy), reads=("So",), writes=(("carry", hh),))
                    kb.op("pool", lambda e: e.tensor_copy(Sbf[:].rearrange("k c v -> k v c"), So[:, :, 0:8]), reads=("So",), writes=("Sbf",))
                    for c in range(8):
                        csl = slice(16 * c, 16 * c + 16)
                        kb.op("pe", lambda e, c=c: e.matmul(po[:, csl], Sbf[:, c, :], qd[:, tt * 128 + 16 * c:tt * 128 + 16 * c + 16], start=True, stop=False),
                              reads=("Sbf", "qd"), writes=("po",))
                        kb.op("pe", lambda e, c=c: e.matmul(po[:, csl], vh, AT[:, csl], start=False, stop=True),
                              reads=("v_tok", "AT"), writes=("po",))
                    kb.op("act", lambda e: e.activation(out=osq[:], in_=po[:, :], func=AF.Square), reads=("po",), writes=("osq",))
                    kb.op("pe", lambda e: e.matmul(pss[:, :], ones[:], osq[:], start=True, stop=True), reads=("ones", "osq"), writes=("pss",))
                    kb.op("act", lambda e: e.activation(out=sdv[:], in_=pss[:, :], func=AF.Sqrt, bias=epst[:, 0:1], scale=1.0 / 128.0),
                          reads=("pss", "epst"), writes=("sdv",))
                    kb.op("dve", lambda e: e.reciprocal(out=rsv[:], in_=sdv[:]), reads=("sdv",), writes=("rsv",))
                    kb.op("dve", lambda e: e.tensor_tensor(out=t1[:], in0=po[:, :], in1=rsv[:], op=ALU.mult), reads=("po", "rsv"), writes=("t1",))
                    kb.op("dve", lambda e: e.scalar_tensor_tensor(out=yT[:, hh, tsl], in0=t1[:], scalar=gn[:, 0:1], in1=sg[:, tsl],
                                                                  op0=ALU.mult, op1=ALU.mult), reads=("t1", "gn", "sg"), writes=("yT",))
            for tt in range(GT):
                t = g * GT + tt
                hb = hb2[t % 2]
                hk = ("hb", t % 2)
                for cg in range(2):
                    for hh in range(8):
                        kb.op("pe", lambda e, hh=hh: e.matmul(pS[:, cg * 512:(cg + 1) * 512], yT[:, hh, tt * 128:(tt + 1) * 128],
                                                              w_out[:, hh, cg * 512:(cg + 1) * 512], start=(hh == 0), stop=(hh == 7)),
                              reads=("yT", "w_out"), writes=("pS",))
                kb.dma("sp", lambda e: e.dma_start(out=hb[:], in_=hsrc[t * 128:(t + 1) * 128, :]), reads=((hsrc_name, t),), writes=(hk,))
                for cg in range(2):
                    kb.op("dve", lambda e: e.tensor_tensor(out=hb[:, cg * 512:(cg + 1) * 512], in0=pS[:, cg * 512:(cg + 1) * 512],
                                                           in1=hb[:, cg * 512:(cg + 1) * 512], op=ALU.add), reads=("pS", hk), writes=(hk,))
                kb.dma("sp", lambda e: e.dma_start(out=hdst[t * 128:(t + 1) * 128, :], in_=hb[:]), reads=(hk,), writes=((hdst_name, t),))


        kb.barrier()
def fox_phase(kb, cm, li, hsrc, hsrc_name, hdst, hdst_name, W, scr_d):
    NT = cm.NT
    T = cm.T
    GT = min(4, NT)
    NTOK = GT * 128
    qT_d, kT_d, v_d, ca_d, o_d = scr_d["qT"], scr_d["kT"], scr_d["v"], scr_d["ca"], scr_d["o"]
    with contextlib.ExitStack() as es:
        sb = lambda n, s, d: kb.sb(n, s, d, es)
        ps = lambda n, s, d: kb.ps(n, s, d, es)
        w_in = sb("fw_in", [128, 8, 3072], BF16)
        w_f = sb("fw_f", [128, 8, 16], BF16)
        gb = sb("gb", [128, D], F32)
        bfb = sb("bfb", [128, 16], F32)
        tri = sb("tri", [128, 128], F32)
        onesf = sb("onesf", [128, 128], F32)
        hb2 = [sb("hb", [128, D], F32) for _ in range(2)]
        hn = sb("hn", [128, D], F32)
        hnb = sb("hnb", [128, D], BF16)
        scr = sb("scr", [128, D], F32)
        stat = sb("stat", [128, 4], F32)
        hnT = sb("hnT", [128, 8, NTOK], BF16)
        ob = [sb("ob", [128, NTOK], BF16) for _ in range(2)]
        vb = [sb("vb", [128, 1024], BF16) for _ in range(2)]
        fz = sb("fz", [128, 16], F32)
        fe = sb("fe", [128, 16], F32)
        lf = sb("lf", [128, 16], F32)
        negc = sb("negc", [128, 16], F32)
        carry_b = sb("carry_b", [128, 16], F32)
        ctok = sb("ctok", [128, 16], F32)
        negct = sb("negct", [128, 16], F32)
        identf = sb("identf", [128, 128], F32)
        cT = sb("cT", [16, NTOK], F32)
        hi = sb("hi", [16, NTOK], BF16)
        hi32 = sb("hi32", [16, NTOK], F32)
        r1 = sb("r1", [16, NTOK], F32)
        mid = sb("mid", [16, NTOK], BF16)
        mid32 = sb("mid32", [16, NTOK], F32)
        r2 = sb("r2", [16, NTOK], F32)
        lo = sb("lo", [16, NTOK], BF16)
        pst = ps("pst", [128, 8, 128], BF16)
        pp = [ps("pp", [128, 512], F32) for _ in range(2)]
        pf = ps("pf", [128, 16], F32)
        pc = ps("pc", [16, NTOK], F32)
        pct = ps("pct", [128, 16], F32)
        ptot = ps("ptot", [128, 16], F32)

        for c in range(8):
            kb.dma("pool", lambda e, c=c: e.dma_start(out=w_in[:, c, :], in_=W["fox_w_in"][0, c * 128:(c + 1) * 128, 0:3072]), writes=("fw_in",))
            kb.dma("pool", lambda e, c=c: e.dma_start(out=w_f[:, c, :], in_=W["fox_w_in"][0, c * 128:(c + 1) * 128, 3072:3088]), writes=("fw_f",))
        kb.dma("sp", lambda e: e.dma_start(out=gb[:], in_=W["norm_mix"][li].partition_broadcast(128)), writes=("gb",))
        kb.dma("sp", lambda e: e.dma_start(out=bfb[:], in_=W["fox_b_f"][0].partition_broadcast(128)), writes=("bfb",))
        kb.dma("sp", lambda e: e.dma_start(out=tri[:], in_=W["c_tri"][:, :]), writes=("tri",))
        kb.dma("sp", lambda e: e.dma_start(out=onesf[:], in_=W["c_ones"][:, :]), writes=("onesf",))
        kb.op("dve", lambda e: e.memset(carry_b[:], 0.0), writes=("carry_b",))
        kb.dma("sp", lambda e: e.dma_start(out=identf[:], in_=cm.ident_d[:, :]), writes=("identf",))
        ppi = [0]
        for g in range(NT // GT):
            for tt in range(GT):
                t = g * GT + tt
                load_norm_T(kb, cm, t, hsrc, hsrc_name, hb2[t % 2], ("hb", t % 2), gb, "gb", hn, hnb, scr, stat, pst,
                            hnT[:, :, tt * 128:(tt + 1) * 128], "hnT")
            for which, dst in ((0, qT_d), (1, kT_d)):
                for ch in range(8):
                    b = ppi[0] % 2
                    ppi[0] += 1
                    col0 = which * 1024 + ch * 128
                    for c in range(8):
                        kb.op("pe", lambda e, c=c: e.matmul(pp[b][:, 0:NTOK], w_in[:, c, col0:col0 + 128], hnT[:, c, :], start=(c == 0), stop=(c == 7)),
                              reads=("fw_in", "hnT"), writes=(("pp", b),))
                    kb.op("act", lambda e: e.activation(out=ob[b][:], in_=pp[b][:, 0:NTOK], func=AF.Copy, scale=(0.125 if which == 0 else 1.0)),
                          reads=(("pp", b),), writes=(("ob", b),))
                    kb.dma("sp", lambda e: e.dma_start(out=dst[ch * 128:(ch + 1) * 128, g * NTOK:(g + 1) * NTOK], in_=ob[b][:]),
                           reads=(("ob", b),), writes=(("qk_d", which, ch, g),))
            for tt in range(GT):
                t = g * GT + tt
                vbb = vb[t % 2]
                for cg in range(2):
                    b = ppi[0] % 2
                    ppi[0] += 1
                    for c in range(8):
                        kb.op("pe", lambda e, c=c: e.matmul(pp[b][:, :], hnT[:, c, tt * 128:(tt + 1) * 128],
                                                            w_in[:, c, 2048 + cg * 512:2048 + (cg + 1) * 512], start=(c == 0), stop=(c == 7)),
                              reads=("fw_in", "hnT"), writes=(("pp", b),))
                    kb.op("act", lambda e: e.activation(out=vbb[:, cg * 512:(cg + 1) * 512], in_=pp[b][:, :], func=AF.Copy),
                          reads=(("pp", b),), writes=(("vb", t % 2),))
                kb.dma("sp", lambda e: e.dma_start(out=v_d[t * 128:(t + 1) * 128, :], in_=vbb[:]), reads=(("vb", t % 2),), writes=(("v_d", t),))
                for c in range(8):
                    kb.op("pe", lambda e, c=c: e.matmul(pf[:, :], hnT[:, c, tt * 128:(tt + 1) * 128], w_f[:, c, :], start=(c == 0), stop=(c == 7)),
                          reads=("fw_f", "hnT"), writes=("pf",))
                kb.op("dve", lambda e: e.tensor_tensor(out=fz[:], in0=pf[:, :], in1=bfb[:], op=ALU.add), reads=("pf", "bfb"), writes=("fz",))
                kb.op("act", lambda e: e.activation(out=fe[:], in_=fz[:], func=AF.Exp, scale=-1.0), reads=("fz",), writes=("fe",))
                kb.op("dve", lambda e: e.tensor_scalar(out=fe[:], in0=fe[:], scalar1=1.0, scalar2=None, op0=ALU.add), reads=("fe",), writes=("fe",))
                kb.op("act", lambda e: e.activation(out=lf[:], in_=fe[:], func=AF.Ln), reads=("fe",), writes=("lf",))
                kb.op("dve", lambda e: e.tensor_scalar(out=lf[:], in0=lf[:], scalar1=-1.0, scalar2=None, op0=ALU.mult), reads=("lf",), writes=("lf",))
                kb.op("pe", lambda e: e.matmul(pct[:, :], tri[:], lf[:], start=True, stop=True), reads=("lf", "tri"), writes=("pct",))
                kb.op("dve", lambda e: e.tensor_tensor(out=ctok[:], in0=pct[:, :], in1=carry_b[:], op=ALU.add), reads=("pct", "carry_b"), writes=("ctok",))
                kb.op("pe", lambda e: e.matmul(ptot[:, :], onesf[:], lf[:], start=True, stop=True), reads=("lf", "onesf"), writes=("ptot",))
                kb.op("dve", lambda e: e.tensor_tensor(out=carry_b[:], in0=carry_b[:], in1=ptot[:, :], op=ALU.add), reads=("ptot", "carry_b"), writes=("carry_b",))
                kb.op("act", lambda e: e.activation(out=negct[:], in_=ctok[:], func=AF.Copy, scale=-1.0), reads=("ctok",), writes=("negct",))
                kb.dma("sp", lambda e: e.dma_start(out=scr_d["negc"][t * 128:(t + 1) * 128, :], in_=negct[:]), reads=("negct",), writes=(("negc_d", t),))
                kb.op("pe", lambda e: e.transpose(out=pc[:, tt * 128:(tt + 1) * 128], in_=ctok[:], identity=identf[:]), reads=("ctok", "identf"), writes=("pc",))
            kb.op("act", lambda e: e.activation(out=cT[:], in_=pc[:, :], func=AF.Copy), reads=("pc",), writes=("cT",))
            kb.op("dve", lambda e: e.tensor_copy(hi[:], cT[:]), reads=("cT",), writes=("hi",))
            kb.op("dve", lambda e: e.tensor_copy(hi32[:], hi[:]), reads=("hi",), writes=("hi32",))
            kb.op("dve", lambda e: e.tensor_tensor(out=r1[:], in0=cT[:], in1=hi32[:], op=ALU.subtract), reads=("cT", "hi32"), writes=("r1",))
            kb.op("dve", lambda e: e.tensor_copy(mid[:], r1[:]), reads=("r1",), writes=("mid",))
            kb.op("dve", lambda e: e.tensor_copy(mid32[:], mid[:]), reads=("mid",), writes=("mid32",))
            kb.op("dve", lambda e: e.tensor_tensor(out=r2[:], in0=r1[:], in1=mid32[:], op=ALU.subtract), reads=("r1", "mid32"), writes=("r2",))
            kb.op("dve", lambda e: e.tensor_copy(lo[:], r2[:]), reads=("r2",), writes=("lo",))
            for k3, src_t, sk in ((0, hi, "hi"), (1, mid, "mid"), (2, lo, "lo")):
                kb.dma("sp", lambda e: e.dma_start(out=ca_d[:, k3, g * NTOK:(g + 1) * NTOK], in_=src_t[:]), reads=(sk,), writes=(("ca_d", g),))

        kb.barrier()
    allqk = tuple(("qk_d", w, ch, g) for w in range(2) for ch in range(8) for g in range(NT // GT))
    allv = tuple(("v_d", t) for t in range(NT))
    allca = tuple(("ca_d", g) for g in range(NT // GT))
    allnegc = tuple(("negc_d", t) for t in range(NT))
    with contextlib.ExitStack() as es:
        sb = lambda n, s, d: kb.sb(n, s, d, es)
        ps = lambda n, s, d: kb.ps(n, s, d, es)
        q_aug = [sb("q_aug", [67, T], BF16) for _ in range(2)]
        k_aug = [sb("k_aug", [67, T], BF16) for _ in range(2)]
        V_aug = [sb("V_aug", [128, NT, 65], BF16) for _ in range(2)]
        negc = sb("negc", [128, NT, 16], F32)
        identb = cm.ident
        nmask = sb("nmask", [128, 128], BF16)
        PT = [sb("PT", [128, 512], BF16) for _ in range(2)]
        rcp = sb("rcp", [128, 1], F32)
        otk = [sb("otk", [128, 64], BF16) for _ in range(2)]
        pS = [ps("pS", [128, 512], F32) for _ in range(2)]
        pO = [ps("pO", [128, 65], F32) for _ in range(4)]
        kb.dma("pool", lambda e: e.dma_start(out=nmask[:], in_=W["c_negmask"][:, :]), writes=("nmask",))
        kb.dma("sp", lambda e: e.dma_start(out=negc[:], in_=scr_d["negc"][:, :].rearrange("(t p) h -> p t h", p=128)), reads=allnegc, writes=("negc",))
        si = [0]
        for hh in range(16):
            hb_ = hh % 2
            qa, ka, va = q_aug[hb_], k_aug[hb_], V_aug[hb_]
            qk_, kk_, vk_ = ("q_aug", hb_), ("k_aug", hb_), ("V_aug", hb_)
            kb.dma("sp", lambda e: e.dma_start(out=qa[0:64, :], in_=qT_d[hh * 64:(hh + 1) * 64, :]), reads=allqk, writes=(qk_,))
            kb.dma("sp", lambda e: e.dma_start(out=qa[64:67, :], in_=ca_d[hh, :, :]), reads=allca, writes=(qk_,))
            kb.dma("sp", lambda e: e.dma_start(out=ka[0:64, :], in_=kT_d[hh * 64:(hh + 1) * 64, :]), reads=allqk, writes=(kk_,))
            kb.dma("pool", lambda e: e.dma_start(out=ka[64:67, :], in_=W["c_ones3"][:, 0:T]), writes=(kk_,))
            kb.dma("sp", lambda e: e.dma_start(out=va[:, :, 0:64], in_=v_d[:, hh * 64:(hh + 1) * 64].rearrange("(t p) d -> p t d", p=128)),
                   reads=allv, writes=(vk_,))
            kb.dma("pool", lambda e: e.dma_start(out=va[:, :, 64:65], in_=W["c_ones3"][0:1, 0:NT * 128].rearrange("o (t p) -> p t o", p=128),
                                                 allow_slow_non_contiguous=True), writes=(vk_,))
            for i in range(NT // GT):
                nj = GT * i + GT
                c0 = i * NTOK

                def emit_S(j):
                    r = max(0, j - GT * i)
                    b = si[0] % 2
                    si[0] += 1
                    lhs = ka[0:67, j * 128:(j + 1) * 128]
                    if j >= GT * i:
                        kb.op("pe", lambda e: e.matmul(pS[b][:, r * 128:(r + 1) * 128], lhs, qa[0:67, c0 + r * 128:c0 + (r + 1) * 128], start=True, stop=False),
                              reads=(qk_, kk_), writes=(("pS", b),))
                        kb.op("pe", lambda e: e.matmul(pS[b][:, r * 128:(r + 1) * 128], identb[:], nmask[:], start=False, stop=True),
                              reads=("ident", "nmask"), writes=(("pS", b),))
                        if r < GT - 1:
                            kb.op("pe", lambda e: e.matmul(pS[b][:, (r + 1) * 128:NTOK], lhs, qa[0:67, c0 + (r + 1) * 128:c0 + NTOK], start=True, stop=True),
                                  reads=(qk_, kk_), writes=(("pS", b),))
                    else:
                        kb.op("pe", lambda e: e.matmul(pS[b][:, 0:NTOK], lhs, qa[0:67, c0:c0 + NTOK], start=True, stop=True),
                              reads=(qk_, kk_), writes=(("pS", b),))
                    return b, r

                nxt = emit_S(0)
                for j in range(nj):
                    b, r = nxt
                    if j + 1 < nj:
                        nxt = emit_S(j + 1)
                    kb.op("act", lambda e: e.activation(out=PT[b][:, r * 128:NTOK], in_=pS[b][:, r * 128:NTOK], func=AF.Exp,
                                                        bias=negc[:, j, hh:hh + 1], scale=1.0), reads=(("pS", b), "negc"), writes=(("PT", b),))
                    for rr in range(r, GT):
                        kb.op("pe", lambda e, rr=rr: e.matmul(pO[rr][:, :], PT[b][:, rr * 128:(rr + 1) * 128], va[:, j, :], start=(j == 0), stop=(j == GT * i + rr)),
                              reads=(("PT", b), vk_), writes=(("pO", rr),))
                for rr in range(GT):
                    t = GT * i + rr
                    ob_ = otk[t % 2]
                    kb.op("dve", lambda e: e.reciprocal(out=rcp[:], in_=pO[rr][:, 64:65]), reads=(("pO", rr),), writes=("rcp",))
                    kb.op("dve", lambda e: e.tensor_scalar(out=ob_[:], in0=pO[rr][:, 0:64], scalar1=rcp[:, 0:1], scalar2=None, op0=ALU.mult),
                          reads=(("pO", rr), "rcp"), writes=(("otk", t % 2),))
                    kb.dma("sp", lambda e: e.dma_start(out=o_d[t * 128:(t + 1) * 128, hh * 64:(hh + 1) * 64], in_=ob_[:]),
                           reads=(("otk", t % 2),), writes=(("o_d", t, hh),))

        kb.barrier()
    with contextlib.ExitStack() as es:
        sb = lambda n, s, d: kb.sb(n, s, d, es)
        ps = lambda n, s, d: kb.ps(n, s, d, es)
        w_out = sb("fw_out", [128, 8, 1024], BF16)
        hb2 = [sb("hb", [128, D], F32) for _ in range(2)]
        o_t = [sb("o_t", [128, D], BF16) for _ in range(2)]
        oT = sb("oT", [128, 8, 128], BF16)
        pst = ps("pst", [128, 8, 128], BF16)
        pm = ps("pm", [128, 1024], F32)
        for c in range(8):
            kb.dma("pool", lambda e, c=c: e.dma_start(out=w_out[:, c, :], in_=W["fox_w_out"][0, c * 128:(c + 1) * 128, :]), writes=("fw_out",))
        for t in range(NT):
            b = t % 2
            kb.dma("sp", lambda e: e.dma_start(out=o_t[b][:], in_=o_d[t * 128:(t + 1) * 128, :]),
                   reads=tuple(("o_d", t, hh) for hh in range(16)), writes=(("o_t", b),))
            kb.dma("sp", lambda e: e.dma_start(out=hb2[b][:], in_=hsrc[t * 128:(t + 1) * 128, :]), reads=((hsrc_name, t),), writes=(("hb", b),))
            transpose_tile(kb, cm, o_t[b], ("o_t", b), oT[:, :, :], "oT", pst, "pst")
            for cg in range(2):
                for c in range(8):
                    kb.op("pe", lambda e, c=c: e.matmul(pm[:, cg * 512:(cg + 1) * 512], oT[:, c, :], w_out[:, c, cg * 512:(cg + 1) * 512], start=(c == 0), stop=(c == 7)),
                          reads=("oT", "fw_out"), writes=("pm",))
                kb.op("dve", lambda e: e.tensor_tensor(out=hb2[b][:, cg * 512:(cg + 1) * 512], in0=pm[:, cg * 512:(cg + 1) * 512],
                                                       in1=hb2[b][:, cg * 512:(cg + 1) * 512], op=ALU.add), reads=("pm", ("hb", b)), writes=(("hb", b),))
            kb.dma("sp", lambda e: e.dma_start(out=hdst[t * 128:(t + 1) * 128, :], in_=hb2[b][:]), reads=(("hb", b),), writes=((hdst_name, t),))


        kb.barrier()
def ssd_phase(kb, cm, li, hsrc, hsrc_name, hdst, hdst_name, W, y_d):
    NT = cm.NT
    GT = min(2, NT)
    NTOK = GT * 128
    with contextlib.ExitStack() as es:
        sb = lambda n, s, d: kb.sb(n, s, d, es)
        ps = lambda n, s, d: kb.ps(n, s, d, es)
        w_x = sb("w_x", [128, 8, 4096], BF16)
        w_dt = sb("w_dt", [128, 8, 32], BF16)
        gb = sb("gb", [128, D], F32)
        cw = sb("cw", [128, 32, 4], F32)
        cbias = sb("cbias", [128, 32], F32)
        dtb = sb("dtb", [128, 32], F32)
        aneg = sb("aneg", [128, 32], F32)
        Db = sb("Db", [128, 32], F32)
        tri = sb("tri", [128, 128], F32)
        onesf = sb("onesf", [128, 128], F32)
        identf = sb("identf", [128, 128], F32)
        nmaskf = sb("nmaskf", [128, 128], F32)
        cmaskT = sb("cmaskT", [128, 128], F32)
        hb2 = [sb("hb", [128, D], F32) for _ in range(2)]
        hn = sb("hn", [128, D], F32)
        hnb = sb("hnb", [128, D], BF16)
        scr = sb("scr", [128, D], F32)
        stat = sb("stat", [128, 4], F32)
        hnT = sb("hnT", [128, 8, NTOK], BF16)
        xp = [sb("xp", [128, NTOK + 3], F32) for _ in range(2)]
        acc = [sb("acc", [128, NTOK], F32) for _ in range(2)]
        halo = sb("halo", [128, 32, 3], F32)
        xc = sb("xc", [128, 32, NTOK], BF16)
        x_tok = sb("x_tok", [128, GT, 2048], BF16)
        B_tok = sb("B_tok", [128, GT, 1024], BF16)
        xb = sb("xb", [128, 32], F32)
        dtt = sb("dtt", [128, 32], F32)
        dA = sb("dA", [128, 32], F32)
        cum = sb("cum", [128, 32], F32)
        negcum = sb("negcum", [128, 32], F32)
        dd = sb("dd", [128, 32], F32)
        dec_end = sb("dec_end", [128, 32], F32)
        ecum = sb("ecum", [128, 32], F32)
        etot = sb("etot", [128, 32], F32)
        xdt = sb("xdt", [128, 2048], BF16)
        xdd = sb("xdd", [128, 2048], BF16)
        cbm = sb("cbm", [128, 128], F32)
        tsc = sb("tsc", [128, 128], F32)
        LT = sb("LT", [128, 128], F32)
        MT = sb("MT", [128, 128], BF16)
        S = sb("S", [128, 32, 64], F32)
        S_bf = sb("S_bf", [128, 32, 64], BF16)
        yi = sb("yi", [128, 256], F32)
        y_sb = sb("y_sb", [128, 2048], F32)
        tmp = sb("tmp", [128, 2048], F32)
        pst = ps("pst", [128, 8, 128], BF16)
        pp = [ps("pp", [128, 512], F32) for _ in range(2)]
        pdc = ps("pdc", [128, 96], F32)
        pcb = ps("pcb", [128, 128], F32)
        pcr = ps("pcr", [128, 128], F32)
        py = ps("py", [128, 512], F32)
        pSu = ps("pSu", [128, 256], F32)

        for c in range(8):
            kb.dma("pool", lambda e, c=c: e.dma_start(out=w_x[:, c, :], in_=W["ssm_w_in"][0, c * 128:(c + 1) * 128, 2048:6144]), writes=("w_x",))
            kb.dma("pool", lambda e, c=c: e.dma_start(out=w_dt[:, c, :], in_=W["ssm_w_in"][0, c * 128:(c + 1) * 128, 6144:6176]), writes=("w_dt",))
        kb.dma("sp", lambda e: e.dma_start(out=gb[:], in_=W["norm_mix"][li].partition_broadcast(128)), writes=("gb",))
        kb.dma("sp", lambda e: e.dma_start(out=cw[:], in_=W["ssm_conv_wT"][:, :, :]), writes=("cw",))
        kb.dma("sp", lambda e: e.dma_start(out=cbias[:], in_=W["ssm_conv_bT"][:, :]), writes=("cbias",))
        kb.dma("sp", lambda e: e.dma_start(out=dtb[:], in_=W["ssm_dt_bias"][0].partition_broadcast(128)), writes=("dtb",))
        kb.dma("sp", lambda e: e.dma_start(out=aneg[:], in_=W["ssm_a_log"][0].partition_broadcast(128)), writes=("aneg",))
        kb.dma("sp", lambda e: e.dma_start(out=Db[:], in_=W["ssm_d"][0].partition_broadcast(128)), writes=("Db",))
        kb.dma("sp", lambda e: e.dma_start(out=tri[:], in_=W["c_tri"][:, :]), writes=("tri",))
        kb.dma("sp", lambda e: e.dma_start(out=cmaskT[:], in_=W["c_tri"][:, :]), writes=("cmaskT",))
        kb.dma("sp", lambda e: e.dma_start(out=onesf[:], in_=W["c_ones"][:, :]), writes=("onesf",))
        kb.dma("sp", lambda e: e.dma_start(out=identf[:], in_=cm.ident_d[:, :]), writes=("identf",))
        kb.dma("sp", lambda e: e.dma_start(out=nmaskf[:], in_=W["c_negmask"][:, :]), writes=("nmaskf",))
        kb.op("act", lambda e: e.activation(out=aneg[:], in_=aneg[:], func=AF.Exp), reads=("aneg",), writes=("aneg",))
        kb.op("dve", lambda e: e.tensor_scalar(out=aneg[:], in0=aneg[:], scalar1=-1.0, scalar2=None, op0=ALU.mult), reads=("aneg",), writes=("aneg",))
        kb.op("dve", lambda e: e.memset(halo[:], 0.0), writes=("halo",))
        kb.op("dve", lambda e: e.memset(S[:], 0.0), writes=("S",))
        kb.op("dve", lambda e: e.memset(S_bf[:], 0.0), writes=("S_bf",))
        ppi = [0]
        for g2 in range(NT // GT):
            for tt in range(GT):
                t = g2 * GT + tt
                load_norm_T(kb, cm, t, hsrc, hsrc_name, hb2[t % 2], ("hb", t % 2), gb, "gb", hn, hnb, scr, stat, pst,
                            hnT[:, :, tt * 128:(tt + 1) * 128], "hnT")
            for ch in range(32):
                b = ppi[0] % 2
                ppi[0] += 1
                for c in range(8):
                    kb.op("pe", lambda e, c=c: e.matmul(pp[b][:, 0:NTOK], w_x[:, c, ch * 128:(ch + 1) * 128], hnT[:, c, :], start=(c == 0), stop=(c == 7)),
                          reads=("w_x", "hnT"), writes=(("pp", b),))
                xk, ak = ("xp", b), ("acc", b)
                kb.op("act", lambda e: e.activation(out=xp[b][:, 3:3 + NTOK], in_=pp[b][:, 0:NTOK], func=AF.Copy), reads=(("pp", b),), writes=(xk,))
                kb.op("pool", lambda e: e.tensor_copy(xp[b][:, 0:3], halo[:, ch, :]), reads=("halo", xk), writes=(xk,))
                kb.op("dve", lambda e: e.tensor_scalar(out=acc[b][:], in0=xp[b][:, 0:NTOK], scalar1=cw[:, ch, 0:1], scalar2=cbias[:, ch:ch + 1],
                                                       op0=ALU.mult, op1=ALU.add), reads=(xk, "cw", "cbias"), writes=(ak,))
                for k in range(1, 4):
                    kb.op("dve", lambda e, k=k: e.scalar_tensor_tensor(out=acc[b][:], in0=xp[b][:, k:k + NTOK], scalar=cw[:, ch, k:k + 1], in1=acc[b][:],
                                                                       op0=ALU.mult, op1=ALU.add), reads=(xk, "cw", ak), writes=(ak,))
                kb.op("pool", lambda e: e.tensor_copy(halo[:, ch, :], xp[b][:, NTOK:NTOK + 3]), reads=(xk, "halo"), writes=("halo",))
                kb.op("act", lambda e: e.activation(out=xc[:, ch, :], in_=acc[b][:], func=AF.Silu), reads=(ak,), writes=(("xc", ch),))
            allxc = tuple(("xc", ch) for ch in range(32))
            for tt in range(GT):
                for blk in range(3):
                    for cc in range(8):
                        ch = blk * 8 + cc
                        kb.op("pe", lambda e, cc=cc, ch=ch: e.transpose(out=pst[:, cc, :], in_=xc[:, ch, tt * 128:(tt + 1) * 128], identity=cm.ident[:]),
                              reads=(("xc", ch), "ident"), writes=("pst",))
                    if blk < 2:
                        dst = x_tok[:, tt, blk * 1024:(blk + 1) * 1024].rearrange("p (c k) -> p c k", k=128)
                        kb.op("act", lambda e: e.activation(out=dst, in_=pst[:, :, :], func=AF.Copy), reads=("pst",), writes=("x_tok",))
                    else:
                        dst = B_tok[:, tt, :].rearrange("p (c k) -> p c k", k=128)
                        kb.op("act", lambda e: e.activation(out=dst, in_=pst[:, :, :], func=AF.Copy), reads=("pst",), writes=("B_tok",))
            for tt in range(GT):
                t = g2 * GT + tt
                tsl = slice(tt * 128, (tt + 1) * 128)
                for c in range(8):
                    kb.op("pe", lambda e, c=c: e.matmul(pdc[:, 0:32], hnT[:, c, tsl], w_dt[:, c, :], start=(c == 0), stop=(c == 7)),
                          reads=("w_dt", "hnT"), writes=("pdc",))
                kb.op("dve", lambda e: e.tensor_tensor(out=xb[:], in0=pdc[:, 0:32], in1=dtb[:], op=ALU.add), reads=("pdc", "dtb"), writes=("xb",))
                kb.op("act", lambda e: e.activation(out=xb[:], in_=xb[:], func=AF.Exp), reads=("xb",), writes=("xb",))
                kb.op("dve", lambda e: e.tensor_scalar(out=xb[:], in0=xb[:], scalar1=1.0, scalar2=None, op0=ALU.add), reads=("xb",), writes=("xb",))
                kb.op("act", lambda e: e.activation(out=dtt[:], in_=xb[:], func=AF.Ln), reads=("xb",), writes=("dtt",))
                kb.op("dve", lambda e: e.tensor_tensor(out=dA[:], in0=dtt[:], in1=aneg[:], op=ALU.mult), reads=("dtt", "aneg"), writes=("dA",))
                kb.op("pe", lambda e: e.matmul(pdc[:, 32:64], tri[:], dA[:], start=True, stop=True), reads=("tri", "dA"), writes=("pdc",))
                kb.op("pe", lambda e: e.matmul(pdc[:, 64:96], onesf[:], dA[:], start=True, stop=True), reads=("onesf", "dA"), writes=("pdc",))
                kb.op("act", lambda e: e.activation(out=cum[:], in_=pdc[:, 32:64], func=AF.Copy), reads=("pdc",), writes=("cum",))
                kb.op("act", lambda e: e.activation(out=negcum[:], in_=pdc[:, 32:64], func=AF.Copy, scale=-1.0), reads=("pdc",), writes=("negcum",))
                kb.op("act", lambda e: e.activation(out=ecum[:], in_=pdc[:, 32:64], func=AF.Exp), reads=("pdc",), writes=("ecum",))
                kb.op("act", lambda e: e.activation(out=etot[:], in_=pdc[:, 64:96], func=AF.Exp), reads=("pdc",), writes=("etot",))
                kb.op("dve", lambda e: e.tensor_tensor(out=dd[:], in0=pdc[:, 64:96], in1=cum[:], op=ALU.subtract), reads=("pdc", "cum"), writes=("dd",))
                kb.op("act", lambda e: e.activation(out=dec_end[:], in_=dd[:], func=AF.Exp), reads=("dd",), writes=("dec_end",))
                x3 = x_tok[:, tt, :].rearrange("p (h d) -> p h d", d=64)
                kb.op("dve", lambda e: e.tensor_tensor(out=xdt[:].rearrange("p (h d) -> p h d", d=64), in0=x3,
                                                       in1=dtt[:, :].unsqueeze(2).to_broadcast([128, 32, 64]), op=ALU.mult), reads=("x_tok", "dtt"), writes=("xdt",))
                kb.op("pool", lambda e: e.tensor_tensor(out=xdd[:].rearrange("p (h d) -> p h d", d=64), in0=xdt[:].rearrange("p (h d) -> p h d", d=64),
                                                        in1=dec_end[:, :].unsqueeze(2).to_broadcast([128, 32, 64]), op=ALU.mult), reads=("xdt", "dec_end"), writes=("xdd",))
                for g in range(8):
                    BT = xc[:, 16 + g, tsl]
                    CT = xc[:, 24 + g, tsl]
                    kb.op("pe", lambda e: e.matmul(pcb[:, :], BT, CT, start=True, stop=True), reads=allxc, writes=("pcb",))
                    kb.op("dve", lambda e: e.tensor_tensor(out=cbm[:], in0=pcb[:, :], in1=cmaskT[:], op=ALU.mult), reads=("pcb", "cmaskT"), writes=("cbm",))
                    for h4 in range(4):
                        h = 4 * g + h4
                        hs = slice(h * 64, (h + 1) * 64)
                        kb.op("pool", lambda e: e.tensor_scalar(out=tsc[:], in0=tri[:], scalar1=dA[:, h:h + 1], scalar2=None, op0=ALU.mult),
                              reads=("tri", "dA"), writes=("tsc",))
                        kb.op("pe", lambda e: e.matmul(pcr[:, :], onesf[:], tsc[:], start=True, stop=False), reads=("onesf", "tsc"), writes=("pcr",))
                        kb.op("pe", lambda e: e.matmul(pcr[:, :], identf[:], nmaskf[:], start=False, stop=True), reads=("identf", "nmaskf"), writes=("pcr",))
                        kb.op("act", lambda e: e.activation(out=LT[:], in_=pcr[:, :], func=AF.Exp, bias=negcum[:, h:h + 1], scale=1.0),
                              reads=("pcr", "negcum"), writes=("LT",))
                        kb.op("dve", lambda e: e.tensor_tensor(out=MT[:], in0=LT[:], in1=cbm[:], op=ALU.mult), reads=("LT", "cbm"), writes=("MT",))
                        kb.op("pe", lambda e: e.matmul(py[:, h4 * 64:(h4 + 1) * 64], MT[:], xdt[:, hs], start=True, stop=True), reads=("MT", "xdt"), writes=("py",))
                        kb.op("pe", lambda e: e.matmul(py[:, 256 + h4 * 64:256 + (h4 + 1) * 64], CT, S_bf[:, h, :], start=True, stop=True),
                              reads=allxc + ("S_bf",), writes=("py",))
                        kb.op("pe", lambda e: e.matmul(pSu[:, h4 * 64:(h4 + 1) * 64], B_tok[:, tt, g * 128:(g + 1) * 128], xdd[:, hs], start=True, stop=True),
                              reads=("B_tok", "xdd"), writes=("pSu",))
                    kb.op("act", lambda e: e.activation(out=yi[:], in_=py[:, 0:256], func=AF.Copy), reads=("py",), writes=("yi",))
                    for h4 in range(4):
                        h = 4 * g + h4
                        kb.op("dve", lambda e: e.scalar_tensor_tensor(out=y_sb[:, h * 64:(h + 1) * 64], in0=py[:, 256 + h4 * 64:256 + (h4 + 1) * 64],
                                                                      scalar=ecum[:, h:h + 1], in1=yi[:, h4 * 64:(h4 + 1) * 64], op0=ALU.mult, op1=ALU.add),
                              reads=("py", "ecum", "yi"), writes=("y_sb",))
                    Sg = S[:, 4 * g:4 * g + 4, :]
                    kb.op("dve", lambda e: e.tensor_tensor(out=Sg, in0=Sg, in1=etot[:, 4 * g:4 * g + 4].unsqueeze(2).to_broadcast([128, 4, 64]), op=ALU.mult),
                          reads=("S", "etot"), writes=("S",))
                    kb.op("dve", lambda e: e.tensor_tensor(out=Sg, in0=Sg, in1=pSu[:, :].rearrange("p (h d) -> p h d", d=64), op=ALU.add),
                          reads=("S", "pSu"), writes=("S",))
                    kb.op("act", lambda e: e.activation(out=S_bf[:, 4 * g:4 * g + 4, :], in_=Sg, func=AF.Copy), reads=("S",), writes=("S_bf",))
                kb.op("pool", lambda e: e.tensor_tensor(out=tmp[:].rearrange("p (h d) -> p h d", d=64), in0=x3,
                                                        in1=Db[:, :].unsqueeze(2).to_broadcast([128, 32, 64]), op=ALU.mult), reads=("x_tok", "Db"), writes=("tmp",))
                kb.op("dve", lambda e: e.tensor_tensor(out=y_sb[:], in0=y_sb[:], in1=tmp[:], op=ALU.add), reads=("y_sb", "tmp"), writes=("y_sb",))
                kb.dma("sp", lambda e: e.dma_start(out=y_d[t * 128:(t + 1) * 128, :], in_=y_sb[:]), reads=("y_sb",), writes=(("y_d", t),))
        kb.barrier()
    with contextlib.ExitStack() as es:
        sb = lambda n, s, d: kb.sb(n, s, d, es)
        ps = lambda n, s, d: kb.ps(n, s, d, es)
        w_z = sb("w_z", [128, 8, 2048], BF16)
        w_out = sb("sw_out", [128, 16, 1024], BF16)
        gb = sb("gb", [128, D], F32)
        gnb = sb("gnb", [128, 2048], F32)
        hb2 = [sb("hb", [128, D], F32) for _ in range(2)]
        hn = sb("hn", [128, D], F32)
        hnb = sb("hnb", [128, D], BF16)
        scr = sb("scr", [128, D], F32)
        stat = sb("stat", [128, 4], F32)
        hnT = sb("hnT", [128, 8, 128], BF16)
        zs = sb("zs", [128, 2048], F32)
        yb = sb("yb", [128, 2048], F32)
        sq = sb("sq", [128, 2048], F32)
        ss = sb("ss", [128, 8], F32)
        yn = sb("yn", [128, 2048], BF16)
        yT = sb("yT", [128, 16, 128], BF16)
        pst = ps("pst", [128, 8, 128], BF16)
        pp = [ps("pp", [128, 512], F32) for _ in range(2)]
        pm = ps("pm", [128, 1024], F32)
        for c in range(8):
            kb.dma("pool", lambda e, c=c: e.dma_start(out=w_z[:, c, :], in_=W["ssm_w_in"][0, c * 128:(c + 1) * 128, 0:2048]), writes=("w_z",))
        for c in range(16):
            kb.dma("pool", lambda e, c=c: e.dma_start(out=w_out[:, c, :], in_=W["ssm_w_out"][0, c * 128:(c + 1) * 128, :]), writes=("sw_out",))
        kb.dma("sp", lambda e: e.dma_start(out=gb[:], in_=W["norm_mix"][li].partition_broadcast(128)), writes=("gb",))
        kb.dma("sp", lambda e: e.dma_start(out=gnb[:], in_=W["ssm_gnorm"][0].partition_broadcast(128)), writes=("gnb",))
        ppi = [0]
        for t in range(NT):
            hb = hb2[t % 2]
            hk = ("hb", t % 2)
            load_norm_T(kb, cm, t, hsrc, hsrc_name, hb, hk, gb, "gb", hn, hnb, scr, stat, pst, hnT[:, :, :], "hnT")
            kb.dma("sp", lambda e: e.dma_start(out=yb[:], in_=y_d[t * 128:(t + 1) * 128, :]), reads=(("y_d", t),), writes=("yb",))
            for cg in range(4):
                b = ppi[0] % 2
                ppi[0] += 1
                for c in range(8):
                    kb.op("pe", lambda e, c=c: e.matmul(pp[b][:, :], hnT[:, c, :], w_z[:, c, cg * 512:(cg + 1) * 512], start=(c == 0), stop=(c == 7)),
                          reads=("w_z", "hnT"), writes=(("pp", b),))
                kb.op("act", lambda e: e.activation(out=zs[:, cg * 512:(cg + 1) * 512], in_=pp[b][:, :], func=AF.Silu), reads=(("pp", b),), writes=("zs",))
            kb.op("dve", lambda e: e.tensor_tensor(out=yb[:], in0=yb[:], in1=zs[:], op=ALU.mult), reads=("yb", "zs"), writes=("yb",))
            for g in range(8):
                kb.op("act", lambda e, g=g: e.activation(out=sq[:, g * 256:(g + 1) * 256], in_=yb[:, g * 256:(g + 1) * 256], func=AF.Square,
                                                         accum_out=ss[:, g:g + 1]), reads=("yb",), writes=("sq", "ss"))
            kb.op("dve", lambda e: e.tensor_scalar(out=ss[:], in0=ss[:], scalar1=1.0 / 256.0, scalar2=EPS, op0=ALU.mult, op1=ALU.add), reads=("ss",), writes=("ss",))
            kb.op("act", lambda e: e.activation(out=ss[:], in_=ss[:], func=AF.Sqrt), reads=("ss",), writes=("ss",))
            kb.op("dve", lambda e: e.reciprocal(out=ss[:], in_=ss[:]), reads=("ss",), writes=("ss",))
            kb.op("dve", lambda e: e.tensor_tensor(out=yb[:].rearrange("p (g k) -> p g k", k=256), in0=yb[:].rearrange("p (g k) -> p g k", k=256),
                                                   in1=ss[:, :].unsqueeze(2).to_broadcast([128, 8, 256]), op=ALU.mult), reads=("yb", "ss"), writes=("yb",))
            kb.op("dve", lambda e: e.tensor_tensor(out=yn[:], in0=yb[:], in1=gnb[:], op=ALU.mult), reads=("yb", "gnb"), writes=("yn",))
            for blk in range(2):
                for cc in range(8):
                    kb.op("pe", lambda e, cc=cc: e.transpose(out=pst[:, cc, :], in_=yn[:, (blk * 8 + cc) * 128:(blk * 8 + cc + 1) * 128], identity=cm.ident[:]),
                          reads=("yn", "ident"), writes=("pst",))
                kb.op("act", lambda e: e.activation(out=yT[:, blk * 8:(blk + 1) * 8, :], in_=pst[:, :, :], func=AF.Copy), reads=("pst",), writes=("yT",))
            for cg in range(2):
                for c in range(16):
                    kb.op("pe", lambda e, c=c: e.matmul(pm[:, cg * 512:(cg + 1) * 512], yT[:, c, :], w_out[:, c, cg * 512:(cg + 1) * 512], start=(c == 0), stop=(c == 15)),
                          reads=("yT", "sw_out"), writes=("pm",))
                kb.op("dve", lambda e: e.tensor_tensor(out=hb[:, cg * 512:(cg + 1) * 512], in0=pm[:, cg * 512:(cg + 1) * 512],
                                                       in1=hb[:, cg * 512:(cg + 1) * 512], op=ALU.add), reads=("pm", hk), writes=(hk,))
            kb.dma("sp", lambda e: e.dma_start(out=hdst[t * 128:(t + 1) * 128, :], in_=hb[:]), reads=(hk,), writes=((hdst_name, t),))
        kb.barrier()


def final_phase(kb, cm, hsrc, hsrc_name, y, W):
    NT = cm.NT
    with contextlib.ExitStack() as es:
        sb = lambda n, s, d: kb.sb(n, s, d, es)
        gb = sb("gbf", [128, D], F32)
        h_t = [sb("hf", [128, D], F32) for _ in range(2)]
        o_t = [sb("of", [128, D], F32) for _ in range(2)]
        scr = sb("scrf", [128, D], F32)
        stat = [sb("statf", [128, 4], F32) for _ in range(2)]
        kb.dma("sp", lambda e: e.dma_start(out=gb[:], in_=W["norm_final"].partition_broadcast(128)), writes=("gbf",))
        for t in range(NT):
            b = t % 2
            kb.dma("sp", lambda e: e.dma_start(out=h_t[b][:], in_=hsrc[t * 128:(t + 1) * 128, :]),
                   reads=((hsrc_name, t),), writes=(("hf", b),))
            rmsnorm_tile(kb, h_t[b], ("hf", b), gb, "gbf", o_t[b], ("of", b), scr, "scrf", stat[b], ("statf", b))
            kb.dma("sp", lambda e: e.dma_start(out=y[t * 128:(t + 1) * 128, :], in_=o_t[b][:]),
                   reads=(("of", b),), writes=(("y", t),))


        kb.barrier()
WEIGHT_SPECS = {
    "norm_mix": [4, 1024], "norm_ffn": [4, 1024], "norm_final": [1024], "hgrn_lb_logits": [4, 1024],
    "hgrn_w_in": [2, 1024, 4096], "hgrn_gnorm": [2, 128], "hgrn_w_out": [2, 1024, 1024],
    "fox_w_in": [1, 1024, 3088], "fox_b_f": [1, 16], "fox_w_out": [1, 1024, 1024],
    "ssm_w_in": [1, 1024, 6176], "ssm_conv_w": [1, 4, 4096], "ssm_conv_b": [1, 4096],
    "ssm_dt_bias": [1, 32], "ssm_a_log": [1, 32], "ssm_d": [1, 32], "ssm_gnorm": [1, 2048],
    "ssm_w_out": [1, 2048, 1024], "peer_w_q": [4, 1024, 2048], "peer_keysT": [4, 16, 128, 128],
    "peer_u0": [16384, 1024], "peer_v0": [16384, 1024], "peer_u1": [16384, 1024], "peer_v1": [16384, 1024],
    "peer_u2": [16384, 1024], "peer_v2": [16384, 1024], "peer_u3": [16384, 1024], "peer_v3": [16384, 1024],
    "ssm_conv_wT": [128, 32, 4], "ssm_conv_bT": [128, 32],
    "c_tri": [128, 128], "c_negmask": [128, 128], "c_ones3": [3, 8192],
    "hgrn_lbT": [128, 4, 8], "c_ones": [128, 128], "c_rmask": [128, 512], "c_maskT16": [128, 128], "c_cmask": [128, 8],
}


def build(T=SEQ, plan=("peer0", "final"), used=None):
    kb = KB()
    cm = Common(kb, T)
    x = kb.dram("x", [T, D], F32, kind="ExternalInput")
    y = kb.dram("y", [T, D], F32, kind="ExternalOutput")
    hA = kb.dram("hA", [T, D], F32)
    W = {}
    names = set()
    for p in plan:
        if p.startswith("peer"):
            names |= {"peer_w_q", "peer_keysT", "peer_u" + p[4:], "peer_v" + p[4:], "norm_ffn"}
        if p == "final":
            names |= {"norm_final"}
        if p.startswith("ssd"):
            names |= {"ssm_w_in", "ssm_conv_wT", "ssm_conv_bT", "ssm_dt_bias", "ssm_a_log", "ssm_d", "ssm_gnorm", "ssm_w_out", "norm_mix",
                      "c_tri", "c_negmask", "c_ones"}
        if p.startswith("fox"):
            names |= {"fox_w_in", "fox_b_f", "fox_w_out", "norm_mix", "c_tri", "c_negmask", "c_ones3", "c_ones"}
        if p.startswith("hgrn"):
            names |= {"hgrn_w_in", "hgrn_w_out", "hgrn_gnorm", "hgrn_lbT", "norm_mix", "c_ones", "c_rmask", "c_maskT16", "c_cmask"}
    for n in sorted(names):
        W[n] = kb.dram(n, WEIGHT_SPECS[n], F32, kind="ExternalInput")
    peer_layers = [int(p[4:]) for p in plan if p.startswith("peer")]
    for l in peer_layers:
        W["peer_uvb%d" % l] = kb.dram("peer_uvb%d" % l, [16384, 2048], BF16)
    if peer_layers:
        convert_phase(kb, W, peer_layers)
    cur, cur_name = x, "x"
    for p in plan:
        kb.barrier()
        if p.startswith("peer"):
            li = int(p[4:])
            peer_phase(kb, cm, li, cur, cur_name, hA, "hA", W)
            cur, cur_name = hA, "hA"
        elif p.startswith("hgrn"):
            li = int(p[4:])
            hgrn_phase(kb, cm, li, li // 3, cur, cur_name, hA, "hA", W)
            cur, cur_name = hA, "hA"
        elif p.startswith("ssd"):
            li = int(p[3:])
            y_d = kb.dram("s_y", [T, 2048], F32)
            ssd_phase(kb, cm, li, cur, cur_name, hA, "hA", W, y_d)
            cur, cur_name = hA, "hA"
        elif p.startswith("fox"):
            li = int(p[3:])
            dk = "ExternalOutput" if DEBUG_SCRATCH else "Internal"
            scr_d = {"qT": kb.dram("f_qT", [1024, T], BF16, dk), "kT": kb.dram("f_kT", [1024, T], BF16, dk), "v": kb.dram("f_v", [T, 1024], BF16, dk),
                     "ca": kb.dram("f_ca", [16, 3, T], BF16, dk), "o": kb.dram("f_o", [T, 1024], BF16, dk), "negc": kb.dram("f_negc", [T, 16], F32, dk)}
            fox_phase(kb, cm, li, cur, cur_name, hA, "hA", W, scr_d)
            cur, cur_name = hA, "hA"
        elif p == "final":
            final_phase(kb, cm, cur, cur_name, y, W)
    kb.finish([("y", t) for t in range(cm.NT)])
    kb.es.close()
    return kb, sorted(names)


def host_consts():
    s = np.arange(128)
    c = {"c_ident": np.eye(128, dtype=np.float32)}
    c["c_ones"] = np.ones((128, 128), np.float32)
    c["c_rmask"] = np.tile((np.arange(512) % 16 != 0).astype(np.float32)[None, :], (128, 1))
    c["c_maskT16"] = ((s[:, None] // 16 == s[None, :] // 16) & (s[:, None] <= s[None, :])).astype(np.float32)
    c["c_cmask"] = (s[:, None] // 16 == np.arange(8)[None, :]).astype(np.float32)
    c["c_tri"] = (s[:, None] <= s[None, :]).astype(np.float32)
    c["c_negmask"] = np.where(s[:, None] <= s[None, :], 0.0, -30000.0).astype(np.float32)
    c["c_ones3"] = np.ones((3, 8192), np.float32)
    return c


def layout_weights(inp):
    out = dict(inp)
    for l in range(4):
        if "peer_u" in inp:
            out["peer_u%d" % l] = np.asarray(inp["peer_u"])[l]
            out["peer_v%d" % l] = np.asarray(inp["peer_v"])[l]
    if "peer_keys" in inp:
        pk = np.asarray(inp["peer_keys"])
        out["peer_keysT"] = np.ascontiguousarray(pk.reshape(4, 16, 128, 128).transpose(0, 1, 3, 2))
    if "ssm_conv_w" in inp:
        out["ssm_conv_wT"] = np.ascontiguousarray(np.asarray(inp["ssm_conv_w"])[0].reshape(4, 32, 128).transpose(2, 1, 0))
        out["ssm_conv_bT"] = np.ascontiguousarray(np.asarray(inp["ssm_conv_b"])[0].reshape(32, 128).T)
    if "hgrn_lb_logits" in inp:
        out["hgrn_lbT"] = np.ascontiguousarray(np.asarray(inp["hgrn_lb_logits"]).reshape(4, 8, 128).transpose(2, 0, 1))
    return out


FULL_PLAN = ("hgrn0", "peer0", "fox1", "peer1", "ssd2", "peer2", "hgrn3", "peer3", "final")


def kernel(**inputs):
    inp = layout_weights({k: np.asarray(v) for k, v in inputs.items()})
    inp.update(host_consts())
    kb, names = build(SEQ, plan=FULL_PLAN)
    shared = {n: np.ascontiguousarray(inp[n], dtype=np.float32) for n in names}
    shared["c_ident"] = inp["c_ident"]
    in_maps = []
    for c in range(NCORES):
        m = dict(shared)
        m["x"] = np.ascontiguousarray(inp["x"][c], dtype=np.float32)
        in_maps.append(m)
    res = run_bass_kernel_spmd(kb.nc, in_maps, core_ids=list(range(NCORES)))
    return np.stack([np.asarray(r["y"]) for r in res.results], axis=0).astype(np.float32)
```
